# Optimizing a Trainium2 kernel written in Bass

```python
import math
import jax, jax.numpy as jnp
from jax import lax
import numpy as np


D_MODEL = 1024
BATCH = 8
SEQ = 4096
DEPTH = 4

N_MEM = 256
N_BRANCH = 4
W_BRANCH = D_MODEL // 2
POOL_WINDOWS = (2, 4, 8, 16)
POOL_GROUP = W_BRANCH // len(POOL_WINDOWS)
DIL_GROUPS = ((128, 1), (512, 4), (2048, 16))
ATT_HEADS = 8
ATT_HEAD_DIM = W_BRANCH // ATT_HEADS
SSM_GROUP = 16
SSM_GROUPS = W_BRANCH // SSM_GROUP
SSM_STATE = 64
SGU_CHUNK = 128
SGU_GROUPS = 4
SGU_GROUP_DIM = W_BRANCH // SGU_GROUPS
X_HEADS = 4
X_HEAD_DIM = 128
D_FF = 4 * D_MODEL
REL_BUCKETS = 32
REL_MAX_DIST = 2048
EPS = 1e-6
NEG_INF = -1e30
N_ATT_COLS = 3 * len(DIL_GROUPS) * W_BRANCH
OFF_POOL = 0
OFF_ATT = OFF_POOL + W_BRANCH
OFF_SSM = OFF_ATT + N_ATT_COLS
OFF_SGU = OFF_SSM + W_BRANCH
OFF_GATE = OFF_SGU + 2 * W_BRANCH
IN_WIDTH = OFF_GATE + N_BRANCH * D_MODEL

kernel_name = 'hybrid_gated_parallel_mixer_block'


def rmsnorm(x, g):
    xf = x.astype(jnp.float32)
    y = xf * lax.rsqrt(jnp.mean(xf * xf, axis=-1, keepdims=True) + EPS)
    return (y * g.astype(jnp.float32)).astype(x.dtype)


def _t5_bucket(n):
    exact = REL_BUCKETS // 2
    nf = np.maximum(n, 1).astype(np.float32)
    large = exact + (np.log(nf / exact) / np.log(REL_MAX_DIST / exact) * (REL_BUCKETS - exact)).astype(np.int32)
    large = np.minimum(large, REL_BUCKETS - 1)
    return np.where(n < exact, n, large).astype(np.int32)


def _band_pattern(band, dil):
    i = np.arange(band)[:, None]
    kk = np.arange(2 * band)[None, :]
    dist = band + i - kk
    local = (dist >= 0) & (dist <= band)
    bucket = _t5_bucket(np.clip(dist, 0, band) * dil)
    return local, bucket


def pool_mixer(h, w_pool, scale):
    B, S, _ = h.shape
    hf = h.astype(jnp.float32)
    cs = jnp.pad(jnp.cumsum(hf, axis=1), ((0, 0), (1, 0), (0, 0)))
    t = jnp.arange(S)
    outs = []
    for gi, w in enumerate(POOL_WINDOWS):
        sl = slice(gi * POOL_GROUP, (gi + 1) * POOL_GROUP)
        lo = jnp.maximum(t + 1 - w, 0)
        cnt = jnp.minimum(t + 1, w).astype(jnp.float32)
        mean = (cs[:, 1:, sl] - cs[:, lo, sl]) / cnt[None, :, None]
        outs.append(mean - hf[..., sl])
    p = jnp.stack(outs, axis=2)
    y = jnp.einsum('bsgc,gcd->bsgd', p, w_pool.astype(jnp.float32)).reshape(B, S, W_BRANCH)
    return (y * scale.astype(jnp.float32)).astype(h.dtype)


def _dilated_group(q, k, v, bias, local, band, dil):
    B, S, H, E = q.shape
    L = S // dil
    nb = -(-L // band)
    Lp = nb * band

    def to_sub(a):
        a = a.reshape(B, L, dil, H, E).transpose(0, 2, 1, 3, 4)
        a = jnp.pad(a, ((0, 0), (0, 0), (0, Lp - L), (0, 0), (0, 0)))
        return a.reshape(B, dil, nb, band, H, E)

    def with_prev(a):
        prev = jnp.pad(a, ((0, 0), (0, 0), (1, 0), (0, 0), (0, 0), (0, 0)))[:, :, :-1]
        return jnp.concatenate([prev, a], axis=3)

    qs = to_sub(q)
    kk = with_prev(to_sub(k))
    vv = with_prev(to_sub(v))
    logits = jnp.einsum('brnqhe,brnkhe->brnhqk', qs, kk).astype(jnp.float32) * (E ** -0.5) + bias
    first = (jnp.arange(nb) == 0)[:, None, None] & (jnp.arange(2 * band) < band)[None, None, :]
    valid = local[None] & ~first
    logits = jnp.where(valid[None, None, :, None], logits, NEG_INF)
    m = jnp.max(logits, axis=-1, keepdims=True)
    p = jnp.exp(logits - m)
    s = jnp.sum(p, axis=-1, keepdims=True)
    o = jnp.einsum('brnhqk,brnkhe->brnqhe', (p / s).astype(v.dtype), vv)
    lse = (m + jnp.log(s))[..., 0]
    o = o.reshape(B, dil, Lp, H, E)[:, :, :L].transpose(0, 2, 1, 3, 4).reshape(B, S, H, E)
    lse = lse.transpose(0, 1, 2, 4, 3).reshape(B, dil, Lp, H)[:, :, :L]
    lse = lse.transpose(0, 2, 1, 3).reshape(B, S, H)
    return o, lse


def dilated_attention(qkv, rel_bias):
    B, S, _ = qkv.shape
    ng = len(DIL_GROUPS)
    q, k, v = [a.reshape(B, S, ng, ATT_HEADS, ATT_HEAD_DIM) for a in jnp.split(qkv, 3, axis=-1)]
    outs, lses = [], []
    for g, (win, dil) in enumerate(DIL_GROUPS):
        band = win // dil
        local, bucket = _band_pattern(band, dil)
        bias = rel_bias[jnp.asarray(bucket)][..., g * ATT_HEADS:(g + 1) * ATT_HEADS]
        bias = bias.transpose(2, 0, 1).astype(jnp.float32)
        o, lse = _dilated_group(q[:, :, g], k[:, :, g], v[:, :, g], bias, jnp.asarray(local), band, dil)
        outs.append(o.astype(jnp.float32))
        lses.append(lse)
    wts = jax.nn.softmax(jnp.stack(lses, axis=0), axis=0)
    out = jnp.sum(wts[..., None] * jnp.stack(outs, axis=0), axis=0)
    return out.reshape(B, S, W_BRANCH).astype(qkv.dtype)


def s5_mixer(u, a_re, a_im, log_dt, b_re, b_im, c_re, c_im, d_skip, w_glu, b_glu):
    f32 = jnp.float32
    B, S, _ = u.shape
    uf = u.astype(f32).reshape(B, S, SSM_GROUPS, SSM_GROUP)
    lam_re = jnp.minimum(a_re.astype(f32), -1e-4)
    lam_im = a_im.astype(f32)
    dt = jnp.exp(log_dt.astype(f32))[:, None]
    mag = jnp.exp(lam_re * dt)
    ab_re, ab_im = mag * jnp.cos(lam_im * dt), mag * jnp.sin(lam_im * dt)
    den = lam_re * lam_re + lam_im * lam_im
    f_re = ((ab_re - 1.0) * lam_re + ab_im * lam_im) / den
    f_im = (ab_im * lam_re - (ab_re - 1.0) * lam_im) / den
    br, bi = b_re.astype(f32), b_im.astype(f32)
    bb_re = f_re[..., None] * br - f_im[..., None] * bi
    bb_im = f_re[..., None] * bi + f_im[..., None] * br
    bu_re = jnp.einsum('bsgc,gpc->bsgp', uf, bb_re)
    bu_im = jnp.einsum('bsgc,gpc->bsgp', uf, bb_im)

    def combine(e1, e2):
        a1r, a1i, b1r, b1i = e1
        a2r, a2i, b2r, b2i = e2
        return (a2r * a1r - a2i * a1i, a2r * a1i + a2i * a1r,
                a2r * b1r - a2i * b1i + b2r, a2r * b1i + a2i * b1r + b2i)

    ar = jnp.broadcast_to(ab_re[None, None], (1, S, SSM_GROUPS, SSM_STATE))
    ai = jnp.broadcast_to(ab_im[None, None], (1, S, SSM_GROUPS, SSM_STATE))
    _, _, hr, hi = lax.associative_scan(combine, (ar, ai, bu_re, bu_im), axis=1)
    y = (jnp.einsum('bsgp,gcp->bsgc', hr, c_re.astype(f32))
         - jnp.einsum('bsgp,gcp->bsgc', hi, c_im.astype(f32))
         + uf * d_skip.astype(f32).reshape(SSM_GROUPS, SSM_GROUP))
    g = jax.nn.gelu(y.reshape(B, S, W_BRANCH))
    out = g * jax.nn.sigmoid(g @ w_glu.astype(f32) + b_glu.astype(f32))
    return out.astype(u.dtype)


def sgu_mixer(z, ln_g, ln_b, w_s, b_s):
    B, S, _ = z.shape
    z = jax.nn.gelu(z)
    u, v = jnp.split(z, 2, axis=-1)
    vf = v.astype(jnp.float32)
    mu = jnp.mean(vf, axis=-1, keepdims=True)
    var = jnp.mean(jnp.square(vf - mu), axis=-1, keepdims=True)
    vf = (vf - mu) * lax.rsqrt(var + EPS) * ln_g.astype(jnp.float32) + ln_b.astype(jnp.float32)
    vf = vf.reshape(B, S // SGU_CHUNK, SGU_CHUNK, SGU_GROUPS, SGU_GROUP_DIM)
    tri = jnp.tril(jnp.ones((SGU_CHUNK, SGU_CHUNK), jnp.float32))
    ws = w_s.astype(jnp.float32) * tri[None]
    sv = jnp.einsum('gts,bnsgc->bntgc', ws, vf) + b_s.astype(jnp.float32).T[:, :, None]
    return (u.astype(jnp.float32) * sv.reshape(B, S, W_BRANCH)).astype(z.dtype)


def cross_attn(h, mem_n, w_cq, w_ckv, w_co):
    B, S, _ = h.shape
    q = (h @ w_cq).reshape(B, S, X_HEADS, X_HEAD_DIM)
    k, v = jnp.split(mem_n @ w_ckv, 2, axis=-1)
    k = k.reshape(B, N_MEM, X_HEADS, X_HEAD_DIM)
    v = v.reshape(B, N_MEM, X_HEADS, X_HEAD_DIM)
    logits = jnp.einsum('bshe,bmhe->bhsm', q, k).astype(jnp.float32) * (X_HEAD_DIM ** -0.5)
    p = jax.nn.softmax(logits, axis=-1).astype(h.dtype)
    o = jnp.einsum('bhsm,bmhe->bshe', p, v).reshape(B, S, X_HEADS * X_HEAD_DIM)
    return o @ w_co


def setup_inputs(seed: int = 0) -> dict:
    key = jax.random.key(seed)
    ks = iter(jax.random.split(key, 48))
    f32 = jnp.float32
    L = DEPTH

    def nrm(shape, scale):
        return jax.random.normal(next(ks), shape, f32) * scale

    def gain(shape):
        return 1.0 + nrm(shape, 0.02)

    d = {}
    d['x'] = nrm((BATCH, SEQ, D_MODEL), 1.0)
    d['mem'] = nrm((BATCH, N_MEM, D_MODEL), 1.0)
    d['rel_bias'] = nrm((REL_BUCKETS, len(DIL_GROUPS) * ATT_HEADS), 0.5)
    d['g_mix_pre'] = gain((L, D_MODEL))
    d['g_mix_post'] = gain((L, D_MODEL))
    d['w_in'] = nrm((L, D_MODEL, IN_WIDTH), D_MODEL ** -0.5)
    d['gate_b'] = nrm((L, N_BRANCH, D_MODEL), 0.01)
    d['pool_w'] = nrm((L, len(POOL_WINDOWS), POOL_GROUP, POOL_GROUP), POOL_GROUP ** -0.5)
    d['pool_scale'] = 1.0 + nrm((L, W_BRANCH), 0.1)
    n_idx = jnp.arange(SSM_STATE, dtype=f32)
    d['a_re'] = -0.5 + nrm((L, SSM_GROUPS, SSM_STATE), 0.01)
    d['a_im'] = jnp.pi * n_idx + nrm((L, SSM_GROUPS, SSM_STATE), 0.01)
    d['log_dt'] = jax.random.uniform(next(ks), (L, SSM_GROUPS), f32, math.log(1e-3), math.log(1e-1))
    d['b_re'] = nrm((L, SSM_GROUPS, SSM_STATE, SSM_GROUP), (2 * SSM_GROUP) ** -0.5)
    d['b_im'] = nrm((L, SSM_GROUPS, SSM_STATE, SSM_GROUP), (2 * SSM_GROUP) ** -0.5)
    d['c_re'] = nrm((L, SSM_GROUPS, SSM_GROUP, SSM_STATE), (2 * SSM_STATE) ** -0.5)
    d['c_im'] = nrm((L, SSM_GROUPS, SSM_GROUP, SSM_STATE), (2 * SSM_STATE) ** -0.5)
    d['d_skip'] = nrm((L, W_BRANCH), 1.0)
    d['w_glu'] = nrm((L, W_BRANCH, W_BRANCH), W_BRANCH ** -0.5)
    d['b_glu'] = nrm((L, W_BRANCH), 0.01)
    d['sgu_ln_g'] = gain((L, W_BRANCH))
    d['sgu_ln_b'] = nrm((L, W_BRANCH), 0.01)
    d['w_s'] = nrm((L, SGU_GROUPS, SGU_CHUNK, SGU_CHUNK), SGU_CHUNK ** -0.5)
    d['b_s'] = 1.0 + nrm((L, SGU_GROUPS, SGU_CHUNK), 0.01)
    d['w_up'] = nrm((L, N_BRANCH, W_BRANCH, D_MODEL), W_BRANCH ** -0.5)
    d['w_out'] = nrm((L, D_MODEL, D_MODEL), D_MODEL ** -0.5)
    d['g_x_pre'] = gain((L, D_MODEL))
    d['g_x_post'] = gain((L, D_MODEL))
    d['g_mem'] = gain((L, D_MODEL))
    d['w_cq'] = nrm((L, D_MODEL, X_HEADS * X_HEAD_DIM), D_MODEL ** -0.5)
    d['w_ckv'] = nrm((L, D_MODEL, 2 * X_HEADS * X_HEAD_DIM), D_MODEL ** -0.5)
    d['w_co'] = nrm((L, X_HEADS * X_HEAD_DIM, D_MODEL), (X_HEADS * X_HEAD_DIM) ** -0.5)
    d['g_ff_pre'] = gain((L, D_MODEL))
    d['g_ff_post'] = gain((L, D_MODEL))
    d['w_ff1'] = nrm((L, D_MODEL, D_FF), D_MODEL ** -0.5)
    d['w_ff2'] = nrm((L, D_FF, D_MODEL), D_FF ** -0.5)
    return d


def reference(x, mem, rel_bias, g_mix_pre, g_mix_post, w_in, gate_b, pool_w, pool_scale,
              a_re, a_im, log_dt, b_re, b_im, c_re, c_im, d_skip, w_glu, b_glu,
              sgu_ln_g, sgu_ln_b, w_s, b_s, w_up, w_out, g_x_pre, g_x_post, g_mem,
              w_cq, w_ckv, w_co, g_ff_pre, g_ff_post, w_ff1, w_ff2):
    B, S, _ = x.shape
    for l in range(DEPTH):
        h = rmsnorm(x, g_mix_pre[l])
        proj = h @ w_in[l]
        a_out = pool_mixer(proj[..., OFF_POOL:OFF_ATT], pool_w[l], pool_scale[l])
        b_out = dilated_attention(proj[..., OFF_ATT:OFF_SSM], rel_bias)
        c_out = s5_mixer(proj[..., OFF_SSM:OFF_SGU], a_re[l], a_im[l], log_dt[l], b_re[l], b_im[l],
                         c_re[l], c_im[l], d_skip[l], w_glu[l], b_glu[l])
        d_out = sgu_mixer(proj[..., OFF_SGU:OFF_GATE], sgu_ln_g[l], sgu_ln_b[l], w_s[l], b_s[l])
        gates = jax.nn.sigmoid(proj[..., OFF_GATE:].reshape(B, S, N_BRANCH, D_MODEL) + gate_b[l])
        branches = (a_out, b_out, c_out, d_out)
        merged = gates[:, :, 0] * (branches[0] @ w_up[l, 0])
        for i in range(1, N_BRANCH):
            merged = merged + gates[:, :, i] * (branches[i] @ w_up[l, i])
        x = x + rmsnorm(merged @ w_out[l], g_mix_post[l])
        h = rmsnorm(x, g_x_pre[l])
        mem_n = rmsnorm(mem, g_mem[l])
        x = x + rmsnorm(cross_attn(h, mem_n, w_cq[l], w_ckv[l], w_co[l]), g_x_post[l])
        h = rmsnorm(x, g_ff_pre[l])
        ff = jnp.square(jax.nn.relu(h @ w_ff1[l])) @ w_ff2[l]
        x = x + rmsnorm(ff, g_ff_post[l])
    return x
```

```python
import contextlib
import numpy as np
import concourse.bass as bass
import concourse.mybir as mybir
from concourse.bass_utils import run_bass_kernel_spmd

F32 = mybir.dt.float32
BF16 = mybir.dt.bfloat16
U8 = mybir.dt.uint8
I32 = mybir.dt.int32
ALU = mybir.AluOpType
AF = mybir.ActivationFunctionType

SEM_CAP = 30000
DMA_SLOTS = 8
ARENA_BYTES = 212480

S = 4096
D = 1024
NT = S // 128
DEPTH = 4
EPS = 1e-6
NMEM = 256
IN_WIDTH = 10752
OFF_POOL, OFF_ATT, OFF_SSM, OFF_SGU, OFF_GATE = 0, 512, 5120, 5632, 6656
DILS = (1, 4, 16)


class Res:
    __slots__ = ("name", "lw", "rd")

    def __init__(self, name=""):
        self.name = name
        self.lw = None
        self.rd = []


class T:
    def __init__(self, apview, name):
        self.v = apview
        self.name = name
        self._res = {}

    def __getitem__(self, k):
        return self.v[k]

    def r(self, key=None):
        x = self._res.get(key)
        if x is None:
            x = Res(f"{self.name}:{key}")
            self._res[key] = x
        return x


class Op:
    __slots__ = ("eng", "fn", "deps", "signal", "dma", "slot", "use", "idx", "cnt")

    def __init__(self, eng, fn, dma):
        self.eng = eng
        self.fn = fn
        self.deps = []
        self.signal = False
        self.dma = dma
        self.slot = None
        self.use = None
        self.idx = None
        self.cnt = None


class Prog:
    ENGS = ("pe", "act", "dve", "pool", "sp")

    def __init__(self, nc):
        self.nc = nc
        self.ops = {e: [] for e in self.ENGS}
        self.ndma = {e: 0 for e in self.ENGS}
        self.waited = {e: {} for e in self.ENGS}
        self.pending = {e: [] for e in self.ENGS}
        self.nops = 0

    def _need(self, op, key):
        if key is None:
            return
        X = op.eng
        if key[0] == "c":
            _, Y, j = key
            if Y == X and not op.dma:
                return
            k = ("c", Y)
            if self.waited[X].get(k, -1) >= j:
                return
            self.waited[X][k] = j
            self.ops[Y][j].signal = True
            op.deps.append(key)
        else:
            _, q, slot, use = key
            k = ("d", q, slot)
            if self.waited[X].get(k, 0) >= use:
                return
            self.waited[X][k] = use
            op.deps.append(key)

    def _same(self, o, key):
        eng = o.eng
        if eng == "pe":
            return
        k = ("c", eng)
        if self.waited[eng].get(k, -1) < key[2]:
            self.waited[eng][k] = key[2]
            self.ops[eng][key[2]].signal = True
            o.deps.append(key)

    def barrier(self):
        keys = []
        for e in self.ENGS:
            if self.ops[e]:
                last = None
                for o in reversed(self.ops[e]):
                    if not o.dma:
                        last = o
                        break
                if last is not None:
                    keys.append(("c", e, last.idx))
            n = self.ndma[e]
            for s in range(min(n, DMA_SLOTS)):
                keys.append(("d", e, s, (n - 1 - s) // DMA_SLOTS + 1))
        for e in self.ENGS:
            self.pending[e] = list(keys)

    def op(self, eng, fn, reads=(), writes=(), dma=False):
        o = Op(eng, fn, dma)
        o.idx = len(self.ops[eng])
        if self.pending[eng]:
            for k in self.pending[eng]:
                if k[0] == "c" and k[1] == eng:
                    if dma:
                        self._need(o, k)
                    continue
                self._need(o, k)
            self.pending[eng] = []
        if dma:
            n = self.ndma[eng]
            self.ndma[eng] = n + 1
            o.slot = n % DMA_SLOTS
            o.use = n // DMA_SLOTS + 1
            mykey = ("d", eng, o.slot, o.use)
            if o.use > 1:
                self._need(o, ("d", eng, o.slot, o.use - 1))
        else:
            mykey = ("c", eng, o.idx)
        for r in reads:
            lw = r.lw
            if lw is not None:
                if lw[0] == "c" and lw[1] == eng and not dma:
                    self._same(o, lw)
                else:
                    self._need(o, lw)
        for w in writes:
            if w.lw is not None:
                if w.lw[0] == "c" and w.lw[1] == eng and not dma:
                    self._same(o, w.lw)
                else:
                    self._need(o, w.lw)
            for rk in w.rd:
                if rk[0] == "c" and rk[1] == eng and not dma:
                    self._same(o, rk)
                else:
                    self._need(o, rk)
        for r in reads:
            r.rd.append(mykey)
            if len(r.rd) > 48:
                last = {}
                for kk in r.rd:
                    kid = kk[:2] if kk[0] == "c" else kk[:3]
                    if kid not in last or last[kid][-1] < kk[-1]:
                        last[kid] = kk
                r.rd = list(last.values())
        for w in writes:
            w.lw = mykey
            w.rd = []
        self.ops[eng].append(o)
        self.nops += 1
        return o

    def emit(self):
        nc = self.nc
        nsig = {}
        for e in self.ENGS:
            c = 0
            for o in self.ops[e]:
                if o.signal and not o.dma:
                    c += 1
                    o.cnt = c
            nsig[e] = c
        with contextlib.ExitStack() as st:
            csem = {}
            for e in self.ENGS:
                n = max(1, -(-nsig[e] // SEM_CAP))
                csem[e] = [st.enter_context(nc.semaphore(f"c_{e}_{i}")) for i in range(n)]
            dsem = {}
            for e in self.ENGS:
                if self.ndma[e]:
                    dsem[e] = [st.enter_context(nc.semaphore(f"d_{e}_{i}")) for i in range(DMA_SLOTS)]
            block = st.enter_context(nc.Block())

            def run(eng_name, engobj):
                for o in self.ops[eng_name]:
                    for d in o.deps:
                        if d[0] == "c":
                            p = self.ops[d[1]][d[2]]
                            c = p.cnt - 1
                            engobj.wait_ge(csem[d[1]][c // SEM_CAP], c % SEM_CAP + 1)
                        else:
                            engobj.wait_ge(dsem[d[1]][d[2]], 16 * d[3])
                    ins = o.fn(engobj)
                    if o.dma:
                        ins.then_inc(dsem[eng_name][o.slot], 16)
                    elif o.signal:
                        c = o.cnt - 1
                        ins.then_inc(csem[eng_name][c // SEM_CAP], 1)

            @block.tensor
            def _(e):
                run("pe", e)

            @block.scalar
            def _(e):
                run("act", e)

            @block.vector
            def _(e):
                run("dve", e)

            @block.gpsimd
            def _(e):
                run("pool", e)

            @block.sync
            def _(e):
                run("sp", e)
                for q in self.ENGS:
                    n = self.ndma[q]
                    for s in range(min(n, DMA_SLOTS)):
                        e.wait_ge(dsem[q][s], 16 * ((n - 1 - s) // DMA_SLOTS + 1))


DT_SIZE = {F32: 4, BF16: 2, U8: 1, I32: 4}


class KB:
    def __init__(self, nc, st, dram):
        self.nc = nc
        self.P = Prog(nc)
        self.d = dram
        self.arena = st.enter_context(nc.sbuf_tensor("arena", [128, ARENA_BYTES], U8))
        self.aoff = 0
        self.perm_off = 0
        self.psb = [T(st.enter_context(nc.psum_tensor(f"psb{i}", [128, 512], F32)), f"psb{i}") for i in range(8)]
        self.ps_i = 0
        self.dres = {}
        self.uid = 0

    def alloc(self, name, free_shape, dt, perm=False):
        n = int(np.prod(free_shape)) * DT_SIZE[dt]
        n_al = (n + 63) // 64 * 64
        off = self.aoff
        assert off + n_al <= ARENA_BYTES, f"arena overflow at {name}: {off}+{n_al}"
        self.aoff += n_al
        v = self.arena[:, off:off + n].bitcast(dt)
        if len(free_shape) == 2:
            v = v.rearrange("p (a b) -> p a b", a=free_shape[0])
        elif len(free_shape) == 3:
            v = v.rearrange("p (a b c) -> p a b c", a=free_shape[0], b=free_shape[1])
        self.uid += 1
        return T(v, f"{name}{self.uid}")

    def mark_perm(self):
        self.perm_off = self.aoff

    def reset(self):
        self.P.barrier()
        self.aoff = self.perm_off

    def ps(self, n=1):
        if n == 2 and self.ps_i % 2 == 1:
            self.ps_i += 1
        out = [self.psb[(self.ps_i + i) % 8] for i in range(n)]
        self.ps_i = (self.ps_i + n) % 8
        return out

    def dr(self, name, key=None):
        k = (name, key)
        x = self.dres.get(k)
        if x is None:
            x = Res(f"dram:{name}:{key}")
            self.dres[k] = x
        return x

    def dma(self, q, out, in_, reads=(), writes=()):
        return self.P.op(q, lambda e: e.dma_start(out=out, in_=in_), reads=reads, writes=writes, dma=True)

    def mm(self, out, lhsT, rhs, start, stop, reads, writes):
        return self.P.op("pe", lambda e: e.matmul(out, lhsT=lhsT, rhs=rhs, start=start, stop=stop),
                         reads=reads, writes=writes)

    def tr(self, out, in_, ident, reads, writes):
        return self.P.op("pe", lambda e: e.transpose(out=out, in_=in_, identity=ident), reads=reads, writes=writes)

    def act(self, out, in_, func, reads, writes, bias=None, scale=None, accum_out=None):
        kw = {}
        if bias is not None:
            kw["bias"] = bias
        if scale is not None:
            kw["scale"] = scale
        if accum_out is not None:
            kw["accum_out"] = accum_out
        return self.P.op("act", lambda e: e.activation(out=out, in_=in_, func=func, **kw), reads=reads, writes=writes)

    def v(self, eng, name, reads, writes, *a, **kw):
        return self.P.op(eng, lambda e: getattr(e, name)(*a, **kw), reads=reads, writes=writes)

    def setup_consts(self):
        self.ident_f = self.alloc("identf", [128], F32, perm=True)
        self.ident = self.alloc("ident", [128], BF16, perm=True)
        self.ones = self.alloc("ones", [128], BF16, perm=True)
        self.dma("sp", self.ident_f[:], self.d["c_ident"], writes=[self.ident_f.r()])
        self.v("dve", "tensor_copy", [self.ident_f.r()], [self.ident.r()], out=self.ident[:], in_=self.ident_f[:])
        self.v("dve", "memset", [], [self.ones.r()], self.ones[:], 1.0)
        self.stat_i = 0
        self.stats = self.alloc("stats", [64, 4], F32, perm=True)

    def stat(self):
        i = self.stat_i
        self.stat_i = (i + 1) % 64
        return self.stats[:, i, :], self.stats.r(i)

    def load_bcast(self, name, row_ap, n):
        t = self.alloc(name, [n], F32)
        self.dma("sp", t[:], row_ap.broadcast_to([128, n]), writes=[t.r()])
        return t

    def load_w(self, name, src, kch, ncols):
        t = self.alloc(name, [kch, ncols], BF16)
        step = max(1, (4 * 1024 * 1024) // (128 * ncols * 4))
        for k0 in range(0, kch, step):
            k1 = min(kch, k0 + step)
            self.dma("pool", t[:, k0:k1, :], src[k0 * 128:k1 * 128, :].rearrange("(c p) n -> p c n", p=128),
                     writes=[t.r(k) for k in range(k0, k1)])
        return t

    def rstd_from_ss(self, ss_ap, ss_res, n):
        sq, sq_r = self.stat()
        self.act(sq[:, 0:1], ss_ap, AF.Ln, [ss_res, self.epsb.r()], [sq_r], bias=self.epsb[:, 0:1], scale=1.0 / n)
        self.act(sq[:, 1:2], sq[:, 0:1], AF.Exp, [sq_r], [sq_r], scale=-0.5)
        return sq[:, 1:2], sq_r

    def norm_tile(self, xt, xt_res, gt, hT, hT_res, col0, junk, defer=False):
        st_, st_r = self.stat()
        self.act(junk[:], xt, AF.Square, [xt_res], [junk.r(), st_r], accum_out=st_[:, 0:1])
        rstd, rr = self.rstd_from_ss(st_[:, 0:1], st_r, D)
        hb = self.hb[self.hb_i % len(self.hb)]
        self.hb_i += 1
        self.v("dve", "scalar_tensor_tensor", [xt_res, rr, gt.r()], [hb.r()],
               out=hb[:], in0=xt, scalar=rstd, in1=gt[:], op0=ALU.mult, op1=ALU.mult)
        (pb,) = self.ps(1)
        pv = pb[:].bitcast(BF16).rearrange("p (a b) -> p a b", a=8)
        for k in range(8):
            self.tr(pv[:, k, :], hb[:, k * 128:(k + 1) * 128], self.ident[:], [hb.r(), self.ident.r()], [pb.r()])
        def part_b():
            self.P.op("act", lambda e: e.copy(out=hT[:, 0:8, col0:col0 + 128], in_=pv), reads=[pb.r()], writes=[hT_res])
        if defer:
            return part_b
        part_b()
        return None

    def post_tile(self, pbs, xt, xt_res, gpost, dst_ap, dst_res, junk):
        st_, st_r = self.stat()
        for i in range(2):
            self.act(junk[:, i * 512:(i + 1) * 512], pbs[i][:], AF.Square, [pbs[i].r()], [junk.r(), st_r],
                     accum_out=st_[:, i:i + 1])
        self.v("dve", "tensor_add", [st_r], [st_r], out=st_[:, 2:3], in0=st_[:, 0:1], in1=st_[:, 1:2])
        rstd, rr = self.rstd_from_ss(st_[:, 2:3], st_r, D)
        tmp = self.ptmp[self.ptmp_i % 2]
        self.ptmp_i += 1
        for i in range(2):
            self.v("dve", "tensor_tensor", [pbs[i].r(), gpost.r()], [tmp.r()], out=tmp[:, i * 512:(i + 1) * 512], in0=pbs[i][:],
                   in1=gpost[:, i * 512:(i + 1) * 512], op=ALU.mult)
        self.v("dve", "scalar_tensor_tensor", [tmp.r(), rr, xt_res], [tmp.r()], out=tmp[:], in0=tmp[:], scalar=rstd, in1=xt,
               op0=ALU.mult, op1=ALU.add)
        self.dma("pool", dst_ap, tmp[:], reads=[tmp.r()], writes=[dst_res])

    def common_bufs(self):
        self.epsb = self.alloc("epsb", [1], F32)
        self.v("dve", "memset", [], [self.epsb.r()], self.epsb[:], EPS)
        self.hb = [self.alloc("hb", [D], BF16) for _ in range(2)]
        self.hb_i = 0
        self.ptmp = [self.alloc("ptmp", [D], F32) for _ in range(2)]
        self.ptmp_i = 0
        self.junk = self.alloc("junk", [D], BF16)

    def ffn(self, l, src, dst):
        d = self.d
        G = 256
        ng = S // G
        tpg = G // 128
        self.reset()
        self.common_bufs()
        gpre = self.load_bcast("gpre", d["g_ff_pre"][l:l + 1, :], D)
        gpost = self.load_bcast("gpost", d["g_ff_post"][l:l + 1, :], D)
        w1 = self.load_w("w1", d["w_ff1"][l], 8, 4096)
        w2 = self.load_w("w2", d["w_ff2"][l], 32, D)
        xg = [self.alloc("xg", [tpg, D], F32) for _ in range(2)]
        hTg = [self.alloc("hTg", [8, G], BF16) for _ in range(2)]
        hid = self.alloc("hid", [32, G], BF16)
        rl = self.alloc("rl", [2, G], F32)
        def norm_group(g):
            xb = xg[g % 2]
            hT = hTg[g % 2]
            pend = None
            for t in range(tpg):
                tt = g * tpg + t
                self.dma("sp", xb[:, t, :], d[src][tt * 128:(tt + 1) * 128, :], reads=[self.dr(src, tt)], writes=[xb.r(t)])
                nb_ = self.norm_tile(xb[:, t, :], xb.r(t), gpre, hT, hT.r(), t * 128, self.junk, defer=True)
                if pend is not None:
                    pend()
                pend = nb_
            pend()

        norm_group(0)
        for g in range(ng):
            xb = xg[g % 2]
            hT = hTg[g % 2]
            for j in range(32):
                (pb,) = self.ps(1)
                for k in range(8):
                    self.mm(pb[:, 0:G], w1[:, k, j * 128:(j + 1) * 128], hT[:, k, :], k == 0, k == 7,
                            [w1.r(k), hT.r()], [pb.r()])
                self.act(rl[:, j % 2, :], pb[:, 0:G], AF.Relu, [pb.r()], [rl.r(j % 2)])
                eng = "pool" if j % 2 == 0 else "dve"
                self.v(eng, "tensor_tensor", [rl.r(j % 2)], [hid.r(j)], out=hid[:, j, :], in0=rl[:, j % 2, :],
                       in1=rl[:, j % 2, :], op=ALU.mult)
            if g + 1 < ng:
                norm_group(g + 1)
            for t in range(tpg):
                tt = g * tpg + t
                pbs = self.ps(2)
                for half in range(2):
                    for j in range(32):
                        self.mm(pbs[half][:], hid[:, j, t * 128:(t + 1) * 128], w2[:, j, half * 512:(half + 1) * 512],
                                j == 0, j == 31, [hid.r(j), w2.r(j)], [pbs[half].r()])
                self.post_tile(pbs, xb[:, t, :], xb.r(t), gpost, d[dst][tt * 128:(tt + 1) * 128, :], self.dr(dst, tt), self.junk)

    def cross(self, l, src, dst):
        d = self.d
        G = 512
        ng = S // G
        tpg = G // 128
        self.reset()
        self.common_bufs()
        gpre = self.load_bcast("gpre", d["g_x_pre"][l:l + 1, :], D)
        gpost = self.load_bcast("gpost", d["g_x_post"][l:l + 1, :], D)
        gmem = self.load_bcast("gmem", d["g_mem"][l:l + 1, :], D)
        wq = self.load_w("wq", d["w_cq"][l], 8, 512)
        wkv = self.load_w("wkv", d["w_ckv"][l], 8, 1024)
        wo = self.load_w("wo", d["w_co"][l], 4, D)
        memT = self.alloc("memT", [8, NMEM], BF16)
        kT = self.alloc("kT", [4, NMEM], BF16)
        vv = self.alloc("vv", [2, 512], BF16)
        mt_ = self.alloc("memt", [D], F32)
        for mt in range(2):
            self.dma("sp", mt_[:], d["mem"][mt * 128:(mt + 1) * 128, :], writes=[mt_.r()])
            self.norm_tile(mt_[:], mt_.r(), gmem, memT, memT.r(), mt * 128, self.junk)
        for hd in range(4):
            (pb,) = self.ps(1)
            for k in range(8):
                self.mm(pb[:, 0:NMEM], wkv[:, k, hd * 128:(hd + 1) * 128], memT[:, k, :], k == 0, k == 7,
                        [wkv.r(k), memT.r()], [pb.r()])
            self.P.op("act", lambda e, pb=pb, hd=hd: e.copy(out=kT[:, hd, :], in_=pb[:, 0:NMEM]), reads=[pb.r()], writes=[kT.r()])
        for mt in range(2):
            (pb,) = self.ps(1)
            for k in range(8):
                self.mm(pb[:], memT[:, k, mt * 128:(mt + 1) * 128], wkv[:, k, 512:1024], k == 0, k == 7,
                        [wkv.r(k), memT.r()], [pb.r()])
            self.P.op("act", lambda e, pb=pb, mt=mt: e.copy(out=vv[:, mt, :], in_=pb[:]), reads=[pb.r()], writes=[vv.r()])
        xg = [self.alloc("xg", [tpg, D], F32) for _ in range(2)]
        hTg = [self.alloc("hTg", [8, G], BF16) for _ in range(2)]
        qT = [self.alloc("qT", [G], BF16) for _ in range(3)]
        PT = [self.alloc("PT", [2, G], BF16) for _ in range(3)]
        rden = [self.alloc("rden", [G], F32) for _ in range(2)]
        oTs = [self.alloc("oT", [4, G], BF16) for _ in range(2)]
        scale = 128.0 ** -0.5
        def norm_group(g):
            xb = xg[g % 2]
            hT = hTg[g % 2]
            pend = None
            for t in range(tpg):
                tt = g * tpg + t
                self.dma("sp", xb[:, t, :], d[src][tt * 128:(tt + 1) * 128, :], reads=[self.dr(src, tt)], writes=[xb.r(t)])
                nb_ = self.norm_tile(xb[:, t, :], xb.r(t), gpre, hT, hT.r(), t * 128, self.junk, defer=True)
                if pend is not None:
                    pend()
                pend = nb_
            pend()

        def stA(g, hd):
            hT = hTg[g % 2]
            q = qT[hd % 3]
            (pb,) = self.ps(1)
            for k in range(8):
                self.mm(pb[:], wq[:, k, hd * 128:(hd + 1) * 128], hT[:, k, :], k == 0, k == 7, [wq.r(k), hT.r()], [pb.r()])
            self.P.op("act", lambda e, pb=pb, q=q: e.copy(out=q[:], in_=pb[:]), reads=[pb.r()], writes=[q.r()])

        def stB(g, hd):
            q = qT[hd % 3]
            pt = PT[hd % 3]
            for mt in range(2):
                (pb,) = self.ps(1)
                self.mm(pb[:], kT[:, hd, mt * 128:(mt + 1) * 128], q[:], True, True, [kT.r(), q.r()], [pb.r()])
                self.act(pt[:, mt, :], pb[:], AF.Exp, [pb.r()], [pt.r()], scale=scale)

        def stC(g, hd):
            oT = oTs[g % 2]
            pt = PT[hd % 3]
            rd = rden[hd % 2]
            (pn,) = self.ps(1)
            (pd,) = self.ps(1)
            for mt in range(2):
                self.mm(pn[:], vv[:, mt, hd * 128:(hd + 1) * 128], pt[:, mt, :], mt == 0, mt == 1, [vv.r(), pt.r()], [pn.r()])
            for mt in range(2):
                self.mm(pd[:], self.ones[:], pt[:, mt, :], mt == 0, mt == 1, [self.ones.r(), pt.r()], [pd.r()])
            self.act(rd[:], pd[:], AF.Ln, [pd.r()], [rd.r()])
            self.act(rd[:], rd[:], AF.Exp, [rd.r()], [rd.r()], scale=-1.0)
            self.v("dve", "tensor_tensor", [pn.r(), rd.r()], [oT.r()], out=oT[:, hd, :], in0=pn[:], in1=rd[:], op=ALU.mult)

        norm_group(0)
        stA(0, 0)
        stA(0, 1)
        for g in range(ng):
            xb = xg[g % 2]
            oT = oTs[g % 2]
            stB(g, 0); stA(g, 2); stB(g, 1); stC(g, 0); stA(g, 3); stB(g, 2); stC(g, 1); stB(g, 3); stC(g, 2); stC(g, 3)
            if g + 1 < ng:
                norm_group(g + 1)
                stA(g + 1, 0)
                stA(g + 1, 1)
            for t in range(tpg):
                tt = g * tpg + t
                pbs = self.ps(2)
                for half in range(2):
                    for hd in range(4):
                        self.mm(pbs[half][:], oT[:, hd, t * 128:(t + 1) * 128], wo[:, hd, half * 512:(half + 1) * 512],
                                hd == 0, hd == 3, [oT.r(), wo.r(hd)], [pbs[half].r()])
                self.post_tile(pbs, xb[:, t, :], xb.r(t), gpost, d[dst][tt * 128:(tt + 1) * 128, :], self.dr(dst, tt), self.junk)

    def reset_to(self, off):
        self.P.barrier()
        self.aoff = off

    def mixer(self, l, src, dst, parts=("pool", "att", "s5", "sgu", "merge")):
        d = self.d
        self.reset()
        self.common_bufs()
        hT = self.alloc("hT", [8, S], BF16)
        base = self.aoff
        gpre = self.load_bcast("gpre", d["g_mix_pre"][l:l + 1, :], D)
        xt = [self.alloc("xt", [D], F32) for _ in range(4)]
        hb_save = self.hb
        self.hb = hb_save + [self.alloc("hbx", [D], BF16) for _ in range(2)]
        junks = [self.junk] + [self.alloc("junkx", [D], BF16) for _ in range(1)]
        pend0 = None
        for tt in range(NT):
            x_ = xt[tt % 4]
            self.dma("sp", x_[:], d[src][tt * 128:(tt + 1) * 128, :], reads=[self.dr(src, tt)], writes=[x_.r()])
            nb_ = self.norm_tile(x_[:], x_.r(), gpre, hT, hT.r(), tt * 128, junks[tt % 2], defer=True)
            if pend0 is not None:
                pend0()
            pend0 = nb_
        pend0()
        self.hb = hb_save
        if "pool" in parts or "sgu" in parts:
            self.reset_to(base)
            self.mix_pool_sgu(l, hT)
        if "att" in parts:
            self.reset_to(base)
            self.mix_att(l, hT)
        if "s5" in parts:
            self.reset_to(base)
            self.mix_s5(l, hT)
            self.reset_to(base)
            self.mix_s5_glu(l)
        if "merge" in parts:
            for h2 in range(2):
                self.reset_to(base)
                self.mix_merge_a(l, hT, h2)
            self.reset_to(base)
            self.mix_merge_b(l, src, dst)

    def mix_pool_sgu(self, l, hT):
        d = self.d
        wp = self.load_w("wp", d["w_in"][l][:, OFF_POOL:OFF_POOL + 512], 8, 512)
        pw = self.alloc("pw", [4, 128], BF16)
        self.dma("pool", pw[:], d["pool_w"][l].rearrange("g c e -> c g e"), writes=[pw.r()])
        psc = self.alloc("psc", [4], F32)
        self.dma("sp", psc[:], d["pool_scale_c"][l], writes=[psc.r()])
        invc = self.alloc("invc", [64], F32)
        self.dma("sp", invc[:], d["c_invc"].broadcast_to([128, 64]), writes=[invc.r()])
        hp = self.alloc("hp", [16 + S], F32)
        A = self.alloc("pA", [16 + S], F32)
        B = self.alloc("pB", [16 + S], F32)
        for b_ in (hp, A, B):
            self.v("pool", "memset", [], [b_.r()], b_[:, 0:16], 0.0)
        pbf = self.alloc("pbf", [S], BF16)
        ao = self.alloc("ao", [S], BF16)
        t16 = self.alloc("t16", [16], F32)
        pstate = {}

        def pool_a(gi):
            w = 2 << gi
            for t8 in range(8):
                (pb,) = self.ps(1)
                for k in range(8):
                    self.mm(pb[:], wp[:, k, gi * 128:(gi + 1) * 128], hT[:, k, t8 * 512:(t8 + 1) * 512], k == 0, k == 7,
                            [wp.r(k)], [pb.r()])
                self.P.op("act", lambda e, pb=pb, t8=t8: e.copy(out=hp[:, 16 + t8 * 512:16 + (t8 + 1) * 512], in_=pb[:]),
                          reads=[pb.r()], writes=[hp.r()])
            cur, sh, i = hp, 1, 0
            while sh < w:
                nxt = (A, B)[i % 2]
                self.v("pool", "tensor_add", [cur.r()], [nxt.r()], out=nxt[:, 16:16 + S], in0=cur[:, 16:16 + S],
                       in1=cur[:, 16 - sh:16 - sh + S])
                cur, sh, i = nxt, sh * 2, i + 1
            pstate[gi] = cur

        def pool_b(gi):
            w = 2 << gi
            cur = pstate[gi]
            self.v("dve", "scalar_tensor_tensor", [cur.r(), hp.r()], [pbf.r()], out=pbf[:], in0=cur[:, 16:16 + S],
                   scalar=1.0 / w, in1=hp[:, 16:16 + S], op0=ALU.mult, op1=ALU.subtract)
            self.v("dve", "tensor_tensor", [cur.r(), invc.r()], [t16.r()], out=t16[:], in0=cur[:, 16:32], in1=invc[:, gi * 16:(gi + 1) * 16], op=ALU.mult)
            self.v("dve", "tensor_tensor", [t16.r(), hp.r(), pbf.r()], [pbf.r()], out=pbf[:, 0:16], in0=t16[:], in1=hp[:, 16:32],
                   op=ALU.subtract)
            for t8 in range(8):
                (pb,) = self.ps(1)
                self.mm(pb[:], pw[:, gi, :], pbf[:, t8 * 512:(t8 + 1) * 512], True, True, [pw.r(), pbf.r()], [pb.r()])
                self.v("dve", "tensor_scalar", [pb.r(), psc.r()], [ao.r()], out=ao[:, t8 * 512:(t8 + 1) * 512], in0=pb[:],
                       scalar1=psc[:, gi:gi + 1], scalar2=None, op0=ALU.mult)
            self.dma("sp", d["bout"][0, gi * 128:(gi + 1) * 128, :], ao[:], reads=[ao.r()], writes=[self.dr("bout", (0, gi))])

        wz = self.load_w("wz", d["w_in"][l][:, OFF_SGU:OFF_SGU + 1024], 8, 1024)
        wraw = self.alloc("wraw", [4, 128], F32)
        self.dma("sp", wraw[:], d["wsT"][l].rearrange("g s t -> s g t"), writes=[wraw.r()])
        tril = self.alloc("tril", [128], F32)
        self.dma("sp", tril[:], d["c_tril"], writes=[tril.r()])
        wsb = self.alloc("wsb", [4, 128], BF16)
        for g in range(4):
            self.v("dve", "tensor_tensor", [wraw.r(), tril.r()], [wsb.r()], out=wsb[:, g, :], in0=wraw[:, g, :], in1=tril[:], op=ALU.mult)
        lngc = self.alloc("lngc", [4], F32)
        lnbc = self.alloc("lnbc", [4], F32)
        self.dma("sp", lngc[:], d["sgu_ln_g_c"][l], writes=[lngc.r()])
        self.dma("sp", lnbc[:], d["sgu_ln_b_c"][l], writes=[lnbc.r()])
        bsb = self.load_bcast("bsb", d["b_s_r"][l:l + 1, :], 512)
        bias2 = self.alloc("bias2", [4, 128], F32)
        (prs,) = self.ps(1)
        for g in range(4):
            self.mm(prs[:, g * 128:(g + 1) * 128], self.ones[:], wsb[:, g, :], True, True, [self.ones.r(), wsb.r()], [prs.r()])
        for g in range(4):
            self.v("dve", "scalar_tensor_tensor", [prs.r(), lnbc.r(), bsb.r()], [bias2.r()], out=bias2[:, g, :], in0=prs[:, g * 128:(g + 1) * 128],
                   scalar=lnbc[:, g:g + 1], in1=bsb[:, g * 128:(g + 1) * 128], op0=ALU.mult, op1=ALU.add)
        uT = [self.alloc("uT", [4, 512], BF16) for _ in range(2)]
        dT = [self.alloc("dT", [4, 512], BF16) for _ in range(2)]
        tm = [self.alloc("tm", [4, 128], F32) for _ in range(2)]
        NR = 3
        vg = [self.alloc("vg", [512], F32) for _ in range(NR)]
        vf = [self.alloc("vf", [512], BF16) for _ in range(NR)]
        st6 = [self.alloc("st6", [12], F32) for _ in range(NR)]

        def sgu_u(t8):
            u_ = uT[t8 % 2]
            for c in range(4):
                (pb,) = self.ps(1)
                for k in range(8):
                    self.mm(pb[:], wz[:, k, c * 128:(c + 1) * 128], hT[:, k, t8 * 512:(t8 + 1) * 512], k == 0, k == 7, [wz.r(k)], [pb.r()])
                self.act(u_[:, c, :], pb[:], AF.Gelu, [pb.r()], [u_.r()])

        def sgu_s1(t):
            tok0 = t * 128
            vg_, vf_, s6 = vg[t % NR], vf[t % NR], st6[t % NR]
            (pb,) = self.ps(1)
            for k in range(8):
                self.mm(pb[:], hT[:, k, tok0:tok0 + 128], wz[:, k, 512:1024], k == 0, k == 7, [wz.r(k)], [pb.r()])
            self.act(vg_[:], pb[:], AF.Gelu, [pb.r()], [vg_.r()])
            self.v("dve", "bn_stats", [vg_.r()], [s6.r()], out=s6[:, 0:6], in_=vg_[:])
            self.v("dve", "bn_aggr", [s6.r()], [s6.r()], out=s6[:, 6:8], in_=s6[:, 0:6])
            rstd, rr = self.rstd_from_ss(s6[:, 7:8], s6.r(), 1)
            self.v("dve", "scalar_tensor_tensor", [s6.r(), rr], [s6.r()], out=s6[:, 8:9], in0=s6[:, 6:7], scalar=-1.0, in1=rstd,
                   op0=ALU.mult, op1=ALU.mult)
            self.act(vf_[:], vg_[:], AF.Identity, [vg_.r(), s6.r(), rr], [vf_.r()], bias=s6[:, 8:9], scale=rstd)

        def sgu_s2(t):
            t8, t4 = t // 4, t % 4
            u_ = uT[t8 % 2]
            d_ = dT[t8 % 2]
            vf_ = vf[t % NR]
            tm_ = tm[t % 2]
            (pb2,) = self.ps(1)
            for c in range(4):
                self.mm(pb2[:, c * 128:(c + 1) * 128], vf_[:, c * 128:(c + 1) * 128], wsb[:, c, :], True, True,
                        [vf_.r(), wsb.r()], [pb2.r()])
            for c in range(4):
                self.v("dve", "scalar_tensor_tensor", [pb2.r(), lngc.r(), bias2.r()], [tm_.r()], out=tm_[:, c, :], in0=pb2[:, c * 128:(c + 1) * 128],
                       scalar=lngc[:, c:c + 1], in1=bias2[:, c, :], op0=ALU.mult, op1=ALU.add)
            self.v("dve", "tensor_tensor", [tm_.r(), u_.r()], [d_.r()], out=d_[:, :, t4 * 128:(t4 + 1) * 128],
                   in0=tm_[:], in1=u_[:, :, t4 * 128:(t4 + 1) * 128], op=ALU.mult)
            if t4 == 3:
                self.dma("sp", d["bout"][3, :, t8 * 512:(t8 + 1) * 512].rearrange("(c p) t -> p c t", p=128), d_[:],
                         reads=[d_.r()], writes=[self.dr("bout", (3, t8))])

        LA = 2
        emitted_u = set()

        def need_u(t):
            t8 = t // 4
            if t8 not in emitted_u:
                emitted_u.add(t8)
                sgu_u(t8)

        def sgu_step(t8):
            for t in range(t8 * 4, t8 * 4 + 4):
                if t + LA < 32:
                    need_u(t + LA)
                    sgu_s1(t + LA)
                sgu_s2(t)

        need_u(0)
        for t in range(LA):
            sgu_s1(t)

        pool_a(0)
        for step in range(8):
            sgu_step(step)
            gi = step // 2
            if step % 2 == 0:
                pool_b(gi)
            elif gi + 1 < 4:
                pool_a(gi + 1)

    def mix_att(self, l, hT):
        d = self.d
        amask = self.alloc("amask", [2, 256], BF16)
        self.dma("pool", amask[:], d["c_att_mask"], writes=[amask.r()])
        onesAB = self.alloc("onesAB", [2, 128], BF16)
        self.v("dve", "memset", [], [onesAB.r()], onesAB[:], 0.0)
        self.v("dve", "memset", [], [onesAB.r()], onesAB[:, 0, 0:64], 1.0)
        self.v("dve", "memset", [], [onesAB.r()], onesAB[:, 1, 64:128], 1.0)
        wq = self.alloc("wq", [8, 128], BF16)
        wk = self.alloc("wk", [8, 128], BF16)
        wv = self.alloc("wv", [8, 128], BF16)
        qT = self.alloc("qT", [S], BF16)
        kTA = self.alloc("kTA", [S], BF16)
        kTB = self.alloc("kTB", [S], BF16)
        vT = self.alloc("vT", [S], BF16)
        self.v("dve", "memset", [], [kTA.r()], kTA[64:128, :], 0.0)
        self.v("dve", "memset", [], [kTB.r()], kTB[0:64, :], 0.0)
        vpad = self.alloc("vpad", [32, 2, 128], BF16)
        self.v("pool", "memset", [], [vpad.r()], vpad[:], 0.0)
        acc = self.alloc("acc", [2, S], F32)
        braw = self.alloc("braw", [2, 256], F32)
        ebias = self.alloc("ebias", [2, 256], BF16)
        E = [self.alloc("E", [2, 256], BF16) for _ in range(2)]
        PT = [self.alloc("PT", [2, 256], BF16) for _ in range(4)]
        bo = self.alloc("bo", [S], BF16)
        wsets = [(wq, wk, wv), (self.alloc("wq2", [8, 128], BF16), self.alloc("wk2", [8, 128], BF16), self.alloc("wv2", [8, 128], BF16))]
        braw2 = [braw, self.alloc("braw2", [2, 256], F32)]
        ebias2 = [ebias, self.alloc("ebias2", [2, 256], BF16)]
        E = E + [self.alloc("E3", [2, 256], BF16)]
        units = [(c, g) for c in range(4) for g in range(3)]

        def load_unit(u):
            c, g = units[u]
            wts = wsets[u % 2]
            for wi in range(3):
                col = OFF_ATT + wi * 1536 + g * 512 + c * 128
                self.dma("pool", wts[wi][:, :, 0:128], d["w_in"][l][:, col:col + 128].rearrange("(k p) n -> p k n", p=128),
                         writes=[wts[wi].r()])
            br_, eb_ = braw2[u % 2], ebias2[u % 2]
            self.dma("sp", br_[:], d["att_bias"][g, c], writes=[br_.r()])
            self.act(br_[:], br_[:], AF.Exp, [br_.r()], [br_.r()])
            self.v("dve", "tensor_tensor", [br_.r(), amask.r()], [eb_.r()], out=eb_[:], in0=br_[:], in1=amask[:], op=ALU.mult)

        load_unit(0)
        ipt = 0
        for u, (c, g) in enumerate(units):
            dil = DILS[g]
            L = S // dil
            nb = L // 128
            wts = wsets[u % 2]
            ebias_u = ebias2[u % 2]
            if u + 1 < len(units):
                load_unit(u + 1)
            for wi in range(3):
                for t8 in range(8):
                    (pb,) = self.ps(1)
                    for k in range(8):
                        self.mm(pb[:], wts[wi][:, k, 0:128], hT[:, k, t8 * 512:(t8 + 1) * 512], k == 0, k == 7, [wts[wi].r()], [pb.r()])
                    mw = 512 // dil
                    m0 = t8 * mw

                    def views(dst_, p0, p1):
                        ov = dst_[p0:p1, :].rearrange("p (r m) -> p r m", r=dil)[:, :, m0:m0 + mw]
                        iv = pb[p0:p1, :].rearrange("p (m r) -> p r m", r=dil)
                        return ov, iv
                    if wi == 0:
                        ov, iv = views(qT, 0, 128)
                        self.P.op("act", lambda e, ov=ov, iv=iv: e.mul(out=ov, in_=iv, mul=0.125), reads=[pb.r()], writes=[qT.r()])
                    elif wi == 1:
                        ov, iv = views(kTA, 0, 64)
                        self.v("dve", "tensor_copy", [pb.r()], [kTA.r()], out=ov, in_=iv)
                        ov, iv = views(kTB, 64, 128)
                        self.P.op("act", lambda e, ov=ov, iv=iv: e.copy(out=ov, in_=iv), reads=[pb.r()], writes=[kTB.r()])
                    else:
                        ov, iv = views(vT, 0, 128)
                        self.v("dve", "tensor_copy", [pb.r()], [vT.r()], out=ov, in_=iv)
            for b0 in range(0, 32, 8):
                (pb,) = self.ps(1)
                pv = pb[:].bitcast(BF16).rearrange("p (a b) -> p a b", a=8)
                for bi in range(8):
                    self.tr(pv[:, bi, :], vT[:, (b0 + bi) * 128:(b0 + bi + 1) * 128], self.ident[:], [vT.r(), self.ident.r()], [pb.r()])
                self.P.op("act", lambda e, pv=pv, b0=b0: e.copy(out=vpad[:, b0:b0 + 8, 0, 0:64], in_=pv[:, :, 0:64]), reads=[pb.r()], writes=[vpad.r()])
                self.v("dve", "tensor_copy", [pb.r()], [vpad.r()], out=vpad[:, b0:b0 + 8, 1, 64:128], in_=pv[:, :, 64:128])
            blocks = [(r, n) for r in range(dil) for n in range(nb)]
            ptof = {}

            def stage1(bi):
                nonlocal ipt
                r, n = blocks[bi]
                q0 = r * L + 128 * n
                nq = 256 if n + 1 < nb else 128
                (pb,) = self.ps(1)
                for hd in range(2):
                    kTh = (kTA, kTB)[hd]
                    self.mm(pb[:, hd * 256:hd * 256 + nq], kTh[:, q0:q0 + 128], qT[:, q0:q0 + nq], True, True, [kTh.r(), qT.r()], [pb.r()])
                e_ = E[ipt % 3]
                pt = PT[bi % 4]
                ipt += 1
                pv3 = pb[:].rearrange("p (h j) -> p h j", h=2)
                self.act(e_[:, :, 0:nq], pv3[:, :, 0:nq], AF.Exp, [pb.r()], [e_.r()])
                eng = "pool" if ipt % 2 == 0 else "dve"
                self.v(eng, "tensor_tensor", [e_.r(), ebias_u.r()], [pt.r()], out=pt[:, :, 0:nq], in0=e_[:, :, 0:nq],
                       in1=ebias_u[:, :, 0:nq], op=ALU.mult)
                ptof[bi] = pt

            def stage2(bi):
                r, n = blocks[bi]
                blk = r * nb + n
                pt = ptof[bi]
                (po,) = self.ps(1)
                srcs = [(blk, pt, 0)]
                if n > 0:
                    srcs.append((blk - 1, ptof[bi - 1], 128))
                nmm = 2 * len(srcs)
                for which in range(2):
                    i = 0
                    for (bk, ptile, j0) in srcs:
                        for hd in range(2):
                            lhs = vpad[:, bk, hd, :] if which == 0 else onesAB[:, hd, :]
                            self.mm(po[:, which * 128:(which + 1) * 128], lhs, ptile[:, hd, j0:j0 + 128], i == 0, i == nmm - 1,
                                    [vpad.r(), onesAB.r(), ptile.r()], [po.r()])
                            i += 1
                av = acc[:].rearrange("p a (m r) -> p a m r", r=dil)[:, :, 128 * n:128 * (n + 1), r]
                pov = po[:, 0:256].rearrange("p (a q) -> p a q", a=2)
                if g == 0:
                    self.P.op("act", lambda e, av=av, pov=pov: e.copy(out=av, in_=pov), reads=[po.r()], writes=[acc.r()])
                else:
                    self.v("dve", "tensor_tensor", [po.r(), acc.r()], [acc.r()], out=av, in0=pov, in1=av, op=ALU.add)

            LA = 2
            for bi in range(min(LA, len(blocks))):
                stage1(bi)
            for bi in range(len(blocks)):
                if bi + LA < len(blocks):
                    stage1(bi + LA)
                stage2(bi)
            if g == 2:
                self.act(acc[:, 1, :], acc[:, 1, :], AF.Ln, [acc.r()], [acc.r()])
                self.act(acc[:, 1, :], acc[:, 1, :], AF.Exp, [acc.r()], [acc.r()], scale=-1.0)
                self.v("pool", "tensor_tensor", [acc.r()], [bo.r()], out=bo[:], in0=acc[:, 0, :], in1=acc[:, 1, :], op=ALU.mult)
                self.dma("sp", d["bout"][1, c * 128:(c + 1) * 128, :], bo[:], reads=[bo.r()], writes=[self.dr("bout", (1, c))])

    def mix_s5(self, l, hT):
        d = self.d
        PI_ = float(np.pi)
        tabr = Res("s5tab")

        def T32(name, n=1):
            return self.alloc(name, [n, 32], F32) if n > 1 else self.alloc(name, [32], F32)

        def tt(out, a, b, op, eng="dve"):
            self.v(eng, "tensor_tensor", [tabr], [tabr], out=out, in0=a, in1=b, op=op)

        def ts(out, a, s1, op0, s2=None, op1=None, eng="dve"):
            kw = dict(out=out, in0=a, scalar1=s1, scalar2=s2, op0=op0)
            if op1 is not None:
                kw["op1"] = op1
            self.v(eng, "tensor_scalar", [tabr], [tabr], **kw)

        def actf(out, in_, func):
            self.act(out, in_, func, [tabr], [tabr])

        ldt, are, aim = T32("ldt"), T32("are"), T32("aim")
        self.dma("sp", ldt[:], d["s5_ldt"][l], writes=[tabr])
        self.dma("sp", are[:], d["s5_are"][l], writes=[tabr])
        self.dma("sp", aim[:], d["s5_aim"][l], writes=[tabr])
        sgn = self.alloc("sgn", [4], F32)
        self.dma("sp", sgn[:], d["c_sgn"], writes=[tabr])
        B1 = self.alloc("B1", [32, 16], F32)
        B2 = self.alloc("B2", [32, 16], F32)
        C1 = self.alloc("C1", [32, 16], F32)
        C2 = self.alloc("C2", [32, 16], F32)
        for t_, n_ in ((B1, "s5_b1"), (B2, "s5_b2"), (C1, "s5_c1"), (C2, "s5_c2")):
            self.dma("sp", t_[:], d[n_][l], writes=[tabr])
        swap = self.alloc("swap", [128], F32)
        bdm = self.alloc("bdm", [128], F32)
        rowm = self.alloc("rowm", [8], F32)
        colm = self.alloc("colm", [8, 128], F32)
        dcol = self.alloc("dcol", [4], F32)
        for t_, n_ in ((swap, "c_swap"), (bdm, "c_bdmask"), (rowm, "c_rowmask"), (colm, "c_colmask")):
            self.dma("sp", t_[:], d[n_], writes=[tabr])
        self.dma("sp", dcol[:], d["d_skip_c"][l], writes=[tabr])
        dt, lre, x1, mag, ang = T32("dt"), T32("lre"), T32("x1"), T32("mag"), T32("ang")
        actf(dt[:], ldt[:], AF.Exp)
        ts(lre[:], are[:], -1e-4, ALU.min)
        tt(x1[:], lre[:], dt[:], ALU.mult)
        actf(mag[:], x1[:], AF.Exp)
        tt(ang[:], aim[:], dt[:], ALU.mult)
        z, y_, yf, r_, m_ = T32("z"), T32("y"), T32("yf"), T32("r"), T32("m")
        yi = self.alloc("yi", [32], I32)

        def sin_of(dst, shift):
            ts(z[:], ang[:], shift, ALU.add)
            ts(y_[:], z[:], 1.0 / (2 * PI_), ALU.mult)
            self.v("dve", "tensor_copy", [tabr], [tabr], out=yi[:], in_=y_[:])
            self.v("dve", "tensor_copy", [tabr], [tabr], out=yf[:], in_=yi[:])
            self.v("dve", "scalar_tensor_tensor", [tabr], [tabr], out=r_[:], in0=yf[:], scalar=-2 * PI_, in1=z[:], op0=ALU.mult, op1=ALU.add)
            ts(m_[:], r_[:], PI_, ALU.is_gt, -2 * PI_, ALU.mult)
            tt(r_[:], r_[:], m_[:], ALU.add)
            ts(m_[:], r_[:], -PI_, ALU.is_lt, 2 * PI_, ALU.mult)
            tt(r_[:], r_[:], m_[:], ALU.add)
            ts(r_[:], r_[:], -3.14159, ALU.max, 3.14159, ALU.min)
            actf(dst, r_[:], AF.Sin)

        cosv, sinv, ar, ai = T32("cosv"), T32("sinv"), T32("ar"), T32("ai")
        sin_of(cosv[:], PI_ / 2)
        sin_of(sinv[:], 0.0)
        tt(ar[:], mag[:], cosv[:], ALU.mult)
        tt(ai[:], mag[:], sinv[:], ALU.mult)
        PR, PIm = T32("PR", 9), T32("PI", 9)
        t1, t2 = T32("t1"), T32("t2")
        self.v("dve", "memset", [tabr], [tabr], PR[:, 0, :], 1.0)
        self.v("dve", "memset", [tabr], [tabr], PIm[:, 0, :], 0.0)
        for j in range(8):
            tt(t1[:], PR[:, j, :], ar[:], ALU.mult)
            tt(t2[:], PIm[:, j, :], ai[:], ALU.mult)
            tt(PR[:, j + 1, :], t1[:], t2[:], ALU.subtract)
            tt(t1[:], PR[:, j, :], ai[:], ALU.mult)
            tt(t2[:], PIm[:, j, :], ar[:], ALU.mult)
            tt(PIm[:, j + 1, :], t1[:], t2[:], ALU.add)
        QR, QI, QIs = T32("QR", 9), T32("QI", 9), T32("QIs", 9)
        self.v("dve", "tensor_copy", [tabr], [tabr], out=QR[:, 0, :], in_=PR[:, 8, :])
        self.v("dve", "tensor_copy", [tabr], [tabr], out=QI[:, 0, :], in_=PIm[:, 8, :])
        for i in range(8):
            tt(t1[:], QR[:, i, :], QR[:, i, :], ALU.mult)
            tt(t2[:], QI[:, i, :], QI[:, i, :], ALU.mult)
            tt(QR[:, i + 1, :], t1[:], t2[:], ALU.subtract)
            tt(t1[:], QR[:, i, :], QI[:, i, :], ALU.mult)
            ts(QI[:, i + 1, :], t1[:], 2.0, ALU.mult)
        ts(QIs[:], QI[:], sgn[:, 1:2], ALU.mult)
        PIs1, TA, TB = T32("PIs1", 9), T32("TA", 9), T32("TB", 9)
        ts(PIs1[:], PIm[:], sgn[:, 0:1], ALU.mult)
        ts(TA[:], PR[:], sgn[:, 1:2], ALU.mult)
        ts(TB[:], PIm[:], -1.0, ALU.mult)
        den, am1, fr, fi, FIs, FIs2 = T32("den"), T32("am1"), T32("fr"), T32("fi"), T32("FIs"), T32("FIs2")
        tt(t1[:], lre[:], lre[:], ALU.mult)
        tt(t2[:], aim[:], aim[:], ALU.mult)
        tt(den[:], t1[:], t2[:], ALU.add)
        self.v("dve", "reciprocal", [tabr], [tabr], out=den[:], in_=den[:])
        ts(am1[:], ar[:], -1.0, ALU.add)
        tt(t1[:], am1[:], lre[:], ALU.mult)
        tt(t2[:], ai[:], aim[:], ALU.mult)
        tt(t1[:], t1[:], t2[:], ALU.add)
        tt(fr[:], t1[:], den[:], ALU.mult)
        tt(t1[:], ai[:], lre[:], ALU.mult)
        tt(t2[:], am1[:], aim[:], ALU.mult)
        tt(t1[:], t1[:], t2[:], ALU.subtract)
        tt(fi[:], t1[:], den[:], ALU.mult)
        ts(FIs[:], fi[:], sgn[:, 0:1], ALU.mult)
        ts(FIs2[:], fi[:], sgn[:, 1:2], ALU.mult)

        def bc(tab_ap_q):
            return tab_ap_q.unsqueeze(2).to_broadcast([128, 8, 16])

        bs = self.alloc("bs", [8, 16], F32)
        bw = self.alloc("bw", [8, 16], F32)
        tf1 = self.alloc("tf1", [8, 16], F32)
        tf2 = self.alloc("tf2", [8, 16], F32)
        XB = self.alloc("XB", [8, 128], BF16)
        CP = self.alloc("CP", [8, 128], BF16)
        Cst = self.alloc("Cst", [128], BF16)
        XA = self.alloc("XA", [8, 128], BF16)
        BD = self.alloc("BD", [8, 128], BF16)
        wu = self.alloc("wu5", [8, 128], BF16)
        uq = self.alloc("uq", [S], BF16)
        gq = self.alloc("gq", [S], BF16)
        GS = [self.alloc("GS", [8, 128], BF16) for _ in range(4)]
        AM = [self.alloc("AM", [9, 128], BF16) for _ in range(4)]
        CPm = [self.alloc("CPm", [8, 128], BF16) for _ in range(8)]
        H = [[self.alloc("H", [512], BF16) for _ in range(2)] for _ in range(4)]
        HP = self.alloc("HP", [8, 512], BF16)
        self.v("pool", "memset", [], [HP.r(g8) for g8 in range(8)], HP[:], 0.0)
        ytmp = [self.alloc("ytmp", [512], F32) for _ in range(2)]
        amt9 = [[self.alloc("amt9", [9, 128], BF16) for _ in range(2)] for _ in range(2)]
        swapb = self.alloc("swapb", [128], BF16)
        self.v("dve", "tensor_copy", [tabr], [swapb.r()], out=swapb[:], in_=swap[:])
        colmb = self.alloc("colmb", [8, 128], BF16)
        self.v("dve", "tensor_copy", [tabr], [colmb.r()], out=colmb[:], in_=colm[:])
        for q in range(4):
            gsl = slice(q * 8, (q + 1) * 8)
            tt(bs[:], B1[:, gsl, :], bc(fr[:, gsl]), ALU.mult)
            tt(tf1[:], B2[:, gsl, :], bc(FIs[:, gsl]), ALU.mult)
            tt(bs[:], bs[:], tf1[:], ALU.add)
            tt(bw[:], B2[:, gsl, :], bc(fr[:, gsl]), ALU.mult)
            tt(tf1[:], B1[:, gsl, :], bc(FIs2[:, gsl]), ALU.mult)
            tt(bw[:], bw[:], tf1[:], ALU.add)
            xbr, cpr = XB.r(), CP.r()
            for j in range(8):
                tt(tf1[:], bs[:], bc(PR[:, j, gsl]), ALU.mult)
                tt(tf2[:], bw[:], bc(PIs1[:, j, gsl]), ALU.mult)
                self.v("dve", "tensor_tensor", [tabr], [tabr, xbr], out=XB[:, j, :].rearrange("p (g c) -> p g c", g=8), in0=tf1[:], in1=tf2[:], op=ALU.add)
            for t in range(8):
                tt(tf1[:], C1[:, gsl, :], bc(TA[:, t + 1, gsl]), ALU.mult)
                tt(tf2[:], C2[:, gsl, :], bc(TB[:, t + 1, gsl]), ALU.mult)
                self.v("dve", "tensor_tensor", [tabr], [tabr, cpr], out=CP[:, t, :].rearrange("p (g c) -> p g c", g=8), in0=tf1[:], in1=tf2[:], op=ALU.add)
            self.v("dve", "tensor_scalar", [tabr], [tabr, Cst.r()], out=Cst[:].rearrange("p (g c) -> p g c", g=8), in0=C1[:, gsl, :],
                   scalar1=sgn[:, 1:2], scalar2=None, op0=ALU.mult)
            (pbx,) = self.ps(1)
            pxv = pbx[:].bitcast(BF16).rearrange("p (a b) -> p a b", a=8)
            for s_ in range(8):
                self.tr(pxv[:, s_, :], XB[:, 7 - s_, :], self.ident[:], [xbr, self.ident.r()], [pbx.r()])
            self.P.op("act", lambda e, pxv=pxv: e.copy(out=XA[:], in_=pxv), reads=[pbx.r()], writes=[XA.r()])
            for jh in range(2):
                (pbd,) = self.ps(1)
                for jj in range(4):
                    self.mm(pbd[:, jj * 128:(jj + 1) * 128], XB[:, jh * 4 + jj, :], Cst[:], True, True, [xbr, Cst.r()], [pbd.r()])
                self.v("dve", "tensor_tensor", [pbd.r(), tabr], [BD.r()], out=BD[:, jh * 4:(jh + 1) * 4, :],
                       in0=pbd[:].rearrange("p (j c) -> p j c", j=4), in1=bdm[:].unsqueeze(1).to_broadcast([128, 4, 128]), op=ALU.mult)
            col = OFF_SSM + q * 128
            self.dma("pool", wu[:], d["w_in"][l][:, col:col + 128].rearrange("(k p) n -> p k n", p=128), writes=[wu.r()])
            for t8 in range(8):
                (pb,) = self.ps(1)
                for k in range(8):
                    self.mm(pb[:], wu[:, k, :], hT[:, k, t8 * 512:(t8 + 1) * 512], k == 0, k == 7, [wu.r()], [pb.r()])
                self.P.op("act", lambda e, pb=pb, t8=t8: e.copy(out=uq[:, t8 * 512:(t8 + 1) * 512], in_=pb[:]), reads=[pb.r()], writes=[uq.r()])
            uq8 = uq[:].rearrange("p (k s) -> p s k", s=8)
            for st4 in range(2):
                grp = [st4 * 4 + i for i in range(4)]
                for gi, g8 in enumerate(grp):
                    g = q * 8 + g8
                    self.P.op("act", lambda e, gi=gi, g8=g8: e.activation(out=GS[gi][:], in_=XA[:], func=AF.Copy, scale=rowm[:, g8:g8 + 1]),
                              reads=[XA.r(), tabr], writes=[GS[gi].r()])
                    self.v("dve", "tensor_tensor", [cpr, colmb.r()], [CPm[g8].r()], out=CPm[g8][:], in0=CP[:],
                           in1=colmb[:, g8, :].unsqueeze(1).to_broadcast([128, 8, 128]), op=ALU.mult)
                    am_a = amt9[gi % 2][0]
                    am_b = amt9[gi % 2][1]
                    for i in range(9):
                        self.P.op("act", lambda e, am_a=am_a, i=i, g=g: e.activation(out=am_a[:, i, :], in_=self.ident[:], func=AF.Copy, scale=QR[:, i, g:g + 1]),
                                  reads=[tabr, self.ident.r()], writes=[am_a.r()])
                        self.P.op("act", lambda e, am_b=am_b, i=i, g=g: e.activation(out=am_b[:, i, :], in_=swapb[:], func=AF.Copy, scale=QIs[:, i, g:g + 1]),
                                  reads=[tabr, swapb.r()], writes=[am_b.r()])
                    self.v("dve", "tensor_tensor", [am_a.r(), am_b.r()], [AM[gi].r()], out=AM[gi][:], in0=am_a[:], in1=am_b[:], op=ALU.add)
                cur = {}
                for gi, g8 in enumerate(grp):
                    (pb,) = self.ps(1)
                    for s_ in range(8):
                        self.mm(pb[:], GS[gi][:, s_, :], uq8[:, s_, :], s_ == 0, s_ == 7, [GS[gi].r(), uq.r()], [pb.r()])
                    self.P.op("act", lambda e, pb=pb, gi=gi: e.copy(out=H[gi][0][:], in_=pb[:]), reads=[pb.r()], writes=[H[gi][0].r()])
                    cur[gi] = 0
                for i in range(9):
                    dd = 1 << i
                    for gi, g8 in enumerate(grp):
                        hc = H[gi][cur[gi]]
                        (pb,) = self.ps(1)
                        if i < 8:
                            self.mm(pb[:, dd:512], AM[gi][:, i, :], hc[:, 0:512 - dd], True, True, [AM[gi].r(), hc.r()], [pb.r()])
                            self.v("dve", "tensor_tensor", [pb.r(), hc.r()], [hc.r()], out=hc[:, dd:512], in0=pb[:, dd:512], in1=hc[:, dd:512], op=ALU.add)
                            continue
                        self.mm(pb[:, 0:dd], self.ident[:], hc[:, 0:dd], True, True, [self.ident.r(), hc.r()], [pb.r()])
                        self.mm(pb[:, dd:512], self.ident[:], hc[:, dd:512], True, False, [self.ident.r(), hc.r()], [pb.r()])
                        self.mm(pb[:, dd:512], AM[gi][:, i, :], hc[:, 0:512 - dd], False, True, [AM[gi].r(), hc.r()], [pb.r()])
                        if i < 8:
                            hn = H[gi][1 - cur[gi]]
                            self.P.op("act", lambda e, pb=pb, hn=hn: e.copy(out=hn[:], in_=pb[:]), reads=[pb.r()], writes=[hn.r()])
                            cur[gi] = 1 - cur[gi]
                        else:
                            self.v("dve", "tensor_copy", [pb.r()], [HP.r(g8)], out=HP[:, g8, 1:512], in_=pb[:, 0:511])
            for t in range(8):
                (py,) = self.ps(1)
                n_mm = (t + 1) + 8
                i_mm = 0
                for s_ in range(t + 1):
                    self.mm(py[:], BD[:, t - s_, :], uq8[:, s_, :], i_mm == 0, i_mm == n_mm - 1, [BD.r(), uq.r()], [py.r()])
                    i_mm += 1
                for g8 in range(8):
                    self.mm(py[:], CPm[g8][:, t, :], HP[:, g8, :], i_mm == 0, i_mm == n_mm - 1, [CPm[g8].r(), HP.r(g8)], [py.r()])
                    i_mm += 1
                yt_ = ytmp[t % 2]
                self.v("dve", "scalar_tensor_tensor", [uq.r(), py.r(), tabr], [yt_.r()], out=yt_[:], in0=uq8[:, t, :], scalar=dcol[:, q:q + 1],
                       in1=py[:], op0=ALU.mult, op1=ALU.add)
                self.act(gq[:].rearrange("p (k s) -> p s k", s=8)[:, t, :], yt_[:], AF.Gelu, [yt_.r()], [gq.r()])
            self.dma("sp", d["gsc"][q * 128:(q + 1) * 128, :], gq[:], reads=[gq.r()], writes=[self.dr("gsc", q)])

    def mix_s5_glu(self, l):
        d = self.d
        wgl = self.load_w("wglu", d["w_glu"][l], 4, 512)
        bgl = self.alloc("bgl", [4], F32)
        self.dma("sp", bgl[:], d["b_glu_c"][l], writes=[bgl.r()])
        gT = [self.alloc("gTt", [4, 512], BF16) for _ in range(2)]
        co = [self.alloc("co", [4, 512], BF16) for _ in range(2)]
        sg = [self.alloc("sg5", [512], F32) for _ in range(2)]
        for t8 in range(8):
            g_ = gT[t8 % 2]
            c_ = co[t8 % 2]
            self.dma("sp", g_[:], d["gsc"][:, t8 * 512:(t8 + 1) * 512].rearrange("(c p) t -> p c t", p=128),
                     reads=[self.dr("gsc", q) for q in range(4)], writes=[g_.r()])
            for oc in range(4):
                s_ = sg[oc % 2]
                (pz,) = self.ps(1)
                for kc in range(4):
                    self.mm(pz[:], wgl[:, kc, oc * 128:(oc + 1) * 128], g_[:, kc, :], kc == 0, kc == 3, [wgl.r(kc), g_.r()], [pz.r()])
                self.act(s_[:], pz[:], AF.Sigmoid, [pz.r(), bgl.r()], [s_.r()], bias=bgl[:, oc:oc + 1])
                eng = "pool" if oc % 2 == 0 else "dve"
                self.v(eng, "tensor_tensor", [s_.r(), g_.r()], [c_.r()], out=c_[:, oc, :], in0=s_[:], in1=g_[:, oc, :], op=ALU.mult)
            self.dma("sp", d["bout"][2, :, t8 * 512:(t8 + 1) * 512].rearrange("(c p) t -> p c t", p=128), c_[:],
                     reads=[c_.r()], writes=[self.dr("bout", (2, t8))])

    def mix_merge_a(self, l, hT, h2):
        d = self.d
        wg = self.alloc("wg", [8, 4, 512], BF16)
        for i in range(4):
            col = OFF_GATE + i * 1024 + h2 * 512
            self.dma("pool", wg[:, :, i, :], d["w_in"][l][:, col:col + 512].rearrange("(k p) n -> p k n", p=128), writes=[wg.r(i)])
        wu = self.alloc("wu", [16, 512], BF16)
        for i in range(4):
            self.dma("pool", wu[:, i * 4:(i + 1) * 4, :], d["w_up"][l, i][:, h2 * 512:(h2 + 1) * 512].rearrange("(k p) n -> p k n", p=128),
                     writes=[wu.r(i)])
        gb = self.alloc("gb", [32], F32)
        self.dma("sp", gb[:], d["gate_b_c"][l], writes=[gb.r()])
        brT = [self.alloc("brT", [16, 512], BF16) for _ in range(2)]
        mT = [self.alloc("mT", [4, 512], BF16) for _ in range(2)]
        sg = [self.alloc("sg", [512], F32) for _ in range(2)]
        pr = [self.alloc("pr", [512], F32) for _ in range(2)]
        ac = [self.alloc("ac", [512], F32) for _ in range(2)]
        isg = 0
        for t8 in range(8):
            br = brT[t8 % 2]
            m_ = mT[t8 % 2]
            for i in range(4):
                self.dma("sp", br[:, i * 4:(i + 1) * 4, :], d["bout"][i, :, t8 * 512:(t8 + 1) * 512].rearrange("(c p) t -> p c t", p=128),
                         reads=[self.dr("bout", (i, x)) for x in range(8)], writes=[br.r(i)])
            for dcl in range(4):
                dc = h2 * 4 + dcl
                a_ = ac[dcl % 2]
                for i in range(4):
                    s_ = sg[isg % 2]
                    p_ = pr[isg % 2]
                    isg += 1
                    (pg,) = self.ps(1)
                    for k in range(8):
                        self.mm(pg[:], wg[:, k, i, dcl * 128:(dcl + 1) * 128], hT[:, k, t8 * 512:(t8 + 1) * 512], k == 0, k == 7, [wg.r(i)], [pg.r()])
                    self.act(s_[:], pg[:], AF.Sigmoid, [pg.r(), gb.r()], [s_.r()], bias=gb[:, i * 8 + dc:i * 8 + dc + 1])
                    (pu,) = self.ps(1)
                    for kc in range(4):
                        self.mm(pu[:], wu[:, i * 4 + kc, dcl * 128:(dcl + 1) * 128], br[:, i * 4 + kc, :], kc == 0, kc == 3, [wu.r(i), br.r(i)], [pu.r()])
                    if i == 0:
                        self.v("dve", "tensor_tensor", [pu.r(), s_.r()], [a_.r()], out=a_[:], in0=pu[:], in1=s_[:], op=ALU.mult)
                    else:
                        self.v("dve", "tensor_tensor", [pu.r(), s_.r()], [p_.r()], out=p_[:], in0=pu[:], in1=s_[:], op=ALU.mult)
                        if i < 3:
                            self.v("dve", "tensor_tensor", [p_.r(), a_.r()], [a_.r()], out=a_[:], in0=p_[:], in1=a_[:], op=ALU.add)
                        else:
                            self.v("dve", "tensor_tensor", [p_.r(), a_.r()], [m_.r()], out=m_[:, dcl, :], in0=p_[:], in1=a_[:], op=ALU.add)
            self.dma("sp", d["mrg"][h2 * 512:(h2 + 1) * 512, t8 * 512:(t8 + 1) * 512].rearrange("(c p) t -> p c t", p=128), m_[:],
                     reads=[m_.r()], writes=[self.dr("mrg", (h2, t8))])

    def mix_merge_b(self, l, src, dst):
        d = self.d
        wo = self.load_w("wout", d["w_out"][l], 8, D)
        gpost = self.load_bcast("gpost", d["g_mix_post"][l:l + 1, :], D)
        mT = [self.alloc("mTb", [8, 512], BF16) for _ in range(2)]
        xt = [self.alloc("xtb", [D], F32) for _ in range(2)]
        for t8 in range(8):
            m_ = mT[t8 % 2]
            self.dma("sp", m_[:], d["mrg"][:, t8 * 512:(t8 + 1) * 512].rearrange("(c p) t -> p c t", p=128),
                     reads=[self.dr("mrg", (0, t8)), self.dr("mrg", (1, t8))], writes=[m_.r()])
            for t4 in range(4):
                tt = t8 * 4 + t4
                x_ = xt[tt % 2]
                self.dma("sp", x_[:], d[src][tt * 128:(tt + 1) * 128, :], reads=[self.dr(src, tt)], writes=[x_.r()])
                pbs = self.ps(2)
                for half in range(2):
                    for dc in range(8):
                        self.mm(pbs[half][:], m_[:, dc, t4 * 128:(t4 + 1) * 128], wo[:, dc, half * 512:(half + 1) * 512], dc == 0, dc == 7,
                                [m_.r(), wo.r(dc)], [pbs[half].r()])
                self.post_tile(pbs, x_[:], x_.r(), gpost, d[dst][tt * 128:(tt + 1) * 128, :], self.dr(dst, tt), self.junk)

W_SHAPES = {
    "g_mix_pre": (DEPTH, D), "g_mix_post": (DEPTH, D), "w_in": (DEPTH, D, IN_WIDTH), "gate_b": (DEPTH, 4, D),
    "w_up": (DEPTH, 4, 512, D), "w_out": (DEPTH, D, D),
    "g_x_pre": (DEPTH, D), "g_x_post": (DEPTH, D), "g_mem": (DEPTH, D),
    "w_cq": (DEPTH, D, 512), "w_ckv": (DEPTH, D, 1024), "w_co": (DEPTH, 512, D),
    "g_ff_pre": (DEPTH, D), "g_ff_post": (DEPTH, D), "w_ff1": (DEPTH, D, 4096), "w_ff2": (DEPTH, 4096, D),
}
W_SHAPES.update({
    "pool_w": (DEPTH, 4, 128, 128), "pool_scale_c": (DEPTH, 128, 4), "wsT": (DEPTH, 4, 128, 128),
    "sgu_ln_g_c": (DEPTH, 128, 4), "sgu_ln_b_c": (DEPTH, 128, 4), "b_s_r": (DEPTH, 512), "gate_b_c": (DEPTH, 128, 32),
    "att_bias": (3, 4, 128, 2, 256),
    "s5_are": (DEPTH, 128, 32), "s5_aim": (DEPTH, 128, 32), "s5_ldt": (DEPTH, 128, 32),
    "s5_b1": (DEPTH, 128, 32, 16), "s5_b2": (DEPTH, 128, 32, 16), "s5_c1": (DEPTH, 128, 32, 16), "s5_c2": (DEPTH, 128, 32, 16),
    "d_skip_c": (DEPTH, 128, 4), "b_glu_c": (DEPTH, 128, 4), "w_glu": (DEPTH, 512, 512),
})
CONSTS = {"c_ident": (128, 128), "c_invc": (1, 64), "c_tril": (128, 128), "c_att_mask": (128, 2, 256),
          "c_sgn": (128, 4), "c_swap": (128, 128), "c_bdmask": (128, 128), "c_rowmask": (128, 8), "c_colmask": (128, 8, 128)}
DEBUG_BOUT = False


def build_program(plan):
    nc = bass.Bass("TRN2", target_bir_lowering=False)
    dram = {}
    dram["x"] = nc.dram_tensor("x", [S, D], F32, kind="ExternalInput").ap()
    dram["mem"] = nc.dram_tensor("mem", [NMEM, D], F32, kind="ExternalInput").ap()
    for n, shp in W_SHAPES.items():
        dram[n] = nc.dram_tensor(n, list(shp), F32, kind="ExternalInput").ap()
    for n, shp in CONSTS.items():
        dram[n] = nc.dram_tensor(n, list(shp), F32, kind="ExternalInput").ap()
    dram["y"] = nc.dram_tensor("y", [S, D], F32, kind="ExternalOutput").ap()
    dram["xr"] = nc.dram_tensor("xr", [S, D], F32, kind="Internal").ap()
    dram["bout"] = nc.dram_tensor("bout", [4, 512, S], BF16, kind="ExternalOutput" if DEBUG_BOUT else "Internal").ap()
    dram["mrg"] = nc.dram_tensor("mrg", [D, S], BF16, kind="Internal").ap()
    dram["gsc"] = nc.dram_tensor("gsc", [512, S], BF16, kind="Internal").ap()
    with contextlib.ExitStack() as st:
        kb = KB(nc, st, dram)
        kb.setup_consts()
        kb.mark_perm()
        for i, item in enumerate(plan):
            kind, l = item[0], item[1]
            src = "x" if i == 0 else "xr"
            dst = "y" if i == len(plan) - 1 else "xr"
            getattr(kb, kind)(l, src, dst, *item[2:])
        kb.P.emit()
        nops = kb.P.nops
    return nc, nops


def _t5_bucket(n):
    exact = 16
    nf = np.maximum(n, 1).astype(np.float32)
    large = exact + (np.log(nf / exact) / np.log(2048 / exact) * (32 - exact)).astype(np.int32)
    large = np.minimum(large, 31)
    return np.where(n < exact, n, large).astype(np.int32)


def host_consts():
    c = {"c_ident": np.eye(128, dtype=np.float32)}
    invc = np.zeros((1, 64), np.float32)
    for gi in range(4):
        w = 2 << gi
        invc[0, gi * 16:(gi + 1) * 16] = 1.0 / np.minimum(np.arange(16) + 1, w)
    c["c_invc"] = invc
    s_ = np.arange(128)
    c["c_tril"] = (s_[:, None] <= s_[None, :]).astype(np.float32)
    dist = np.arange(256)[None, :] - np.arange(128)[:, None]
    m = ((dist >= 0) & (dist <= 128)).astype(np.float32)
    c["c_att_mask"] = np.ascontiguousarray(np.broadcast_to(m[:, None, :], (128, 2, 256)))
    sg = np.ones((128, 4), np.float32)
    sg[:64, 0] = -1.0
    sg[64:, 1] = -1.0
    c["c_sgn"] = sg
    p = np.arange(128)
    c["c_swap"] = (p[None, :] == ((p[:, None] + 64) % 128)).astype(np.float32)
    c["c_bdmask"] = ((p[:, None] // 16) == (p[None, :] // 16)).astype(np.float32)
    c["c_rowmask"] = ((p[:, None] // 16) == np.arange(8)[None, :]).astype(np.float32)
    c["c_colmask"] = np.ascontiguousarray(np.broadcast_to(((p[None, None, :] // 16) == np.arange(8)[None, :, None]), (128, 8, 128)).astype(np.float32))
    return c


def host_layout(inputs):
    f = lambda n: np.asarray(inputs[n], dtype=np.float32)
    o = {}
    for n in ("g_mix_pre", "g_mix_post", "w_in", "w_up", "w_out", "g_x_pre", "g_x_post", "g_mem", "w_cq", "w_ckv", "w_co",
              "g_ff_pre", "g_ff_post", "w_ff1", "w_ff2", "pool_w"):
        o[n] = f(n)
    o["gate_b"] = f("gate_b")
    o["pool_scale_c"] = f("pool_scale").reshape(DEPTH, 4, 128).transpose(0, 2, 1)
    o["gate_b_c"] = f("gate_b").reshape(DEPTH, 4, 8, 128).transpose(0, 3, 1, 2).reshape(DEPTH, 128, 32)
    o["wsT"] = f("w_s").transpose(0, 1, 3, 2)
    o["b_s_r"] = f("b_s").reshape(DEPTH, 512)
    o["sgu_ln_g_c"] = f("sgu_ln_g").reshape(DEPTH, 4, 128).transpose(0, 2, 1)
    o["sgu_ln_b_c"] = f("sgu_ln_b").reshape(DEPTH, 4, 128).transpose(0, 2, 1)
    rb = f("rel_bias")
    dist = np.clip(np.arange(256)[None, :] - np.arange(128)[:, None], 0, 128)
    ab = np.zeros((3, 4, 128, 2, 256), np.float32)
    for g, dil in enumerate(DILS):
        bk = _t5_bucket(dist * dil)
        for c in range(4):
            for hd in range(2):
                ab[g, c, :, hd, :] = rb[bk, g * 8 + 2 * c + hd]
    o["att_bias"] = ab
    o["w_glu"] = f("w_glu")
    dup = lambda a: np.concatenate([a, a], axis=1)
    o["s5_are"] = dup(f("a_re").transpose(0, 2, 1))
    o["s5_aim"] = dup(f("a_im").transpose(0, 2, 1))
    o["s5_ldt"] = np.broadcast_to(f("log_dt")[:, None, :], (DEPTH, 128, 32))
    brt, bit = f("b_re").transpose(0, 2, 1, 3), f("b_im").transpose(0, 2, 1, 3)
    crt, cit = f("c_re").transpose(0, 3, 1, 2), f("c_im").transpose(0, 3, 1, 2)
    o["s5_b1"] = np.concatenate([brt, bit], axis=1)
    o["s5_b2"] = np.concatenate([bit, brt], axis=1)
    o["s5_c1"] = np.concatenate([crt, cit], axis=1)
    o["s5_c2"] = np.concatenate([cit, crt], axis=1)
    o["d_skip_c"] = f("d_skip").reshape(DEPTH, 4, 128).transpose(0, 2, 1)
    o["b_glu_c"] = f("b_glu").reshape(DEPTH, 4, 128).transpose(0, 2, 1)
    return {k: np.ascontiguousarray(v, dtype=np.float32) for k, v in o.items()}


FULL_PLAN = [(k, l) for l in range(DEPTH) for k in ("mixer", "cross", "ffn")]


def kernel(**inputs):
    plan = inputs.pop("_plan", FULL_PLAN)
    cores = inputs.pop("_cores", 8)
    import time as _t
    _t0 = _t.time()
    nc, _nops = build_program(plan)
    print(f"[kernel] build {_t.time() - _t0:.1f}s nops={_nops}", flush=True)
    lay = host_layout(inputs)
    shared = {n: lay[n] for n in W_SHAPES}
    shared.update(host_consts())
    x = np.asarray(inputs["x"], dtype=np.float32)
    mem = np.asarray(inputs["mem"], dtype=np.float32)
    in_maps = []
    for c in range(cores):
        m = dict(shared)
        m["x"] = np.ascontiguousarray(x[c])
        m["mem"] = np.ascontiguousarray(mem[c])
        in_maps.append(m)
    _t0 = _t.time()
    res = run_bass_kernel_spmd(nc, in_maps, core_ids=list(range(cores)))
    print(f"[kernel] run {_t.time() - _t0:.1f}s", flush=True)
    if DEBUG_BOUT:
        global _LAST_BOUT
        _LAST_BOUT = [np.asarray(r["bout"]) for r in res.results]
    return np.stack([np.asarray(r["y"], dtype=np.float32) for r in res.results], axis=0)
```

```python
import contextlib
import numpy as np
import concourse.bass as bass
import concourse.mybir as mybir
from concourse.bass_utils import run_bass_kernel_spmd

F32 = mybir.dt.float32
BF16 = mybir.dt.bfloat16
U8 = mybir.dt.uint8
I32 = mybir.dt.int32
ALU = mybir.AluOpType
AF = mybir.ActivationFunctionType

SEM_CAP = 30000
DMA_SLOTS = 8
ARENA_BYTES = 212480

S = 4096
D = 1024
NT = S // 128
DEPTH = 4
EPS = 1e-6
NMEM = 256
IN_WIDTH = 10752
OFF_POOL, OFF_ATT, OFF_SSM, OFF_SGU, OFF_GATE = 0, 512, 5120, 5632, 6656
DILS = (1, 4, 16)


class Res:
    __slots__ = ("name", "lw", "rd")

    def __init__(self, name=""):
        self.name = name
        self.lw = None
        self.rd = []


class T:
    def __init__(self, apview, name):
        self.v = apview
        self.name = name
        self._res = {}

    def __getitem__(self, k):
        return self.v[k]

    def r(self, key=None):
        x = self._res.get(key)
        if x is None:
            x = Res(f"{self.name}:{key}")
            self._res[key] = x
        return x


class Op:
    __slots__ = ("eng", "fn", "deps", "signal", "dma", "slot", "use", "idx", "cnt")

    def __init__(self, eng, fn, dma):
        self.eng = eng
        self.fn = fn
        self.deps = []
        self.signal = False
        self.dma = dma
        self.slot = None
        self.use = None
        self.idx = None
        self.cnt = None


class Prog:
    ENGS = ("pe", "act", "dve", "pool", "sp")

    def __init__(self, nc):
        self.nc = nc
        self.ops = {e: [] for e in self.ENGS}
        self.ndma = {e: 0 for e in self.ENGS}
        self.waited = {e: {} for e in self.ENGS}
        self.pending = {e: [] for e in self.ENGS}
        self.nops = 0

    def _need(self, op, key):
        if key is None:
            return
        X = op.eng
        if key[0] == "c":
            _, Y, j = key
            if Y == X and not op.dma:
                return
            k = ("c", Y)
            if self.waited[X].get(k, -1) >= j:
                return
            self.waited[X][k] = j
            self.ops[Y][j].signal = True
            op.deps.append(key)
        else:
            _, q, slot, use = key
            k = ("d", q, slot)
            if self.waited[X].get(k, 0) >= use:
                return
            self.waited[X][k] = use
            op.deps.append(key)

    def _same(self, o, key):
        eng = o.eng
        if eng == "pe":
            return
        k = ("c", eng)
        if self.waited[eng].get(k, -1) < key[2]:
            self.waited[eng][k] = key[2]
            self.ops[eng][key[2]].signal = True
            o.deps.append(key)

    def barrier(self):
        keys = []
        for e in self.ENGS:
            if self.ops[e]:
                last = None
                for o in reversed(self.ops[e]):
                    if not o.dma:
                        last = o
                        break
                if last is not None:
                    keys.append(("c", e, last.idx))
            n = self.ndma[e]
            for s in range(min(n, DMA_SLOTS)):
                keys.append(("d", e, s, (n - 1 - s) // DMA_SLOTS + 1))
        for e in self.ENGS:
            self.pending[e] = list(keys)

    def op(self, eng, fn, reads=(), writes=(), dma=False):
        o = Op(eng, fn, dma)
        o.idx = len(self.ops[eng])
        if self.pending[eng]:
            for k in self.pending[eng]:
                if k[0] == "c" and k[1] == eng:
                    if dma:
                        self._need(o, k)
                    continue
                self._need(o, k)
            self.pending[eng] = []
        if dma:
            n = self.ndma[eng]
            self.ndma[eng] = n + 1
            o.slot = n % DMA_SLOTS
            o.use = n // DMA_SLOTS + 1
            mykey = ("d", eng, o.slot, o.use)
            if o.use > 1:
                self._need(o, ("d", eng, o.slot, o.use - 1))
        else:
            mykey = ("c", eng, o.idx)
        for r in reads:
            lw = r.lw
            if lw is not None:
                if lw[0] == "c" and lw[1] == eng and not dma:
                    self._same(o, lw)
                else:
                    self._need(o, lw)
        for w in writes:
            if w.lw is not None:
                if w.lw[0] == "c" and w.lw[1] == eng and not dma:
                    self._same(o, w.lw)
                else:
                    self._need(o, w.lw)
            for rk in w.rd:
                if rk[0] == "c" and rk[1] == eng and not dma:
                    self._same(o, rk)
                else:
                    self._need(o, rk)
        for r in reads:
            r.rd.append(mykey)
            if len(r.rd) > 48:
                last = {}
                for kk in r.rd:
                    kid = kk[:2] if kk[0] == "c" else kk[:3]
                    if kid not in last or last[kid][-1] < kk[-1]:
                        last[kid] = kk
                r.rd = list(last.values())
        for w in writes:
            w.lw = mykey
            w.rd = []
        self.ops[eng].append(o)
        self.nops += 1
        return o

    def emit(self):
        nc = self.nc
        nsig = {}
        for e in self.ENGS:
            c = 0
            for o in self.ops[e]:
                if o.signal and not o.dma:
                    c += 1
                    o.cnt = c
            nsig[e] = c
        with contextlib.ExitStack() as st:
            csem = {}
            for e in self.ENGS:
                n = max(1, -(-nsig[e] // SEM_CAP))
                csem[e] = [st.enter_context(nc.semaphore(f"c_{e}_{i}")) for i in range(n)]
            dsem = {}
            for e in self.ENGS:
                if self.ndma[e]:
                    dsem[e] = [st.enter_context(nc.semaphore(f"d_{e}_{i}")) for i in range(DMA_SLOTS)]
            block = st.enter_context(nc.Block())

            def run(eng_name, engobj):
                for o in self.ops[eng_name]:
                    for d in o.deps:
                        if d[0] == "c":
                            p = self.ops[d[1]][d[2]]
                            c = p.cnt - 1
                            engobj.wait_ge(csem[d[1]][c // SEM_CAP], c % SEM_CAP + 1)
                        else:
                            engobj.wait_ge(dsem[d[1]][d[2]], 16 * d[3])
                    ins = o.fn(engobj)
                    if o.dma:
                        ins.then_inc(dsem[eng_name][o.slot], 16)
                    elif o.signal:
                        c = o.cnt - 1
                        ins.then_inc(csem[eng_name][c // SEM_CAP], 1)

            @block.tensor
            def _(e):
                run("pe", e)

            @block.scalar
            def _(e):
                run("act", e)

            @block.vector
            def _(e):
                run("dve", e)

            @block.gpsimd
            def _(e):
                run("pool", e)

            @block.sync
            def _(e):
                run("sp", e)
                for q in self.ENGS:
                    n = self.ndma[q]
                    for s in range(min(n, DMA_SLOTS)):
                        e.wait_ge(dsem[q][s], 16 * ((n - 1 - s) // DMA_SLOTS + 1))


DT_SIZE = {F32: 4, BF16: 2, U8: 1, I32: 4}


class KB:
    def __init__(self, nc, st, dram):
        self.nc = nc
        self.P = Prog(nc)
        self.d = dram
        self.arena = st.enter_context(nc.sbuf_tensor("arena", [128, ARENA_BYTES], U8))
        self.aoff = 0
        self.perm_off = 0
        self.psb = [T(st.enter_context(nc.psum_tensor(f"psb{i}", [128, 512], F32)), f"psb{i}") for i in range(8)]
        self.ps_i = 0
        self.dres = {}
        self.uid = 0

    def alloc(self, name, free_shape, dt, perm=False):
        n = int(np.prod(free_shape)) * DT_SIZE[dt]
        n_al = (n + 63) // 64 * 64
        off = self.aoff
        assert off + n_al <= ARENA_BYTES, f"arena overflow at {name}: {off}+{n_al}"
        self.aoff += n_al
        v = self.arena[:, off:off + n].bitcast(dt)
        if len(free_shape) == 2:
            v = v.rearrange("p (a b) -> p a b", a=free_shape[0])
        elif len(free_shape) == 3:
            v = v.rearrange("p (a b c) -> p a b c", a=free_shape[0], b=free_shape[1])
        self.uid += 1
        return T(v, f"{name}{self.uid}")

    def mark_perm(self):
        self.perm_off = self.aoff

    def reset(self):
        self.P.barrier()
        self.aoff = self.perm_off

    def ps(self, n=1):
        if n == 2 and self.ps_i % 2 == 1:
            self.ps_i += 1
        out = [self.psb[(self.ps_i + i) % 8] for i in range(n)]
        self.ps_i = (self.ps_i + n) % 8
        return out

    def dr(self, name, key=None):
        k = (name, key)
        x = self.dres.get(k)
        if x is None:
            x = Res(f"dram:{name}:{key}")
            self.dres[k] = x
        return x

    def dma(self, q, out, in_, reads=(), writes=()):
        return self.P.op(q, lambda e: e.dma_start(out=out, in_=in_), reads=reads, writes=writes, dma=True)

    def mm(self, out, lhsT, rhs, start, stop, reads, writes):
        return self.P.op("pe", lambda e: e.matmul(out, lhsT=lhsT, rhs=rhs, start=start, stop=stop),
                         reads=reads, writes=writes)

    def tr(self, out, in_, ident, reads, writes):
        return self.P.op("pe", lambda e: e.transpose(out=out, in_=in_, identity=ident), reads=reads, writes=writes)

    def act(self, out, in_, func, reads, writes, bias=None, scale=None, accum_out=None):
        kw = {}
        if bias is not None:
            kw["bias"] = bias
        if scale is not None:
            kw["scale"] = scale
        if accum_out is not None:
            kw["accum_out"] = accum_out
        return self.P.op("act", lambda e: e.activation(out=out, in_=in_, func=func, **kw), reads=reads, writes=writes)

    def v(self, eng, name, reads, writes, *a, **kw):
        return self.P.op(eng, lambda e: getattr(e, name)(*a, **kw), reads=reads, writes=writes)

    def setup_consts(self):
        self.ident_f = self.alloc("identf", [128], F32, perm=True)
        self.ident = self.alloc("ident", [128], BF16, perm=True)
        self.ones = self.alloc("ones", [128], BF16, perm=True)
        self.dma("sp", self.ident_f[:], self.d["c_ident"], writes=[self.ident_f.r()])
        self.v("dve", "tensor_copy", [self.ident_f.r()], [self.ident.r()], out=self.ident[:], in_=self.ident_f[:])
        self.v("dve", "memset", [], [self.ones.r()], self.ones[:], 1.0)
        self.stat_i = 0
        self.stats = self.alloc("stats", [64, 4], F32, perm=True)

    def stat(self):
        i = self.stat_i
        self.stat_i = (i + 1) % 64
        return self.stats[:, i, :], self.stats.r(i)

    def load_bcast(self, name, row_ap, n):
        t = self.alloc(name, [n], F32)
        self.dma("sp", t[:], row_ap.broadcast_to([128, n]), writes=[t.r()])
        return t

    def load_w(self, name, src, kch, ncols):
        t = self.alloc(name, [kch, ncols], BF16)
        step = max(1, (4 * 1024 * 1024) // (128 * ncols * 4))
        for k0 in range(0, kch, step):
            k1 = min(kch, k0 + step)
            self.dma("pool", t[:, k0:k1, :], src[k0 * 128:k1 * 128, :].rearrange("(c p) n -> p c n", p=128),
                     writes=[t.r(k) for k in range(k0, k1)])
        return t

    def rstd_from_ss(self, ss_ap, ss_res, n):
        sq, sq_r = self.stat()
        self.act(sq[:, 0:1], ss_ap, AF.Ln, [ss_res, self.epsb.r()], [sq_r], bias=self.epsb[:, 0:1], scale=1.0 / n)
        self.act(sq[:, 1:2], sq[:, 0:1], AF.Exp, [sq_r], [sq_r], scale=-0.5)
        return sq[:, 1:2], sq_r

    def norm_tile(self, xt, xt_res, gt, hT, hT_res, col0, junk, defer=False):
        st_, st_r = self.stat()
        self.act(junk[:], xt, AF.Square, [xt_res], [junk.r(), st_r], accum_out=st_[:, 0:1])
        rstd, rr = self.rstd_from_ss(st_[:, 0:1], st_r, D)
        hb = self.hb[self.hb_i % len(self.hb)]
        self.hb_i += 1
        self.v("dve", "scalar_tensor_tensor", [xt_res, rr, gt.r()], [hb.r()],
               out=hb[:], in0=xt, scalar=rstd, in1=gt[:], op0=ALU.mult, op1=ALU.mult)
        (pb,) = self.ps(1)
        pv = pb[:].bitcast(BF16).rearrange("p (a b) -> p a b", a=8)
        for k in range(8):
            self.tr(pv[:, k, :], hb[:, k * 128:(k + 1) * 128], self.ident[:], [hb.r(), self.ident.r()], [pb.r()])
        def part_b():
            self.P.op("act", lambda e: e.copy(out=hT[:, 0:8, col0:col0 + 128], in_=pv), reads=[pb.r()], writes=[hT_res])
        if defer:
            return part_b
        part_b()
        return None

    def post_tile(self, pbs, xt, xt_res, gpost, dst_ap, dst_res, junk):
        st_, st_r = self.stat()
        for i in range(2):
            self.act(junk[:, i * 512:(i + 1) * 512], pbs[i][:], AF.Square, [pbs[i].r()], [junk.r(), st_r],
                     accum_out=st_[:, i:i + 1])
        self.v("dve", "tensor_add", [st_r], [st_r], out=st_[:, 2:3], in0=st_[:, 0:1], in1=st_[:, 1:2])
        rstd, rr = self.rstd_from_ss(st_[:, 2:3], st_r, D)
        tmp = self.ptmp[self.ptmp_i % 2]
        self.ptmp_i += 1
        for i in range(2):
            self.v("dve", "tensor_tensor", [pbs[i].r(), gpost.r()], [tmp.r()], out=tmp[:, i * 512:(i + 1) * 512], in0=pbs[i][:],
                   in1=gpost[:, i * 512:(i + 1) * 512], op=ALU.mult)
        self.v("dve", "scalar_tensor_tensor", [tmp.r(), rr, xt_res], [tmp.r()], out=tmp[:], in0=tmp[:], scalar=rstd, in1=xt,
               op0=ALU.mult, op1=ALU.add)
        self.dma("pool", dst_ap, tmp[:], reads=[tmp.r()], writes=[dst_res])

    def common_bufs(self):
        self.epsb = self.alloc("epsb", [1], F32)
        self.v("dve", "memset", [], [self.epsb.r()], self.epsb[:], EPS)
        self.hb = [self.alloc("hb", [D], BF16) for _ in range(2)]
        self.hb_i = 0
        self.ptmp = [self.alloc("ptmp", [D], F32) for _ in range(2)]
        self.ptmp_i = 0
        self.junk = self.alloc("junk", [D], BF16)

    def ffn(self, l, src, dst):
        d = self.d
        G = 256
        ng = S // G
        tpg = G // 128
        self.reset()
        self.common_bufs()
        gpre = self.load_bcast("gpre", d["g_ff_pre"][l:l + 1, :], D)
        gpost = self.load_bcast("gpost", d["g_ff_post"][l:l + 1, :], D)
        w1 = self.load_w("w1", d["w_ff1"][l], 8, 4096)
        w2 = self.load_w("w2", d["w_ff2"][l], 32, D)
        xg = [self.alloc("xg", [tpg, D], F32) for _ in range(2)]
        hTg = [self.alloc("hTg", [8, G], BF16) for _ in range(2)]
        hid = self.alloc("hid", [32, G], BF16)
        rl = self.alloc("rl", [2, G], F32)
        def norm_group(g):
            xb = xg[g % 2]
            hT = hTg[g % 2]
            pend = None
            for t in range(tpg):
                tt = g * tpg + t
                self.dma("sp", xb[:, t, :], d[src][tt * 128:(tt + 1) * 128, :], reads=[self.dr(src, tt)], writes=[xb.r(t)])
                nb_ = self.norm_tile(xb[:, t, :], xb.r(t), gpre, hT, hT.r(), t * 128, self.junk, defer=True)
                if pend is not None:
                    pend()
                pend = nb_
            pend()

        norm_group(0)
        for g in range(ng):
            xb = xg[g % 2]
            hT = hTg[g % 2]
            for j in range(32):
                (pb,) = self.ps(1)
                for k in range(8):
                    self.mm(pb[:, 0:G], w1[:, k, j * 128:(j + 1) * 128], hT[:, k, :], k == 0, k == 7,
                            [w1.r(k), hT.r()], [pb.r()])
                self.act(rl[:, j % 2, :], pb[:, 0:G], AF.Relu, [pb.r()], [rl.r(j % 2)])
                eng = "pool" if j % 2 == 0 else "dve"
                self.v(eng, "tensor_tensor", [rl.r(j % 2)], [hid.r(j)], out=hid[:, j, :], in0=rl[:, j % 2, :],
                       in1=rl[:, j % 2, :], op=ALU.mult)
            if g + 1 < ng:
                norm_group(g + 1)
            for t in range(tpg):
                tt = g * tpg + t
                pbs = self.ps(2)
                for half in range(2):
                    for j in range(32):
                        self.mm(pbs[half][:], hid[:, j, t * 128:(t + 1) * 128], w2[:, j, half * 512:(half + 1) * 512],
                                j == 0, j == 31, [hid.r(j), w2.r(j)], [pbs[half].r()])
                self.post_tile(pbs, xb[:, t, :], xb.r(t), gpost, d[dst][tt * 128:(tt + 1) * 128, :], self.dr(dst, tt), self.junk)

    def cross(self, l, src, dst):
        d = self.d
        G = 512
        ng = S // G
        tpg = G // 128
        self.reset()
        self.common_bufs()
        gpre = self.load_bcast("gpre", d["g_x_pre"][l:l + 1, :], D)
        gpost = self.load_bcast("gpost", d["g_x_post"][l:l + 1, :], D)
        gmem = self.load_bcast("gmem", d["g_mem"][l:l + 1, :], D)
        wq = self.load_w("wq", d["w_cq"][l], 8, 512)
        wkv = self.load_w("wkv", d["w_ckv"][l], 8, 1024)
        wo = self.load_w("wo", d["w_co"][l], 4, D)
        memT = self.alloc("memT", [8, NMEM], BF16)
        kT = self.alloc("kT", [4, NMEM], BF16)
        vv = self.alloc("vv", [2, 512], BF16)
        mt_ = self.alloc("memt", [D], F32)
        for mt in range(2):
            self.dma("sp", mt_[:], d["mem"][mt * 128:(mt + 1) * 128, :], writes=[mt_.r()])
            self.norm_tile(mt_[:], mt_.r(), gmem, memT, memT.r(), mt * 128, self.junk)
        for hd in range(4):
            (pb,) = self.ps(1)
            for k in range(8):
                self.mm(pb[:, 0:NMEM], wkv[:, k, hd * 128:(hd + 1) * 128], memT[:, k, :], k == 0, k == 7,
                        [wkv.r(k), memT.r()], [pb.r()])
            self.P.op("act", lambda e, pb=pb, hd=hd: e.copy(out=kT[:, hd, :], in_=pb[:, 0:NMEM]), reads=[pb.r()], writes=[kT.r()])
        for mt in range(2):
            (pb,) = self.ps(1)
            for k in range(8):
                self.mm(pb[:], memT[:, k, mt * 128:(mt + 1) * 128], wkv[:, k, 512:1024], k == 0, k == 7,
                        [wkv.r(k), memT.r()], [pb.r()])
            self.P.op("act", lambda e, pb=pb, mt=mt: e.copy(out=vv[:, mt, :], in_=pb[:]), reads=[pb.r()], writes=[vv.r()])
        xg = [self.alloc("xg", [tpg, D], F32) for _ in range(2)]
        hTg = [self.alloc("hTg", [8, G], BF16) for _ in range(2)]
        qT = [self.alloc("qT", [G], BF16) for _ in range(3)]
        PT = [self.alloc("PT", [2, G], BF16) for _ in range(3)]
        rden = [self.alloc("rden", [G], F32) for _ in range(2)]
        oTs = [self.alloc("oT", [4, G], BF16) for _ in range(2)]
        scale = 128.0 ** -0.5
        def norm_group(g):
            xb = xg[g % 2]
            hT = hTg[g % 2]
            pend = None
            for t in range(tpg):
                tt = g * tpg + t
                self.dma("sp", xb[:, t, :], d[src][tt * 128:(tt + 1) * 128, :], reads=[self.dr(src, tt)], writes=[xb.r(t)])
                nb_ = self.norm_tile(xb[:, t, :], xb.r(t), gpre, hT, hT.r(), t * 128, self.junk, defer=True)
                if pend is not None:
                    pend()
                pend = nb_
            pend()

        def stA(g, hd):
            hT = hTg[g % 2]
            q = qT[hd % 3]
            (pb,) = self.ps(1)
            for k in range(8):
                self.mm(pb[:], wq[:, k, hd * 128:(hd + 1) * 128], hT[:, k, :], k == 0, k == 7, [wq.r(k), hT.r()], [pb.r()])
            self.P.op("act", lambda e, pb=pb, q=q: e.copy(out=q[:], in_=pb[:]), reads=[pb.r()], writes=[q.r()])

        def stB(g, hd):
            q = qT[hd % 3]
            pt = PT[hd % 3]
            for mt in range(2):
                (pb,) = self.ps(1)
                self.mm(pb[:], kT[:, hd, mt * 128:(mt + 1) * 128], q[:], True, True, [kT.r(), q.r()], [pb.r()])
                self.act(pt[:, mt, :], pb[:], AF.Exp, [pb.r()], [pt.r()], scale=scale)

        def stC(g, hd):
            oT = oTs[g % 2]
            pt = PT[hd % 3]
            rd = rden[hd % 2]
            (pn,) = self.ps(1)
            (pd,) = self.ps(1)
            for mt in range(2):
                self.mm(pn[:], vv[:, mt, hd * 128:(hd + 1) * 128], pt[:, mt, :], mt == 0, mt == 1, [vv.r(), pt.r()], [pn.r()])
            for mt in range(2):
                self.mm(pd[:], self.ones[:], pt[:, mt, :], mt == 0, mt == 1, [self.ones.r(), pt.r()], [pd.r()])
            self.act(rd[:], pd[:], AF.Ln, [pd.r()], [rd.r()])
            self.act(rd[:], rd[:], AF.Exp, [rd.r()], [rd.r()], scale=-1.0)
            self.v("dve", "tensor_tensor", [pn.r(), rd.r()], [oT.r()], out=oT[:, hd, :], in0=pn[:], in1=rd[:], op=ALU.mult)

        norm_group(0)
        stA(0, 0)
        stA(0, 1)
        for g in range(ng):
            xb = xg[g % 2]
            oT = oTs[g % 2]
            stB(g, 0); stA(g, 2); stB(g, 1); stC(g, 0); stA(g, 3); stB(g, 2); stC(g, 1); stB(g, 3); stC(g, 2); stC(g, 3)
            if g + 1 < ng:
                norm_group(g + 1)
                stA(g + 1, 0)
                stA(g + 1, 1)
            for t in range(tpg):
                tt = g * tpg + t
                pbs = self.ps(2)
                for half in range(2):
                    for hd in range(4):
                        self.mm(pbs[half][:], oT[:, hd, t * 128:(t + 1) * 128], wo[:, hd, half * 512:(half + 1) * 512],
                                hd == 0, hd == 3, [oT.r(), wo.r(hd)], [pbs[half].r()])
                self.post_tile(pbs, xb[:, t, :], xb.r(t), gpost, d[dst][tt * 128:(tt + 1) * 128, :], self.dr(dst, tt), self.junk)

    def reset_to(self, off):
        self.P.barrier()
        self.aoff = off

    def mixer(self, l, src, dst, parts=("pool", "att", "s5", "sgu", "merge")):
        d = self.d
        self.reset()
        self.common_bufs()
        hT = self.alloc("hT", [8, S], BF16)
        base = self.aoff
        gpre = self.load_bcast("gpre", d["g_mix_pre"][l:l + 1, :], D)
        xt = [self.alloc("xt", [D], F32) for _ in range(4)]
        hb_save = self.hb
        self.hb = hb_save + [self.alloc("hbx", [D], BF16) for _ in range(2)]
        junks = [self.junk] + [self.alloc("junkx", [D], BF16) for _ in range(1)]
        pend0 = None
        for tt in range(NT):
            x_ = xt[tt % 4]
            self.dma("sp", x_[:], d[src][tt * 128:(tt + 1) * 128, :], reads=[self.dr(src, tt)], writes=[x_.r()])
            nb_ = self.norm_tile(x_[:], x_.r(), gpre, hT, hT.r(), tt * 128, junks[tt % 2], defer=True)
            if pend0 is not None:
                pend0()
            pend0 = nb_
        pend0()
        self.hb = hb_save
        if "pool" in parts or "sgu" in parts:
            self.reset_to(base)
            self.mix_pool_sgu(l, hT)
        if "att" in parts:
            self.reset_to(base)
            self.mix_att(l, hT)
        if "s5" in parts:
            self.reset_to(base)
            self.mix_s5(l, hT)
            self.reset_to(base)
            self.mix_s5_glu(l)
        if "merge" in parts:
            for h2 in range(2):
                self.reset_to(base)
                self.mix_merge_a(l, hT, h2)
            self.reset_to(base)
            self.mix_merge_b(l, src, dst)

    def mix_pool_sgu(self, l, hT):
        d = self.d
        wp = self.load_w("wp", d["w_in"][l][:, OFF_POOL:OFF_POOL + 512], 8, 512)
        pw = self.alloc("pw", [4, 128], BF16)
        self.dma("pool", pw[:], d["pool_w"][l].rearrange("g c e -> c g e"), writes=[pw.r()])
        psc = self.alloc("psc", [4], F32)
        self.dma("sp", psc[:], d["pool_scale_c"][l], writes=[psc.r()])
        invc = self.alloc("invc", [64], F32)
        self.dma("sp", invc[:], d["c_invc"].broadcast_to([128, 64]), writes=[invc.r()])
        hp = self.alloc("hp", [16 + S], F32)
        A = self.alloc("pA", [16 + S], F32)
        B = self.alloc("pB", [16 + S], F32)
        for b_ in (hp, A, B):
            self.v("pool", "memset", [], [b_.r()], b_[:, 0:16], 0.0)
        pbf = self.alloc("pbf", [S], BF16)
        ao = self.alloc("ao", [S], BF16)
        t16 = self.alloc("t16", [16], F32)
        pstate = {}

        def pool_a(gi):
            w = 2 << gi
            for t8 in range(8):
                (pb,) = self.ps(1)
                for k in range(8):
                    self.mm(pb[:], wp[:, k, gi * 128:(gi + 1) * 128], hT[:, k, t8 * 512:(t8 + 1) * 512], k == 0, k == 7,
                            [wp.r(k)], [pb.r()])
                self.P.op("act", lambda e, pb=pb, t8=t8: e.copy(out=hp[:, 16 + t8 * 512:16 + (t8 + 1) * 512], in_=pb[:]),
                          reads=[pb.r()], writes=[hp.r()])
            cur, sh, i = hp, 1, 0
            while sh < w:
                nxt = (A, B)[i % 2]
                self.v("pool", "tensor_add", [cur.r()], [nxt.r()], out=nxt[:, 16:16 + S], in0=cur[:, 16:16 + S],
                       in1=cur[:, 16 - sh:16 - sh + S])
                cur, sh, i = nxt, sh * 2, i + 1
            pstate[gi] = cur

        def pool_b(gi):
            w = 2 << gi
            cur = pstate[gi]
            self.v("dve", "scalar_tensor_tensor", [cur.r(), hp.r()], [pbf.r()], out=pbf[:], in0=cur[:, 16:16 + S],
                   scalar=1.0 / w, in1=hp[:, 16:16 + S], op0=ALU.mult, op1=ALU.subtract)
            self.v("dve", "tensor_tensor", [cur.r(), invc.r()], [t16.r()], out=t16[:], in0=cur[:, 16:32], in1=invc[:, gi * 16:(gi + 1) * 16], op=ALU.mult)
            self.v("dve", "tensor_tensor", [t16.r(), hp.r(), pbf.r()], [pbf.r()], out=pbf[:, 0:16], in0=t16[:], in1=hp[:, 16:32],
                   op=ALU.subtract)
            for t8 in range(8):
                (pb,) = self.ps(1)
                self.mm(pb[:], pw[:, gi, :], pbf[:, t8 * 512:(t8 + 1) * 512], True, True, [pw.r(), pbf.r()], [pb.r()])
                self.v("dve", "tensor_scalar", [pb.r(), psc.r()], [ao.r()], out=ao[:, t8 * 512:(t8 + 1) * 512], in0=pb[:],
                       scalar1=psc[:, gi:gi + 1], scalar2=None, op0=ALU.mult)
            self.dma("sp", d["bout"][0, gi * 128:(gi + 1) * 128, :], ao[:], reads=[ao.r()], writes=[self.dr("bout", (0, gi))])

        wz = self.load_w("wz", d["w_in"][l][:, OFF_SGU:OFF_SGU + 1024], 8, 1024)
        wraw = self.alloc("wraw", [4, 128], F32)
        self.dma("sp", wraw[:], d["wsT"][l].rearrange("g s t -> s g t"), writes=[wraw.r()])
        tril = self.alloc("tril", [128], F32)
        self.dma("sp", tril[:], d["c_tril"], writes=[tril.r()])
        wsb = self.alloc("wsb", [4, 128], BF16)
        for g in range(4):
            self.v("dve", "tensor_tensor", [wraw.r(), tril.r()], [wsb.r()], out=wsb[:, g, :], in0=wraw[:, g, :], in1=tril[:], op=ALU.mult)
        lngc = self.alloc("lngc", [4], F32)
        lnbc = self.alloc("lnbc", [4], F32)
        self.dma("sp", lngc[:], d["sgu_ln_g_c"][l], writes=[lngc.r()])
        self.dma("sp", lnbc[:], d["sgu_ln_b_c"][l], writes=[lnbc.r()])
        bsb = self.load_bcast("bsb", d["b_s_r"][l:l + 1, :], 512)
        bias2 = self.alloc("bias2", [4, 128], F32)
        (prs,) = self.ps(1)
        for g in range(4):
            self.mm(prs[:, g * 128:(g + 1) * 128], self.ones[:], wsb[:, g, :], True, True, [self.ones.r(), wsb.r()], [prs.r()])
        for g in range(4):
            self.v("dve", "scalar_tensor_tensor", [prs.r(), lnbc.r(), bsb.r()], [bias2.r()], out=bias2[:, g, :], in0=prs[:, g * 128:(g + 1) * 128],
                   scalar=lnbc[:, g:g + 1], in1=bsb[:, g * 128:(g + 1) * 128], op0=ALU.mult, op1=ALU.add)
        uT = [self.alloc("uT", [4, 512], BF16) for _ in range(2)]
        dT = [self.alloc("dT", [4, 512], BF16) for _ in range(2)]
        tm = [self.alloc("tm", [4, 128], F32) for _ in range(2)]
        NR = 3
        vg = [self.alloc("vg", [512], F32) for _ in range(NR)]
        vf = [self.alloc("vf", [512], BF16) for _ in range(NR)]
        st6 = [self.alloc("st6", [12], F32) for _ in range(NR)]

        def sgu_u(t8):
            u_ = uT[t8 % 2]
            for c in range(4):
                (pb,) = self.ps(1)
                for k in range(8):
                    self.mm(pb[:], wz[:, k, c * 128:(c + 1) * 128], hT[:, k, t8 * 512:(t8 + 1) * 512], k == 0, k == 7, [wz.r(k)], [pb.r()])
                self.act(u_[:, c, :], pb[:], AF.Gelu, [pb.r()], [u_.r()])

        def sgu_s1(t):
            tok0 = t * 128
            vg_, vf_, s6 = vg[t % NR], vf[t % NR], st6[t % NR]
            (pb,) = self.ps(1)
            for k in range(8):
                self.mm(pb[:], hT[:, k, tok0:tok0 + 128], wz[:, k, 512:1024], k == 0, k == 7, [wz.r(k)], [pb.r()])
            self.act(vg_[:], pb[:], AF.Gelu, [pb.r()], [vg_.r()])
            self.v("dve", "bn_stats", [vg_.r()], [s6.r()], out=s6[:, 0:6], in_=vg_[:])
            self.v("dve", "bn_aggr", [s6.r()], [s6.r()], out=s6[:, 6:8], in_=s6[:, 0:6])
            rstd, rr = self.rstd_from_ss(s6[:, 7:8], s6.r(), 1)
            self.v("dve", "scalar_tensor_tensor", [s6.r(), rr], [s6.r()], out=s6[:, 8:9], in0=s6[:, 6:7], scalar=-1.0, in1=rstd,
                   op0=ALU.mult, op1=ALU.mult)
            self.act(vf_[:], vg_[:], AF.Identity, [vg_.r(), s6.r(), rr], [vf_.r()], bias=s6[:, 8:9], scale=rstd)

        def sgu_s2(t):
            t8, t4 = t // 4, t % 4
            u_ = uT[t8 % 2]
            d_ = dT[t8 % 2]
            vf_ = vf[t % NR]
            tm_ = tm[t % 2]
            (pb2,) = self.ps(1)
            for c in range(4):
                self.mm(pb2[:, c * 128:(c + 1) * 128], vf_[:, c * 128:(c + 1) * 128], wsb[:, c, :], True, True,
                        [vf_.r(), wsb.r()], [pb2.r()])
            for c in range(4):
                self.v("dve", "scalar_tensor_tensor", [pb2.r(), lngc.r(), bias2.r()], [tm_.r()], out=tm_[:, c, :], in0=pb2[:, c * 128:(c + 1) * 128],
                       scalar=lngc[:, c:c + 1], in1=bias2[:, c, :], op0=ALU.mult, op1=ALU.add)
            self.v("dve", "tensor_tensor", [tm_.r(), u_.r()], [d_.r()], out=d_[:, :, t4 * 128:(t4 + 1) * 128],
                   in0=tm_[:], in1=u_[:, :, t4 * 128:(t4 + 1) * 128], op=ALU.mult)
            if t4 == 3:
                self.dma("sp", d["bout"][3, :, t8 * 512:(t8 + 1) * 512].rearrange("(c p) t -> p c t", p=128), d_[:],
                         reads=[d_.r()], writes=[self.dr("bout", (3, t8))])

        LA = 2
        emitted_u = set()

        def need_u(t):
            t8 = t // 4
            if t8 not in emitted_u:
                emitted_u.add(t8)
                sgu_u(t8)

        def sgu_step(t8):
            for t in range(t8 * 4, t8 * 4 + 4):
                if t + LA < 32:
                    need_u(t + LA)
                    sgu_s1(t + LA)
                sgu_s2(t)

        need_u(0)
        for t in range(LA):
            sgu_s1(t)

        pool_a(0)
        for step in range(8):
            sgu_step(step)
            gi = step // 2
            if step % 2 == 0:
                pool_b(gi)
            elif gi + 1 < 4:
                pool_a(gi + 1)

    def mix_att(self, l, hT):
        d = self.d
        amask = self.alloc("amask", [2, 256], BF16)
        self.dma("pool", amask[:], d["c_att_mask"], writes=[amask.r()])
        onesAB = self.alloc("onesAB", [2, 128], BF16)
        self.v("dve", "memset", [], [onesAB.r()], onesAB[:], 0.0)
        self.v("dve", "memset", [], [onesAB.r()], onesAB[:, 0, 0:64], 1.0)
        self.v("dve", "memset", [], [onesAB.r()], onesAB[:, 1, 64:128], 1.0)
        wq = self.alloc("wq", [8, 128], BF16)
        wk = self.alloc("wk", [8, 128], BF16)
        wv = self.alloc("wv", [8, 128], BF16)
        qT = self.alloc("qT", [S], BF16)
        kTA = self.alloc("kTA", [S], BF16)
        kTB = self.alloc("kTB", [S], BF16)
        vT = self.alloc("vT", [S], BF16)
        self.v("dve", "memset", [], [kTA.r()], kTA[64:128, :], 0.0)
        self.v("dve", "memset", [], [kTB.r()], kTB[0:64, :], 0.0)
        vpad = self.alloc("vpad", [32, 2, 128], BF16)
        self.v("pool", "memset", [], [vpad.r()], vpad[:], 0.0)
        acc = self.alloc("acc", [2, S], F32)
        braw = self.alloc("braw", [2, 256], F32)
        ebias = self.alloc("ebias", [2, 256], BF16)
        E = [self.alloc("E", [2, 256], BF16) for _ in range(2)]
        PT = [self.alloc("PT", [2, 256], BF16) for _ in range(4)]
        bo = self.alloc("bo", [S], BF16)
        wsets = [(wq, wk, wv), (self.alloc("wq2", [8, 128], BF16), self.alloc("wk2", [8, 128], BF16), self.alloc("wv2", [8, 128], BF16))]
        braw2 = [braw, self.alloc("braw2", [2, 256], F32)]
        ebias2 = [ebias, self.alloc("ebias2", [2, 256], BF16)]
        E = E + [self.alloc("E3", [2, 256], BF16)]
        units = [(c, g) for c in range(4) for g in range(3)]

        def load_unit(u):
            c, g = units[u]
            wts = wsets[u % 2]
            for wi in range(3):
                col = OFF_ATT + wi * 1536 + g * 512 + c * 128
                self.dma("pool", wts[wi][:, :, 0:128], d["w_in"][l][:, col:col + 128].rearrange("(k p) n -> p k n", p=128),
                         writes=[wts[wi].r()])
            br_, eb_ = braw2[u % 2], ebias2[u % 2]
            self.dma("sp", br_[:], d["att_bias"][g, c], writes=[br_.r()])
            self.act(br_[:], br_[:], AF.Exp, [br_.r()], [br_.r()])
            self.v("dve", "tensor_tensor", [br_.r(), amask.r()], [eb_.r()], out=eb_[:], in0=br_[:], in1=amask[:], op=ALU.mult)

        load_unit(0)
        ipt = 0
        for u, (c, g) in enumerate(units):
            dil = DILS[g]
            L = S // dil
            nb = L // 128
            wts = wsets[u % 2]
            ebias_u = ebias2[u % 2]
            if u + 1 < len(units):
                load_unit(u + 1)
            for wi in range(3):
                for t8 in range(8):
                    (pb,) = self.ps(1)
                    for k in range(8):
                        self.mm(pb[:], wts[wi][:, k, 0:128], hT[:, k, t8 * 512:(t8 + 1) * 512], k == 0, k == 7, [wts[wi].r()], [pb.r()])
                    mw = 512 // dil
                    m0 = t8 * mw

                    def views(dst_, p0, p1):
                        ov = dst_[p0:p1, :].rearrange("p (r m) -> p r m", r=dil)[:, :, m0:m0 + mw]
                        iv = pb[p0:p1, :].rearrange("p (m r) -> p r m", r=dil)
                        return ov, iv
                    if wi == 0:
                        ov, iv = views(qT, 0, 128)
                        self.P.op("act", lambda e, ov=ov, iv=iv: e.mul(out=ov, in_=iv, mul=0.125), reads=[pb.r()], writes=[qT.r()])
                    elif wi == 1:
                        ov, iv = views(kTA, 0, 64)
                        self.v("dve", "tensor_copy", [pb.r()], [kTA.r()], out=ov, in_=iv)
                        ov, iv = views(kTB, 64, 128)
                        self.P.op("act", lambda e, ov=ov, iv=iv: e.copy(out=ov, in_=iv), reads=[pb.r()], writes=[kTB.r()])
                    else:
                        ov, iv = views(vT, 0, 128)
                        self.v("dve", "tensor_copy", [pb.r()], [vT.r()], out=ov, in_=iv)
            for b0 in range(0, 32, 8):
                (pb,) = self.ps(1)
                pv = pb[:].bitcast(BF16).rearrange("p (a b) -> p a b", a=8)
                for bi in range(8):
                    self.tr(pv[:, bi, :], vT[:, (b0 + bi) * 128:(b0 + bi + 1) * 128], self.ident[:], [vT.r(), self.ident.r()], [pb.r()])
                self.P.op("act", lambda e, pv=pv, b0=b0: e.copy(out=vpad[:, b0:b0 + 8, 0, 0:64], in_=pv[:, :, 0:64]), reads=[pb.r()], writes=[vpad.r()])
                self.v("dve", "tensor_copy", [pb.r()], [vpad.r()], out=vpad[:, b0:b0 + 8, 1, 64:128], in_=pv[:, :, 64:128])
            blocks = [(r, n) for r in range(dil) for n in range(nb)]
            ptof = {}

            def stage1(bi):
                nonlocal ipt
                r, n = blocks[bi]
                q0 = r * L + 128 * n
                nq = 256 if n + 1 < nb else 128
                (pb,) = self.ps(1)
                for hd in range(2):
                    kTh = (kTA, kTB)[hd]
                    self.mm(pb[:, hd * 256:hd * 256 + nq], kTh[:, q0:q0 + 128], qT[:, q0:q0 + nq], True, True, [kTh.r(), qT.r()], [pb.r()])
                e_ = E[ipt % 3]
                pt = PT[bi % 4]
                ipt += 1
                pv3 = pb[:].rearrange("p (h j) -> p h j", h=2)
                self.act(e_[:, :, 0:nq], pv3[:, :, 0:nq], AF.Exp, [pb.r()], [e_.r()])
                eng = "pool" if ipt % 2 == 0 else "dve"
                self.v(eng, "tensor_tensor", [e_.r(), ebias_u.r()], [pt.r()], out=pt[:, :, 0:nq], in0=e_[:, :, 0:nq],
                       in1=ebias_u[:, :, 0:nq], op=ALU.mult)
                ptof[bi] = pt

            def stage2(bi):
                r, n = blocks[bi]
                blk = r * nb + n
                pt = ptof[bi]
                (po,) = self.ps(1)
                srcs = [(blk, pt, 0)]
                if n > 0:
                    srcs.append((blk - 1, ptof[bi - 1], 128))
                nmm = 2 * len(srcs)
                for which in range(2):
                    i = 0
                    for (bk, ptile, j0) in srcs:
                        for hd in range(2):
                            lhs = vpad[:, bk, hd, :] if which == 0 else onesAB[:, hd, :]
                            self.mm(po[:, which * 128:(which + 1) * 128], lhs, ptile[:, hd, j0:j0 + 128], i == 0, i == nmm - 1,
                                    [vpad.r(), onesAB.r(), ptile.r()], [po.r()])
                            i += 1
                av = acc[:].rearrange("p a (m r) -> p a m r", r=dil)[:, :, 128 * n:128 * (n + 1), r]
                pov = po[:, 0:256].rearrange("p (a q) -> p a q", a=2)
                if g == 0:
                    self.P.op("act", lambda e, av=av, pov=pov: e.copy(out=av, in_=pov), reads=[po.r()], writes=[acc.r()])
                else:
                    self.v("dve", "tensor_tensor", [po.r(), acc.r()], [acc.r()], out=av, in0=pov, in1=av, op=ALU.add)

            LA = 2
            for bi in range(min(LA, len(blocks))):
                stage1(bi)
            for bi in range(len(blocks)):
                if bi + LA < len(blocks):
                    stage1(bi + LA)
                stage2(bi)
            if g == 2:
                self.act(acc[:, 1, :], acc[:, 1, :], AF.Ln, [acc.r()], [acc.r()])
                self.act(acc[:, 1, :], acc[:, 1, :], AF.Exp, [acc.r()], [acc.r()], scale=-1.0)
                self.v("pool", "tensor_tensor", [acc.r()], [bo.r()], out=bo[:], in0=acc[:, 0, :], in1=acc[:, 1, :], op=ALU.mult)
                self.dma("sp", d["bout"][1, c * 128:(c + 1) * 128, :], bo[:], reads=[bo.r()], writes=[self.dr("bout", (1, c))])

    def mix_s5(self, l, hT):
        d = self.d
        PI_ = float(np.pi)
        tabr = Res("s5tab")

        def T32(name, n=1):
            return self.alloc(name, [n, 32], F32) if n > 1 else self.alloc(name, [32], F32)

        def tt(out, a, b, op, eng="dve"):
            self.v(eng, "tensor_tensor", [tabr], [tabr], out=out, in0=a, in1=b, op=op)

        def ts(out, a, s1, op0, s2=None, op1=None, eng="dve"):
            kw = dict(out=out, in0=a, scalar1=s1, scalar2=s2, op0=op0)
            if op1 is not None:
                kw["op1"] = op1
            self.v(eng, "tensor_scalar", [tabr], [tabr], **kw)

        def actf(out, in_, func):
            self.act(out, in_, func, [tabr], [tabr])

        ldt, are, aim = T32("ldt"), T32("are"), T32("aim")
        self.dma("sp", ldt[:], d["s5_ldt"][l], writes=[tabr])
        self.dma("sp", are[:], d["s5_are"][l], writes=[tabr])
        self.dma("sp", aim[:], d["s5_aim"][l], writes=[tabr])
        sgn = self.alloc("sgn", [4], F32)
        self.dma("sp", sgn[:], d["c_sgn"], writes=[tabr])
        B1 = self.alloc("B1", [32, 16], F32)
        B2 = self.alloc("B2", [32, 16], F32)
        C1 = self.alloc("C1", [32, 16], F32)
        C2 = self.alloc("C2", [32, 16], F32)
        for t_, n_ in ((B1, "s5_b1"), (B2, "s5_b2"), (C1, "s5_c1"), (C2, "s5_c2")):
            self.dma("sp", t_[:], d[n_][l], writes=[tabr])
        bdm = self.alloc("bdm", [128], F32)
        rowm = self.alloc("rowm", [8], F32)
        dcol = self.alloc("dcol", [4], F32)
        for t_, n_ in ((bdm, "c_bdmask"), (rowm, "c_rowmask")):
            self.dma("sp", t_[:], d[n_], writes=[tabr])
        self.dma("sp", dcol[:], d["d_skip_c"][l], writes=[tabr])
        dt, lre, x1, mag, ang = T32("dt"), T32("lre"), T32("x1"), T32("mag"), T32("ang")
        actf(dt[:], ldt[:], AF.Exp)
        ts(lre[:], are[:], -1e-4, ALU.min)
        tt(x1[:], lre[:], dt[:], ALU.mult)
        actf(mag[:], x1[:], AF.Exp)
        tt(ang[:], aim[:], dt[:], ALU.mult)
        z, y_, yf, r_, m_ = T32("z"), T32("y"), T32("yf"), T32("r"), T32("m")
        yi = self.alloc("yi", [32], I32)

        def sin_of(dst, shift):
            ts(z[:], ang[:], shift, ALU.add)
            ts(y_[:], z[:], 1.0 / (2 * PI_), ALU.mult)
            self.v("dve", "tensor_copy", [tabr], [tabr], out=yi[:], in_=y_[:])
            self.v("dve", "tensor_copy", [tabr], [tabr], out=yf[:], in_=yi[:])
            self.v("dve", "scalar_tensor_tensor", [tabr], [tabr], out=r_[:], in0=yf[:], scalar=-2 * PI_, in1=z[:], op0=ALU.mult, op1=ALU.add)
            ts(m_[:], r_[:], PI_, ALU.is_gt, -2 * PI_, ALU.mult)
            tt(r_[:], r_[:], m_[:], ALU.add)
            ts(m_[:], r_[:], -PI_, ALU.is_lt, 2 * PI_, ALU.mult)
            tt(r_[:], r_[:], m_[:], ALU.add)
            ts(r_[:], r_[:], -3.14159, ALU.max, 3.14159, ALU.min)
            actf(dst, r_[:], AF.Sin)

        cosv, sinv, ar, ai = T32("cosv"), T32("sinv"), T32("ar"), T32("ai")
        sin_of(cosv[:], PI_ / 2)
        sin_of(sinv[:], 0.0)
        tt(ar[:], mag[:], cosv[:], ALU.mult)
        tt(ai[:], mag[:], sinv[:], ALU.mult)
        PR, PIm = T32("PR", 9), T32("PI", 9)
        t1, t2 = T32("t1"), T32("t2")
        self.v("dve", "memset", [tabr], [tabr], PR[:, 0, :], 1.0)
        self.v("dve", "memset", [tabr], [tabr], PIm[:, 0, :], 0.0)
        for j in range(8):
            tt(t1[:], PR[:, j, :], ar[:], ALU.mult)
            tt(t2[:], PIm[:, j, :], ai[:], ALU.mult)
            tt(PR[:, j + 1, :], t1[:], t2[:], ALU.subtract)
            tt(t1[:], PR[:, j, :], ai[:], ALU.mult)
            tt(t2[:], PIm[:, j, :], ar[:], ALU.mult)
            tt(PIm[:, j + 1, :], t1[:], t2[:], ALU.add)
        QR, QI, QIs = T32("QR", 9), T32("QI", 9), T32("QIs", 9)
        self.v("dve", "tensor_copy", [tabr], [tabr], out=QR[:, 0, :], in_=PR[:, 8, :])
        self.v("dve", "tensor_copy", [tabr], [tabr], out=QI[:, 0, :], in_=PIm[:, 8, :])
        for i in range(8):
            tt(t1[:], QR[:, i, :], QR[:, i, :], ALU.mult)
            tt(t2[:], QI[:, i, :], QI[:, i, :], ALU.mult)
            tt(QR[:, i + 1, :], t1[:], t2[:], ALU.subtract)
            tt(t1[:], QR[:, i, :], QI[:, i, :], ALU.mult)
            ts(QI[:, i + 1, :], t1[:], 2.0, ALU.mult)
        ts(QIs[:], QI[:], sgn[:, 1:2], ALU.mult)
        PIs1, TA, TB = T32("PIs1", 9), T32("TA", 9), T32("TB", 9)
        ts(PIs1[:], PIm[:], sgn[:, 0:1], ALU.mult)
        ts(TA[:], PR[:], sgn[:, 1:2], ALU.mult)
        ts(TB[:], PIm[:], -1.0, ALU.mult)
        den, am1, fr, fi, FIs, FIs2 = T32("den"), T32("am1"), T32("fr"), T32("fi"), T32("FIs"), T32("FIs2")
        tt(t1[:], lre[:], lre[:], ALU.mult)
        tt(t2[:], aim[:], aim[:], ALU.mult)
        tt(den[:], t1[:], t2[:], ALU.add)
        self.v("dve", "reciprocal", [tabr], [tabr], out=den[:], in_=den[:])
        ts(am1[:], ar[:], -1.0, ALU.add)
        tt(t1[:], am1[:], lre[:], ALU.mult)
        tt(t2[:], ai[:], aim[:], ALU.mult)
        tt(t1[:], t1[:], t2[:], ALU.add)
        tt(fr[:], t1[:], den[:], ALU.mult)
        tt(t1[:], ai[:], lre[:], ALU.mult)
        tt(t2[:], am1[:], aim[:], ALU.mult)
        tt(t1[:], t1[:], t2[:], ALU.subtract)
        tt(fi[:], t1[:], den[:], ALU.mult)
        ts(FIs[:], fi[:], sgn[:, 0:1], ALU.mult)
        ts(FIs2[:], fi[:], sgn[:, 1:2], ALU.mult)

        def bc(tab_ap_q):
            return tab_ap_q.unsqueeze(2).to_broadcast([128, 8, 16])

        bs = self.alloc("bs", [8, 16], F32)
        bw = self.alloc("bw", [8, 16], F32)
        tf1 = self.alloc("tf1", [8, 16], F32)
        tf2 = self.alloc("tf2", [8, 16], F32)
        XB = self.alloc("XB", [8, 128], BF16)
        CP = self.alloc("CP", [8, 128], BF16)
        Cst = self.alloc("Cst", [128], BF16)
        XA = self.alloc("XA", [8, 128], BF16)
        BD = self.alloc("BD", [8, 128], BF16)
        wu = self.alloc("wu5", [8, 128], BF16)
        uq = self.alloc("uq", [S], BF16)
        gq = self.alloc("gq", [S], BF16)
        GS = [self.alloc("GS", [8, 128], BF16) for _ in range(8)]
        AM = [self.alloc("AM", [9, 128], BF16) for _ in range(8)]
        CPm = [self.alloc("CPm", [8, 128], BF16) for _ in range(8)]
        H = [[self.alloc("H", [512], BF16) for _ in range(1)] for _ in range(4)]
        HP = self.alloc("HP", [8, 512], BF16)
        self.v("pool", "memset", [], [HP.r(g8) for g8 in range(8)], HP[:], 0.0)
        ytmp = [self.alloc("ytmp", [512], F32) for _ in range(2)]
        amt9 = [[self.alloc("amt9", [9, 128], BF16) for _ in range(2)] for _ in range(1)]
        swapb = self.alloc("swapb", [128], BF16)
        self.dma("pool", swapb[:], d["c_swap"], writes=[swapb.r()])
        colmb = self.alloc("colmb", [8, 128], BF16)
        self.dma("pool", colmb[:], d["c_colmask"], writes=[colmb.r()])
        for q in range(4):
            gsl = slice(q * 8, (q + 1) * 8)
            tt(bs[:], B1[:, gsl, :], bc(fr[:, gsl]), ALU.mult)
            tt(tf1[:], B2[:, gsl, :], bc(FIs[:, gsl]), ALU.mult)
            tt(bs[:], bs[:], tf1[:], ALU.add)
            tt(bw[:], B2[:, gsl, :], bc(fr[:, gsl]), ALU.mult)
            tt(tf1[:], B1[:, gsl, :], bc(FIs2[:, gsl]), ALU.mult)
            tt(bw[:], bw[:], tf1[:], ALU.add)
            xbr, cpr = XB.r(), CP.r()
            for j in range(8):
                tt(tf1[:], bs[:], bc(PR[:, j, gsl]), ALU.mult)
                tt(tf2[:], bw[:], bc(PIs1[:, j, gsl]), ALU.mult)
                self.v("dve", "tensor_tensor", [tabr], [tabr, xbr], out=XB[:, j, :].rearrange("p (g c) -> p g c", g=8), in0=tf1[:], in1=tf2[:], op=ALU.add)
            for t in range(8):
                tt(tf1[:], C1[:, gsl, :], bc(TA[:, t + 1, gsl]), ALU.mult)
                tt(tf2[:], C2[:, gsl, :], bc(TB[:, t + 1, gsl]), ALU.mult)
                self.v("dve", "tensor_tensor", [tabr], [tabr, cpr], out=CP[:, t, :].rearrange("p (g c) -> p g c", g=8), in0=tf1[:], in1=tf2[:], op=ALU.add)
            self.v("dve", "tensor_scalar", [tabr], [tabr, Cst.r()], out=Cst[:].rearrange("p (g c) -> p g c", g=8), in0=C1[:, gsl, :],
                   scalar1=sgn[:, 1:2], scalar2=None, op0=ALU.mult)
            (pbx,) = self.ps(1)
            pxv = pbx[:].bitcast(BF16).rearrange("p (a b) -> p a b", a=8)
            for s_ in range(8):
                self.tr(pxv[:, s_, :], XB[:, 7 - s_, :], self.ident[:], [xbr, self.ident.r()], [pbx.r()])
            self.P.op("act", lambda e, pxv=pxv: e.copy(out=XA[:], in_=pxv), reads=[pbx.r()], writes=[XA.r()])
            for jh in range(2):
                (pbd,) = self.ps(1)
                for jj in range(4):
                    self.mm(pbd[:, jj * 128:(jj + 1) * 128], XB[:, jh * 4 + jj, :], Cst[:], True, True, [xbr, Cst.r()], [pbd.r()])
                self.v("dve", "tensor_tensor", [pbd.r(), tabr], [BD.r()], out=BD[:, jh * 4:(jh + 1) * 4, :],
                       in0=pbd[:].rearrange("p (j c) -> p j c", j=4), in1=bdm[:].unsqueeze(1).to_broadcast([128, 4, 128]), op=ALU.mult)
            col = OFF_SSM + q * 128
            self.dma("pool", wu[:], d["w_in"][l][:, col:col + 128].rearrange("(k p) n -> p k n", p=128), writes=[wu.r()])
            for t8 in range(8):
                (pb,) = self.ps(1)
                for k in range(8):
                    self.mm(pb[:], wu[:, k, :], hT[:, k, t8 * 512:(t8 + 1) * 512], k == 0, k == 7, [wu.r()], [pb.r()])
                self.P.op("act", lambda e, pb=pb, t8=t8: e.copy(out=uq[:, t8 * 512:(t8 + 1) * 512], in_=pb[:]), reads=[pb.r()], writes=[uq.r()])
            uq8 = uq[:].rearrange("p (k s) -> p s k", s=8)
            def build_thunks(st4):
                ths = []
                for gi in range(4):
                    g8 = st4 * 4 + gi
                    g = q * 8 + g8
                    am_a = amt9[0][0]
                    am_b = amt9[0][1]
                    ths.append(lambda g8=g8: self.P.op("act", lambda e: e.activation(out=GS[g8][:], in_=XA[:], func=AF.Copy, scale=rowm[:, g8:g8 + 1]),
                                                       reads=[XA.r(), tabr], writes=[GS[g8].r()]))
                    ths.append(lambda g8=g8: self.v("dve", "tensor_tensor", [cpr, colmb.r()], [CPm[g8].r()], out=CPm[g8][:], in0=CP[:],
                                                    in1=colmb[:, g8, :].unsqueeze(1).to_broadcast([128, 8, 128]), op=ALU.mult))
                    for i in range(9):
                        ths.append(lambda am_a=am_a, i=i, g=g: self.P.op(
                            "act", lambda e: e.activation(out=am_a[:, i, :], in_=self.ident[:], func=AF.Copy, scale=QR[:, i, g:g + 1]),
                            reads=[tabr, self.ident.r()], writes=[am_a.r()]))
                        ths.append(lambda am_b=am_b, i=i, g=g: self.P.op(
                            "act", lambda e: e.activation(out=am_b[:, i, :], in_=swapb[:], func=AF.Copy, scale=QIs[:, i, g:g + 1]),
                            reads=[tabr, swapb.r()], writes=[am_b.r()]))
                    ths.append(lambda am_a=am_a, am_b=am_b, g8=g8: self.v("dve", "tensor_tensor", [am_a.r(), am_b.r()], [AM[g8].r()],
                                                                          out=AM[g8][:], in0=am_a[:], in1=am_b[:], op=ALU.add))
                return ths

            for th in build_thunks(0):
                th()
            for st4 in range(2):
                grp = [st4 * 4 + i for i in range(4)]
                nxt = build_thunks(st4 + 1) if st4 + 1 < 2 else []
                per = -(-len(nxt) // 9) if nxt else 0
                cur = {}
                for gi, g8 in enumerate(grp):
                    (pb,) = self.ps(1)
                    for s_ in range(8):
                        self.mm(pb[:], GS[g8][:, s_, :], uq8[:, s_, :], s_ == 0, s_ == 7, [GS[g8].r(), uq.r()], [pb.r()])
                    self.P.op("act", lambda e, pb=pb, gi=gi: e.copy(out=H[gi][0][:], in_=pb[:]), reads=[pb.r()], writes=[H[gi][0].r()])
                    cur[gi] = 0
                for i in range(9):
                    dd = 1 << i
                    for gi, g8 in enumerate(grp):
                        hc = H[gi][cur[gi]]
                        (pb,) = self.ps(1)
                        if i < 8:
                            self.mm(pb[:, dd:512], AM[g8][:, i, :], hc[:, 0:512 - dd], True, True, [AM[g8].r(), hc.r()], [pb.r()])
                            self.v("dve", "tensor_tensor", [pb.r(), hc.r()], [hc.r()], out=hc[:, dd:512], in0=pb[:, dd:512], in1=hc[:, dd:512], op=ALU.add)
                            continue
                        self.mm(pb[:, 0:dd], self.ident[:], hc[:, 0:dd], True, True, [self.ident.r(), hc.r()], [pb.r()])
                        self.mm(pb[:, dd:512], self.ident[:], hc[:, dd:512], True, False, [self.ident.r(), hc.r()], [pb.r()])
                        self.mm(pb[:, dd:512], AM[g8][:, i, :], hc[:, 0:512 - dd], False, True, [AM[g8].r(), hc.r()], [pb.r()])
                        self.v("dve", "tensor_copy", [pb.r()], [HP.r(g8)], out=HP[:, g8, 1:512], in_=pb[:, 0:511])
                    for _ in range(per):
                        if nxt:
                            nxt.pop(0)()
                while nxt:
                    nxt.pop(0)()
            for t in range(8):
                (py,) = self.ps(1)
                n_mm = (t + 1) + 8
                i_mm = 0
                for s_ in range(t + 1):
                    self.mm(py[:], BD[:, t - s_, :], uq8[:, s_, :], i_mm == 0, i_mm == n_mm - 1, [BD.r(), uq.r()], [py.r()])
                    i_mm += 1
                for g8 in range(8):
                    self.mm(py[:], CPm[g8][:, t, :], HP[:, g8, :], i_mm == 0, i_mm == n_mm - 1, [CPm[g8].r(), HP.r(g8)], [py.r()])
                    i_mm += 1
                yt_ = ytmp[t % 2]
                self.v("dve", "scalar_tensor_tensor", [uq.r(), py.r(), tabr], [yt_.r()], out=yt_[:], in0=uq8[:, t, :], scalar=dcol[:, q:q + 1],
                       in1=py[:], op0=ALU.mult, op1=ALU.add)
                self.act(gq[:].rearrange("p (k s) -> p s k", s=8)[:, t, :], yt_[:], AF.Gelu, [yt_.r()], [gq.r()])
            self.dma("sp", d["gsc"][q * 128:(q + 1) * 128, :], gq[:], reads=[gq.r()], writes=[self.dr("gsc", q)])

    def mix_s5_glu(self, l):
        d = self.d
        wgl = self.load_w("wglu", d["w_glu"][l], 4, 512)
        bgl = self.alloc("bgl", [4], F32)
        self.dma("sp", bgl[:], d["b_glu_c"][l], writes=[bgl.r()])
        gT = [self.alloc("gTt", [4, 512], BF16) for _ in range(2)]
        co = [self.alloc("co", [4, 512], BF16) for _ in range(2)]
        sg = [self.alloc("sg5", [512], F32) for _ in range(2)]
        for t8 in range(8):
            g_ = gT[t8 % 2]
            c_ = co[t8 % 2]
            self.dma("sp", g_[:], d["gsc"][:, t8 * 512:(t8 + 1) * 512].rearrange("(c p) t -> p c t", p=128),
                     reads=[self.dr("gsc", q) for q in range(4)], writes=[g_.r()])
            for oc in range(4):
                s_ = sg[oc % 2]
                (pz,) = self.ps(1)
                for kc in range(4):
                    self.mm(pz[:], wgl[:, kc, oc * 128:(oc + 1) * 128], g_[:, kc, :], kc == 0, kc == 3, [wgl.r(kc), g_.r()], [pz.r()])
                self.act(s_[:], pz[:], AF.Sigmoid, [pz.r(), bgl.r()], [s_.r()], bias=bgl[:, oc:oc + 1])
                eng = "pool" if oc % 2 == 0 else "dve"
                self.v(eng, "tensor_tensor", [s_.r(), g_.r()], [c_.r()], out=c_[:, oc, :], in0=s_[:], in1=g_[:, oc, :], op=ALU.mult)
            self.dma("sp", d["bout"][2, :, t8 * 512:(t8 + 1) * 512].rearrange("(c p) t -> p c t", p=128), c_[:],
                     reads=[c_.r()], writes=[self.dr("bout", (2, t8))])

    def mix_merge_a(self, l, hT, h2):
        d = self.d
        wg = self.alloc("wg", [8, 4, 512], BF16)
        for i in range(4):
            col = OFF_GATE + i * 1024 + h2 * 512
            self.dma("pool", wg[:, :, i, :], d["w_in"][l][:, col:col + 512].rearrange("(k p) n -> p k n", p=128), writes=[wg.r(i)])
        wu = self.alloc("wu", [16, 512], BF16)
        for i in range(4):
            self.dma("pool", wu[:, i * 4:(i + 1) * 4, :], d["w_up"][l, i][:, h2 * 512:(h2 + 1) * 512].rearrange("(k p) n -> p k n", p=128),
                     writes=[wu.r(i)])
        gb = self.alloc("gb", [32], F32)
        self.dma("sp", gb[:], d["gate_b_c"][l], writes=[gb.r()])
        brT = [self.alloc("brT", [16, 512], BF16) for _ in range(2)]
        mT = [self.alloc("mT", [4, 512], BF16) for _ in range(2)]
        sg = [self.alloc("sg", [512], F32) for _ in range(2)]
        pr = [self.alloc("pr", [512], F32) for _ in range(2)]
        ac = [self.alloc("ac", [512], F32) for _ in range(2)]
        isg = 0
        for t8 in range(8):
            br = brT[t8 % 2]
            m_ = mT[t8 % 2]
            for i in range(4):
                self.dma("sp", br[:, i * 4:(i + 1) * 4, :], d["bout"][i, :, t8 * 512:(t8 + 1) * 512].rearrange("(c p) t -> p c t", p=128),
                         reads=[self.dr("bout", (i, x)) for x in range(8)], writes=[br.r(i)])
            for dcl in range(4):
                dc = h2 * 4 + dcl
                a_ = ac[dcl % 2]
                for i in range(4):
                    s_ = sg[isg % 2]
                    p_ = pr[isg % 2]
                    isg += 1
                    (pg,) = self.ps(1)
                    for k in range(8):
                        self.mm(pg[:], wg[:, k, i, dcl * 128:(dcl + 1) * 128], hT[:, k, t8 * 512:(t8 + 1) * 512], k == 0, k == 7, [wg.r(i)], [pg.r()])
                    self.act(s_[:], pg[:], AF.Sigmoid, [pg.r(), gb.r()], [s_.r()], bias=gb[:, i * 8 + dc:i * 8 + dc + 1])
                    (pu,) = self.ps(1)
                    for kc in range(4):
                        self.mm(pu[:], wu[:, i * 4 + kc, dcl * 128:(dcl + 1) * 128], br[:, i * 4 + kc, :], kc == 0, kc == 3, [wu.r(i), br.r(i)], [pu.r()])
                    if i == 0:
                        self.v("dve", "tensor_tensor", [pu.r(), s_.r()], [a_.r()], out=a_[:], in0=pu[:], in1=s_[:], op=ALU.mult)
                    else:
                        self.v("dve", "tensor_tensor", [pu.r(), s_.r()], [p_.r()], out=p_[:], in0=pu[:], in1=s_[:], op=ALU.mult)
                        if i < 3:
                            self.v("dve", "tensor_tensor", [p_.r(), a_.r()], [a_.r()], out=a_[:], in0=p_[:], in1=a_[:], op=ALU.add)
                        else:
                            self.v("dve", "tensor_tensor", [p_.r(), a_.r()], [m_.r()], out=m_[:, dcl, :], in0=p_[:], in1=a_[:], op=ALU.add)
            self.dma("sp", d["mrg"][h2 * 512:(h2 + 1) * 512, t8 * 512:(t8 + 1) * 512].rearrange("(c p) t -> p c t", p=128), m_[:],
                     reads=[m_.r()], writes=[self.dr("mrg", (h2, t8))])

    def mix_merge_b(self, l, src, dst):
        d = self.d
        wo = self.load_w("wout", d["w_out"][l], 8, D)
        gpost = self.load_bcast("gpost", d["g_mix_post"][l:l + 1, :], D)
        mT = [self.alloc("mTb", [8, 512], BF16) for _ in range(2)]
        xt = [self.alloc("xtb", [D], F32) for _ in range(2)]
        for t8 in range(8):
            m_ = mT[t8 % 2]
            self.dma("sp", m_[:], d["mrg"][:, t8 * 512:(t8 + 1) * 512].rearrange("(c p) t -> p c t", p=128),
                     reads=[self.dr("mrg", (0, t8)), self.dr("mrg", (1, t8))], writes=[m_.r()])
            for t4 in range(4):
                tt = t8 * 4 + t4
                x_ = xt[tt % 2]
                self.dma("sp", x_[:], d[src][tt * 128:(tt + 1) * 128, :], reads=[self.dr(src, tt)], writes=[x_.r()])
                pbs = self.ps(2)
                for half in range(2):
                    for dc in range(8):
                        self.mm(pbs[half][:], m_[:, dc, t4 * 128:(t4 + 1) * 128], wo[:, dc, half * 512:(half + 1) * 512], dc == 0, dc == 7,
                                [m_.r(), wo.r(dc)], [pbs[half].r()])
                self.post_tile(pbs, x_[:], x_.r(), gpost, d[dst][tt * 128:(tt + 1) * 128, :], self.dr(dst, tt), self.junk)

W_SHAPES = {
    "g_mix_pre": (DEPTH, D), "g_mix_post": (DEPTH, D), "w_in": (DEPTH, D, IN_WIDTH), "gate_b": (DEPTH, 4, D),
    "w_up": (DEPTH, 4, 512, D), "w_out": (DEPTH, D, D),
    "g_x_pre": (DEPTH, D), "g_x_post": (DEPTH, D), "g_mem": (DEPTH, D),
    "w_cq": (DEPTH, D, 512), "w_ckv": (DEPTH, D, 1024), "w_co": (DEPTH, 512, D),
    "g_ff_pre": (DEPTH, D), "g_ff_post": (DEPTH, D), "w_ff1": (DEPTH, D, 4096), "w_ff2": (DEPTH, 4096, D),
}
W_SHAPES.update({
    "pool_w": (DEPTH, 4, 128, 128), "pool_scale_c": (DEPTH, 128, 4), "wsT": (DEPTH, 4, 128, 128),
    "sgu_ln_g_c": (DEPTH, 128, 4), "sgu_ln_b_c": (DEPTH, 128, 4), "b_s_r": (DEPTH, 512), "gate_b_c": (DEPTH, 128, 32),
    "att_bias": (3, 4, 128, 2, 256),
    "s5_are": (DEPTH, 128, 32), "s5_aim": (DEPTH, 128, 32), "s5_ldt": (DEPTH, 128, 32),
    "s5_b1": (DEPTH, 128, 32, 16), "s5_b2": (DEPTH, 128, 32, 16), "s5_c1": (DEPTH, 128, 32, 16), "s5_c2": (DEPTH, 128, 32, 16),
    "d_skip_c": (DEPTH, 128, 4), "b_glu_c": (DEPTH, 128, 4), "w_glu": (DEPTH, 512, 512),
})
CONSTS = {"c_ident": (128, 128), "c_invc": (1, 64), "c_tril": (128, 128), "c_att_mask": (128, 2, 256),
          "c_sgn": (128, 4), "c_swap": (128, 128), "c_bdmask": (128, 128), "c_rowmask": (128, 8), "c_colmask": (128, 8, 128)}
DEBUG_BOUT = False


def build_program(plan):
    nc = bass.Bass("TRN2", target_bir_lowering=False)
    dram = {}
    dram["x"] = nc.dram_tensor("x", [S, D], F32, kind="ExternalInput").ap()
    dram["mem"] = nc.dram_tensor("mem", [NMEM, D], F32, kind="ExternalInput").ap()
    for n, shp in W_SHAPES.items():
        dram[n] = nc.dram_tensor(n, list(shp), F32, kind="ExternalInput").ap()
    for n, shp in CONSTS.items():
        dram[n] = nc.dram_tensor(n, list(shp), F32, kind="ExternalInput").ap()
    dram["y"] = nc.dram_tensor("y", [S, D], F32, kind="ExternalOutput").ap()
    dram["xr"] = nc.dram_tensor("xr", [S, D], F32, kind="Internal").ap()
    dram["bout"] = nc.dram_tensor("bout", [4, 512, S], BF16, kind="ExternalOutput" if DEBUG_BOUT else "Internal").ap()
    dram["mrg"] = nc.dram_tensor("mrg", [D, S], BF16, kind="Internal").ap()
    dram["gsc"] = nc.dram_tensor("gsc", [512, S], BF16, kind="Internal").ap()
    with contextlib.ExitStack() as st:
        kb = KB(nc, st, dram)
        kb.setup_consts()
        kb.mark_perm()
        for i, item in enumerate(plan):
            kind, l = item[0], item[1]
            src = "x" if i == 0 else "xr"
            dst = "y" if i == len(plan) - 1 else "xr"
            getattr(kb, kind)(l, src, dst, *item[2:])
        kb.P.emit()
        nops = kb.P.nops
    return nc, nops


def _t5_bucket(n):
    exact = 16
    nf = np.maximum(n, 1).astype(np.float32)
    large = exact + (np.log(nf / exact) / np.log(2048 / exact) * (32 - exact)).astype(np.int32)
    large = np.minimum(large, 31)
    return np.where(n < exact, n, large).astype(np.int32)


def host_consts():
    c = {"c_ident": np.eye(128, dtype=np.float32)}
    invc = np.zeros((1, 64), np.float32)
    for gi in range(4):
        w = 2 << gi
        invc[0, gi * 16:(gi + 1) * 16] = 1.0 / np.minimum(np.arange(16) + 1, w)
    c["c_invc"] = invc
    s_ = np.arange(128)
    c["c_tril"] = (s_[:, None] <= s_[None, :]).astype(np.float32)
    dist = np.arange(256)[None, :] - np.arange(128)[:, None]
    m = ((dist >= 0) & (dist <= 128)).astype(np.float32)
    c["c_att_mask"] = np.ascontiguousarray(np.broadcast_to(m[:, None, :], (128, 2, 256)))
    sg = np.ones((128, 4), np.float32)
    sg[:64, 0] = -1.0
    sg[64:, 1] = -1.0
    c["c_sgn"] = sg
    p = np.arange(128)
    c["c_swap"] = (p[None, :] == ((p[:, None] + 64) % 128)).astype(np.float32)
    c["c_bdmask"] = ((p[:, None] // 16) == (p[None, :] // 16)).astype(np.float32)
    c["c_rowmask"] = ((p[:, None] // 16) == np.arange(8)[None, :]).astype(np.float32)
    c["c_colmask"] = np.ascontiguousarray(np.broadcast_to(((p[None, None, :] // 16) == np.arange(8)[None, :, None]), (128, 8, 128)).astype(np.float32))
    return c


def host_layout(inputs):
    f = lambda n: np.asarray(inputs[n], dtype=np.float32)
    o = {}
    for n in ("g_mix_pre", "g_mix_post", "w_in", "w_up", "w_out", "g_x_pre", "g_x_post", "g_mem", "w_cq", "w_ckv", "w_co",
              "g_ff_pre", "g_ff_post", "w_ff1", "w_ff2", "pool_w"):
        o[n] = f(n)
    o["gate_b"] = f("gate_b")
    o["pool_scale_c"] = f("pool_scale").reshape(DEPTH, 4, 128).transpose(0, 2, 1)
    o["gate_b_c"] = f("gate_b").reshape(DEPTH, 4, 8, 128).transpose(0, 3, 1, 2).reshape(DEPTH, 128, 32)
    o["wsT"] = f("w_s").transpose(0, 1, 3, 2)
    o["b_s_r"] = f("b_s").reshape(DEPTH, 512)
    o["sgu_ln_g_c"] = f("sgu_ln_g").reshape(DEPTH, 4, 128).transpose(0, 2, 1)
    o["sgu_ln_b_c"] = f("sgu_ln_b").reshape(DEPTH, 4, 128).transpose(0, 2, 1)
    rb = f("rel_bias")
    dist = np.clip(np.arange(256)[None, :] - np.arange(128)[:, None], 0, 128)
    ab = np.zeros((3, 4, 128, 2, 256), np.float32)
    for g, dil in enumerate(DILS):
        bk = _t5_bucket(dist * dil)
        for c in range(4):
            for hd in range(2):
                ab[g, c, :, hd, :] = rb[bk, g * 8 + 2 * c + hd]
    o["att_bias"] = ab
    o["w_glu"] = f("w_glu")
    dup = lambda a: np.concatenate([a, a], axis=1)
    o["s5_are"] = dup(f("a_re").transpose(0, 2, 1))
    o["s5_aim"] = dup(f("a_im").transpose(0, 2, 1))
    o["s5_ldt"] = np.broadcast_to(f("log_dt")[:, None, :], (DEPTH, 128, 32))
    brt, bit = f("b_re").transpose(0, 2, 1, 3), f("b_im").transpose(0, 2, 1, 3)
    crt, cit = f("c_re").transpose(0, 3, 1, 2), f("c_im").transpose(0, 3, 1, 2)
    o["s5_b1"] = np.concatenate([brt, bit], axis=1)
    o["s5_b2"] = np.concatenate([bit, brt], axis=1)
    o["s5_c1"] = np.concatenate([crt, cit], axis=1)
    o["s5_c2"] = np.concatenate([cit, crt], axis=1)
    o["d_skip_c"] = f("d_skip").reshape(DEPTH, 4, 128).transpose(0, 2, 1)
    o["b_glu_c"] = f("b_glu").reshape(DEPTH, 4, 128).transpose(0, 2, 1)
    return {k: np.ascontiguousarray(v, dtype=np.float32) for k, v in o.items()}


FULL_PLAN = [(k, l) for l in range(DEPTH) for k in ("mixer", "cross", "ffn")]


def kernel(**inputs):
    plan = inputs.pop("_plan", FULL_PLAN)
    cores = inputs.pop("_cores", 8)
    import time as _t
    _t0 = _t.time()
    nc, _nops = build_program(plan)
    print(f"[kernel] build {_t.time() - _t0:.1f}s nops={_nops}", flush=True)
    lay = host_layout(inputs)
    shared = {n: lay[n] for n in W_SHAPES}
    shared.update(host_consts())
    x = np.asarray(inputs["x"], dtype=np.float32)
    mem = np.asarray(inputs["mem"], dtype=np.float32)
    in_maps = []
    for c in range(cores):
        m = dict(shared)
        m["x"] = np.ascontiguousarray(x[c])
        m["mem"] = np.ascontiguousarray(mem[c])
        in_maps.append(m)
    _t0 = _t.time()
    res = run_bass_kernel_spmd(nc, in_maps, core_ids=list(range(cores)))
    print(f"[kernel] run {_t.time() - _t0:.1f}s", flush=True)
    if DEBUG_BOUT:
        global _LAST_BOUT
        _LAST_BOUT = [np.asarray(r["bout"]) for r in res.results]
    return np.stack([np.asarray(r["y"], dtype=np.float32) for r in res.results], axis=0)
```

```python
import contextlib
import numpy as np
import concourse.bass as bass
import concourse.mybir as mybir
from concourse.bass_utils import run_bass_kernel_spmd

F32 = mybir.dt.float32
BF16 = mybir.dt.bfloat16
U8 = mybir.dt.uint8
I32 = mybir.dt.int32
ALU = mybir.AluOpType
AF = mybir.ActivationFunctionType

SEM_CAP = 30000
DMA_SLOTS = 8
ARENA_BYTES = 212480

S = 4096
D = 1024
NT = S // 128
DEPTH = 4
EPS = 1e-6
NMEM = 256
IN_WIDTH = 10752
OFF_POOL, OFF_ATT, OFF_SSM, OFF_SGU, OFF_GATE = 0, 512, 5120, 5632, 6656
DILS = (1, 4, 16)


class Res:
    __slots__ = ("name", "lw", "rd")

    def __init__(self, name=""):
        self.name = name
        self.lw = None
        self.rd = []


class T:
    def __init__(self, apview, name):
        self.v = apview
        self.name = name
        self._res = {}

    def __getitem__(self, k):
        return self.v[k]

    def r(self, key=None):
        x = self._res.get(key)
        if x is None:
            x = Res(f"{self.name}:{key}")
            self._res[key] = x
        return x


class Op:
    __slots__ = ("eng", "fn", "deps", "signal", "dma", "slot", "use", "idx", "cnt")

    def __init__(self, eng, fn, dma):
        self.eng = eng
        self.fn = fn
        self.deps = []
        self.signal = False
        self.dma = dma
        self.slot = None
        self.use = None
        self.idx = None
        self.cnt = None


class Prog:
    ENGS = ("pe", "act", "dve", "pool", "sp")

    def __init__(self, nc):
        self.nc = nc
        self.ops = {e: [] for e in self.ENGS}
        self.ndma = {e: 0 for e in self.ENGS}
        self.waited = {e: {} for e in self.ENGS}
        self.pending = {e: [] for e in self.ENGS}
        self.nops = 0

    def _need(self, op, key):
        if key is None:
            return
        X = op.eng
        if key[0] == "c":
            _, Y, j = key
            if Y == X and not op.dma:
                return
            k = ("c", Y)
            if self.waited[X].get(k, -1) >= j:
                return
            self.waited[X][k] = j
            self.ops[Y][j].signal = True
            op.deps.append(key)
        else:
            _, q, slot, use = key
            k = ("d", q, slot)
            if self.waited[X].get(k, 0) >= use:
                return
            self.waited[X][k] = use
            op.deps.append(key)

    def _same(self, o, key):
        eng = o.eng
        if eng == "pe":
            return
        k = ("c", eng)
        if self.waited[eng].get(k, -1) < key[2]:
            self.waited[eng][k] = key[2]
            self.ops[eng][key[2]].signal = True
            o.deps.append(key)

    def barrier(self):
        keys = []
        for e in self.ENGS:
            if self.ops[e]:
                last = None
                for o in reversed(self.ops[e]):
                    if not o.dma:
                        last = o
                        break
                if last is not None:
                    keys.append(("c", e, last.idx))
            n = self.ndma[e]
            for s in range(min(n, DMA_SLOTS)):
                keys.append(("d", e, s, (n - 1 - s) // DMA_SLOTS + 1))
        for e in self.ENGS:
            self.pending[e] = list(keys)

    def op(self, eng, fn, reads=(), writes=(), dma=False):
        o = Op(eng, fn, dma)
        o.idx = len(self.ops[eng])
        if self.pending[eng]:
            for k in self.pending[eng]:
                if k[0] == "c" and k[1] == eng:
                    if dma:
                        self._need(o, k)
                    continue
                self._need(o, k)
            self.pending[eng] = []
        if dma:
            n = self.ndma[eng]
            self.ndma[eng] = n + 1
            o.slot = n % DMA_SLOTS
            o.use = n // DMA_SLOTS + 1
            mykey = ("d", eng, o.slot, o.use)
            if o.use > 1:
                self._need(o, ("d", eng, o.slot, o.use - 1))
        else:
            mykey = ("c", eng, o.idx)
        for r in reads:
            lw = r.lw
            if lw is not None:
                if lw[0] == "c" and lw[1] == eng and not dma:
                    self._same(o, lw)
                else:
                    self._need(o, lw)
        for w in writes:
            if w.lw is not None:
                if w.lw[0] == "c" and w.lw[1] == eng and not dma:
                    self._same(o, w.lw)
                else:
                    self._need(o, w.lw)
            for rk in w.rd:
                if rk[0] == "c" and rk[1] == eng and not dma:
                    self._same(o, rk)
                else:
                    self._need(o, rk)
        for r in reads:
            r.rd.append(mykey)
            if len(r.rd) > 48:
                last = {}
                for kk in r.rd:
                    kid = kk[:2] if kk[0] == "c" else kk[:3]
                    if kid not in last or last[kid][-1] < kk[-1]:
                        last[kid] = kk
                r.rd = list(last.values())
        for w in writes:
            w.lw = mykey
            w.rd = []
        self.ops[eng].append(o)
        self.nops += 1
        return o

    def emit(self):
        nc = self.nc
        nsig = {}
        for e in self.ENGS:
            c = 0
            for o in self.ops[e]:
                if o.signal and not o.dma:
                    c += 1
                    o.cnt = c
            nsig[e] = c
        with contextlib.ExitStack() as st:
            csem = {}
            for e in self.ENGS:
                n = max(1, -(-nsig[e] // SEM_CAP))
                csem[e] = [st.enter_context(nc.semaphore(f"c_{e}_{i}")) for i in range(n)]
            dsem = {}
            for e in self.ENGS:
                if self.ndma[e]:
                    dsem[e] = [st.enter_context(nc.semaphore(f"d_{e}_{i}")) for i in range(DMA_SLOTS)]
            block = st.enter_context(nc.Block())

            def run(eng_name, engobj):
                for o in self.ops[eng_name]:
                    for d in o.deps:
                        if d[0] == "c":
                            p = self.ops[d[1]][d[2]]
                            c = p.cnt - 1
                            engobj.wait_ge(csem[d[1]][c // SEM_CAP], c % SEM_CAP + 1)
                        else:
                            engobj.wait_ge(dsem[d[1]][d[2]], 16 * d[3])
                    ins = o.fn(engobj)
                    if o.dma:
                        ins.then_inc(dsem[eng_name][o.slot], 16)
                    elif o.signal:
                        c = o.cnt - 1
                        ins.then_inc(csem[eng_name][c // SEM_CAP], 1)

            @block.tensor
            def _(e):
                run("pe", e)

            @block.scalar
            def _(e):
                run("act", e)

            @block.vector
            def _(e):
                run("dve", e)

            @block.gpsimd
            def _(e):
                run("pool", e)

            @block.sync
            def _(e):
                run("sp", e)
                for q in self.ENGS:
                    n = self.ndma[q]
                    for s in range(min(n, DMA_SLOTS)):
                        e.wait_ge(dsem[q][s], 16 * ((n - 1 - s) // DMA_SLOTS + 1))


DT_SIZE = {F32: 4, BF16: 2, U8: 1, I32: 4}


class KB:
    def __init__(self, nc, st, dram):
        self.nc = nc
        self.P = Prog(nc)
        self.d = dram
        self.arena = st.enter_context(nc.sbuf_tensor("arena", [128, ARENA_BYTES], U8))
        self.aoff = 0
        self.perm_off = 0
        self.top_reserved = 0
        self.pre_w1 = None
        self.psb = [T(st.enter_context(nc.psum_tensor(f"psb{i}", [128, 512], F32)), f"psb{i}") for i in range(8)]
        self.ps_i = 0
        self.dres = {}
        self.uid = 0

    def alloc(self, name, free_shape, dt, perm=False):
        n = int(np.prod(free_shape)) * DT_SIZE[dt]
        n_al = (n + 63) // 64 * 64
        off = self.aoff
        if perm == "top":
            self.top_reserved += n_al
            off = ARENA_BYTES - self.top_reserved
            assert off >= self.aoff, f"arena top overflow at {name}"
        else:
            assert off + n_al <= ARENA_BYTES - self.top_reserved, f"arena overflow at {name}: {off}+{n_al}"
            self.aoff += n_al
        v = self.arena[:, off:off + n].bitcast(dt)
        if len(free_shape) == 2:
            v = v.rearrange("p (a b) -> p a b", a=free_shape[0])
        elif len(free_shape) == 3:
            v = v.rearrange("p (a b c) -> p a b c", a=free_shape[0], b=free_shape[1])
        self.uid += 1
        return T(v, f"{name}{self.uid}")

    def mark_perm(self):
        self.perm_off = self.aoff

    def reset(self):
        self.P.barrier()
        self.aoff = self.perm_off

    def ps(self, n=1):
        if n == 2 and self.ps_i % 2 == 1:
            self.ps_i += 1
        out = [self.psb[(self.ps_i + i) % 8] for i in range(n)]
        self.ps_i = (self.ps_i + n) % 8
        return out

    def dr(self, name, key=None):
        k = (name, key)
        x = self.dres.get(k)
        if x is None:
            x = Res(f"dram:{name}:{key}")
            self.dres[k] = x
        return x

    def dma(self, q, out, in_, reads=(), writes=()):
        return self.P.op(q, lambda e: e.dma_start(out=out, in_=in_), reads=reads, writes=writes, dma=True)

    def mm(self, out, lhsT, rhs, start, stop, reads, writes):
        return self.P.op("pe", lambda e: e.matmul(out, lhsT=lhsT, rhs=rhs, start=start, stop=stop),
                         reads=reads, writes=writes)

    def tr(self, out, in_, ident, reads, writes):
        return self.P.op("pe", lambda e: e.transpose(out=out, in_=in_, identity=ident), reads=reads, writes=writes)

    def act(self, out, in_, func, reads, writes, bias=None, scale=None, accum_out=None):
        kw = {}
        if bias is not None:
            kw["bias"] = bias
        if scale is not None:
            kw["scale"] = scale
        if accum_out is not None:
            kw["accum_out"] = accum_out
        return self.P.op("act", lambda e: e.activation(out=out, in_=in_, func=func, **kw), reads=reads, writes=writes)

    def v(self, eng, name, reads, writes, *a, **kw):
        return self.P.op(eng, lambda e: getattr(e, name)(*a, **kw), reads=reads, writes=writes)

    def setup_consts(self):
        self.ident_f = self.alloc("identf", [128], F32, perm=True)
        self.ident = self.alloc("ident", [128], BF16, perm=True)
        self.ones = self.alloc("ones", [128], BF16, perm=True)
        self.dma("sp", self.ident_f[:], self.d["c_ident"], writes=[self.ident_f.r()])
        self.v("dve", "tensor_copy", [self.ident_f.r()], [self.ident.r()], out=self.ident[:], in_=self.ident_f[:])
        self.v("dve", "memset", [], [self.ones.r()], self.ones[:], 1.0)
        self.stat_i = 0
        self.stats = self.alloc("stats", [64, 4], F32, perm=True)

    def stat(self):
        i = self.stat_i
        self.stat_i = (i + 1) % 64
        return self.stats[:, i, :], self.stats.r(i)

    def load_bcast(self, name, row_ap, n):
        t = self.alloc(name, [n], F32)
        self.dma("sp", t[:], row_ap.broadcast_to([128, n]), writes=[t.r()])
        return t

    def load_w(self, name, src, kch, ncols, top=False):
        t = self.alloc(name, [kch, ncols], BF16, perm="top" if top else False)
        step = max(1, (4 * 1024 * 1024) // (128 * ncols * 4))
        for k0 in range(0, kch, step):
            k1 = min(kch, k0 + step)
            self.dma("pool", t[:, k0:k1, :], src[k0 * 128:k1 * 128, :].rearrange("(c p) n -> p c n", p=128),
                     writes=[t.r(k) for k in range(k0, k1)])
        return t

    def rstd_from_ss(self, ss_ap, ss_res, n):
        sq, sq_r = self.stat()
        self.act(sq[:, 0:1], ss_ap, AF.Ln, [ss_res, self.epsb.r()], [sq_r], bias=self.epsb[:, 0:1], scale=1.0 / n)
        self.act(sq[:, 1:2], sq[:, 0:1], AF.Exp, [sq_r], [sq_r], scale=-0.5)
        return sq[:, 1:2], sq_r

    def norm_tile(self, xt, xt_res, gt, hT, hT_res, col0, junk, defer=False):
        st_, st_r = self.stat()
        self.act(junk[:], xt, AF.Square, [xt_res], [junk.r(), st_r], accum_out=st_[:, 0:1])
        rstd, rr = self.rstd_from_ss(st_[:, 0:1], st_r, D)
        hb = self.hb[self.hb_i % len(self.hb)]
        self.hb_i += 1
        self.v("dve", "scalar_tensor_tensor", [xt_res, rr, gt.r()], [hb.r()],
               out=hb[:], in0=xt, scalar=rstd, in1=gt[:], op0=ALU.mult, op1=ALU.mult)
        (pb,) = self.ps(1)
        pv = pb[:].bitcast(BF16).rearrange("p (a b) -> p a b", a=8)
        for k in range(8):
            self.tr(pv[:, k, :], hb[:, k * 128:(k + 1) * 128], self.ident[:], [hb.r(), self.ident.r()], [pb.r()])
        def part_b():
            self.P.op("act", lambda e: e.copy(out=hT[:, 0:8, col0:col0 + 128], in_=pv), reads=[pb.r()], writes=[hT_res])
        if defer:
            return part_b
        part_b()
        return None

    def post_tile(self, pbs, xt, xt_res, gpost, dst_ap, dst_res, junk):
        st_, st_r = self.stat()
        for i in range(2):
            self.act(junk[:, i * 512:(i + 1) * 512], pbs[i][:], AF.Square, [pbs[i].r()], [junk.r(), st_r],
                     accum_out=st_[:, i:i + 1])
        self.v("dve", "tensor_add", [st_r], [st_r], out=st_[:, 2:3], in0=st_[:, 0:1], in1=st_[:, 1:2])
        rstd, rr = self.rstd_from_ss(st_[:, 2:3], st_r, D)
        tmp = self.ptmp[self.ptmp_i % 2]
        self.ptmp_i += 1
        for i in range(2):
            self.v("dve", "tensor_tensor", [pbs[i].r(), gpost.r()], [tmp.r()], out=tmp[:, i * 512:(i + 1) * 512], in0=pbs[i][:],
                   in1=gpost[:, i * 512:(i + 1) * 512], op=ALU.mult)
        self.v("dve", "scalar_tensor_tensor", [tmp.r(), rr, xt_res], [tmp.r()], out=tmp[:], in0=tmp[:], scalar=rstd, in1=xt,
               op0=ALU.mult, op1=ALU.add)
        self.dma("pool", dst_ap, tmp[:], reads=[tmp.r()], writes=[dst_res])

    def common_bufs(self):
        self.epsb = self.alloc("epsb", [1], F32)
        self.v("dve", "memset", [], [self.epsb.r()], self.epsb[:], EPS)
        self.hb = [self.alloc("hb", [D], BF16) for _ in range(2)]
        self.hb_i = 0
        self.ptmp = [self.alloc("ptmp", [D], F32) for _ in range(2)]
        self.ptmp_i = 0
        self.junk = self.alloc("junk", [D], BF16)

    def ffn(self, l, src, dst):
        d = self.d
        G = 256
        ng = S // G
        tpg = G // 128
        self.reset()
        self.common_bufs()
        gpre = self.load_bcast("gpre", d["g_ff_pre"][l:l + 1, :], D)
        gpost = self.load_bcast("gpost", d["g_ff_post"][l:l + 1, :], D)
        if self.pre_w1 is not None:
            w1 = self.pre_w1
        else:
            w1 = self.load_w("w1", d["w_ff1"][l], 8, 4096)
        w2 = self.load_w("w2", d["w_ff2"][l], 32, D)
        xg = [self.alloc("xg", [tpg, D], F32) for _ in range(2)]
        hTg = [self.alloc("hTg", [8, G], BF16) for _ in range(2)]
        hid = self.alloc("hid", [32, G], BF16)
        rl = self.alloc("rl", [2, G], F32)
        def norm_group(g):
            xb = xg[g % 2]
            hT = hTg[g % 2]
            pend = None
            for t in range(tpg):
                tt = g * tpg + t
                self.dma("sp", xb[:, t, :], d[src][tt * 128:(tt + 1) * 128, :], reads=[self.dr(src, tt)], writes=[xb.r(t)])
                nb_ = self.norm_tile(xb[:, t, :], xb.r(t), gpre, hT, hT.r(), t * 128, self.junk, defer=True)
                if pend is not None:
                    pend()
                pend = nb_
            pend()

        norm_group(0)
        for g in range(ng):
            xb = xg[g % 2]
            hT = hTg[g % 2]
            for j in range(32):
                (pb,) = self.ps(1)
                for k in range(8):
                    self.mm(pb[:, 0:G], w1[:, k, j * 128:(j + 1) * 128], hT[:, k, :], k == 0, k == 7,
                            [w1.r(k), hT.r()], [pb.r()])
                self.act(rl[:, j % 2, :], pb[:, 0:G], AF.Relu, [pb.r()], [rl.r(j % 2)])
                eng = "pool" if j % 2 == 0 else "dve"
                self.v(eng, "tensor_tensor", [rl.r(j % 2)], [hid.r(j)], out=hid[:, j, :], in0=rl[:, j % 2, :],
                       in1=rl[:, j % 2, :], op=ALU.mult)
            if g + 1 < ng:
                norm_group(g + 1)
            for t in range(tpg):
                tt = g * tpg + t
                pbs = self.ps(2)
                for half in range(2):
                    for j in range(32):
                        self.mm(pbs[half][:], hid[:, j, t * 128:(t + 1) * 128], w2[:, j, half * 512:(half + 1) * 512],
                                j == 0, j == 31, [hid.r(j), w2.r(j)], [pbs[half].r()])
                self.post_tile(pbs, xb[:, t, :], xb.r(t), gpost, d[dst][tt * 128:(tt + 1) * 128, :], self.dr(dst, tt), self.junk)

    def cross(self, l, src, dst):
        d = self.d
        G = 512
        ng = S // G
        tpg = G // 128
        self.reset()
        self.common_bufs()
        gpre = self.load_bcast("gpre", d["g_x_pre"][l:l + 1, :], D)
        gpost = self.load_bcast("gpost", d["g_x_post"][l:l + 1, :], D)
        gmem = self.load_bcast("gmem", d["g_mem"][l:l + 1, :], D)
        wq = self.load_w("wq", d["w_cq"][l], 8, 512)
        wkv = self.load_w("wkv", d["w_ckv"][l], 8, 1024)
        wo = self.load_w("wo", d["w_co"][l], 4, D)
        memT = self.alloc("memT", [8, NMEM], BF16)
        kT = self.alloc("kT", [4, NMEM], BF16)
        vv = self.alloc("vv", [2, 512], BF16)
        mt_ = self.alloc("memt", [D], F32)
        for mt in range(2):
            self.dma("sp", mt_[:], d["mem"][mt * 128:(mt + 1) * 128, :], writes=[mt_.r()])
            self.norm_tile(mt_[:], mt_.r(), gmem, memT, memT.r(), mt * 128, self.junk)
        for hd in range(4):
            (pb,) = self.ps(1)
            for k in range(8):
                self.mm(pb[:, 0:NMEM], wkv[:, k, hd * 128:(hd + 1) * 128], memT[:, k, :], k == 0, k == 7,
                        [wkv.r(k), memT.r()], [pb.r()])
            self.P.op("act", lambda e, pb=pb, hd=hd: e.copy(out=kT[:, hd, :], in_=pb[:, 0:NMEM]), reads=[pb.r()], writes=[kT.r()])
        for mt in range(2):
            (pb,) = self.ps(1)
            for k in range(8):
                self.mm(pb[:], memT[:, k, mt * 128:(mt + 1) * 128], wkv[:, k, 512:1024], k == 0, k == 7,
                        [wkv.r(k), memT.r()], [pb.r()])
            self.P.op("act", lambda e, pb=pb, mt=mt: e.copy(out=vv[:, mt, :], in_=pb[:]), reads=[pb.r()], writes=[vv.r()])
        xg = [self.alloc("xg", [tpg, D], F32) for _ in range(2)]
        hTg = [self.alloc("hTg", [8, G], BF16) for _ in range(2)]
        qT = [self.alloc("qT", [G], BF16) for _ in range(3)]
        PT = [self.alloc("PT", [2, G], BF16) for _ in range(3)]
        rden = [self.alloc("rden", [G], F32) for _ in range(2)]
        oTs = [self.alloc("oT", [4, G], BF16) for _ in range(2)]
        scale = 128.0 ** -0.5
        def norm_group(g):
            xb = xg[g % 2]
            hT = hTg[g % 2]
            pend = None
            for t in range(tpg):
                tt = g * tpg + t
                self.dma("sp", xb[:, t, :], d[src][tt * 128:(tt + 1) * 128, :], reads=[self.dr(src, tt)], writes=[xb.r(t)])
                nb_ = self.norm_tile(xb[:, t, :], xb.r(t), gpre, hT, hT.r(), t * 128, self.junk, defer=True)
                if pend is not None:
                    pend()
                pend = nb_
            pend()

        def stA(g, hd):
            hT = hTg[g % 2]
            q = qT[hd % 3]
            (pb,) = self.ps(1)
            for k in range(8):
                self.mm(pb[:], wq[:, k, hd * 128:(hd + 1) * 128], hT[:, k, :], k == 0, k == 7, [wq.r(k), hT.r()], [pb.r()])
            self.P.op("act", lambda e, pb=pb, q=q: e.copy(out=q[:], in_=pb[:]), reads=[pb.r()], writes=[q.r()])

        def stB(g, hd):
            q = qT[hd % 3]
            pt = PT[hd % 3]
            for mt in range(2):
                (pb,) = self.ps(1)
                self.mm(pb[:], kT[:, hd, mt * 128:(mt + 1) * 128], q[:], True, True, [kT.r(), q.r()], [pb.r()])
                self.act(pt[:, mt, :], pb[:], AF.Exp, [pb.r()], [pt.r()], scale=scale)

        def stC(g, hd):
            oT = oTs[g % 2]
            pt = PT[hd % 3]
            rd = rden[hd % 2]
            (pn,) = self.ps(1)
            (pd,) = self.ps(1)
            for mt in range(2):
                self.mm(pn[:], vv[:, mt, hd * 128:(hd + 1) * 128], pt[:, mt, :], mt == 0, mt == 1, [vv.r(), pt.r()], [pn.r()])
            for mt in range(2):
                self.mm(pd[:], self.ones[:], pt[:, mt, :], mt == 0, mt == 1, [self.ones.r(), pt.r()], [pd.r()])
            self.act(rd[:], pd[:], AF.Ln, [pd.r()], [rd.r()])
            self.act(rd[:], rd[:], AF.Exp, [rd.r()], [rd.r()], scale=-1.0)
            self.v("dve", "tensor_tensor", [pn.r(), rd.r()], [oT.r()], out=oT[:, hd, :], in0=pn[:], in1=rd[:], op=ALU.mult)

        norm_group(0)
        stA(0, 0)
        stA(0, 1)
        self.pre_w1 = self.load_w("w1pre", d["w_ff1"][l], 8, 4096, top=True)
        for g in range(ng):
            xb = xg[g % 2]
            oT = oTs[g % 2]
            stB(g, 0); stA(g, 2); stB(g, 1); stC(g, 0); stA(g, 3); stB(g, 2); stC(g, 1); stB(g, 3); stC(g, 2); stC(g, 3)
            if g + 1 < ng:
                norm_group(g + 1)
                stA(g + 1, 0)
                stA(g + 1, 1)
            for t in range(tpg):
                tt = g * tpg + t
                pbs = self.ps(2)
                for half in range(2):
                    for hd in range(4):
                        self.mm(pbs[half][:], oT[:, hd, t * 128:(t + 1) * 128], wo[:, hd, half * 512:(half + 1) * 512],
                                hd == 0, hd == 3, [oT.r(), wo.r(hd)], [pbs[half].r()])
                self.post_tile(pbs, xb[:, t, :], xb.r(t), gpost, d[dst][tt * 128:(tt + 1) * 128, :], self.dr(dst, tt), self.junk)

    def reset_to(self, off):
        self.P.barrier()
        self.aoff = off

    def mixer(self, l, src, dst, parts=("pool", "att", "s5", "sgu", "merge")):
        d = self.d
        self.reset()
        self.top_reserved = 0
        self.pre_w1 = None
        self.common_bufs()
        hT = self.alloc("hT", [8, S], BF16)
        base = self.aoff
        gpre = self.load_bcast("gpre", d["g_mix_pre"][l:l + 1, :], D)
        xt = [self.alloc("xt", [D], F32) for _ in range(4)]
        hb_save = self.hb
        self.hb = hb_save + [self.alloc("hbx", [D], BF16) for _ in range(2)]
        junks = [self.junk] + [self.alloc("junkx", [D], BF16) for _ in range(1)]
        pend0 = None
        for tt in range(NT):
            x_ = xt[tt % 4]
            self.dma("sp", x_[:], d[src][tt * 128:(tt + 1) * 128, :], reads=[self.dr(src, tt)], writes=[x_.r()])
            nb_ = self.norm_tile(x_[:], x_.r(), gpre, hT, hT.r(), tt * 128, junks[tt % 2], defer=True)
            if pend0 is not None:
                pend0()
            pend0 = nb_
        pend0()
        self.hb = hb_save
        if "pool" in parts or "sgu" in parts:
            self.reset_to(base)
            self.mix_pool_sgu(l, hT)
        if "att" in parts:
            self.reset_to(base)
            self.mix_att(l, hT)
        if "s5" in parts:
            self.reset_to(base)
            self.mix_s5(l, hT)
            self.reset_to(base)
            self.mix_s5_glu(l)
        if "merge" in parts:
            for h2 in range(2):
                self.reset_to(base)
                self.mix_merge_a(l, hT, h2)
            self.reset_to(base)
            self.mix_merge_b(l, src, dst)

    def mix_pool_sgu(self, l, hT):
        d = self.d
        wp = self.load_w("wp", d["w_in"][l][:, OFF_POOL:OFF_POOL + 512], 8, 512)
        pw = self.alloc("pw", [4, 128], BF16)
        self.dma("pool", pw[:], d["pool_w"][l].rearrange("g c e -> c g e"), writes=[pw.r()])
        psc = self.alloc("psc", [4], F32)
        self.dma("sp", psc[:], d["pool_scale_c"][l], writes=[psc.r()])
        invc = self.alloc("invc", [64], F32)
        self.dma("sp", invc[:], d["c_invc"].broadcast_to([128, 64]), writes=[invc.r()])
        hp = self.alloc("hp", [16 + S], F32)
        A = self.alloc("pA", [16 + S], F32)
        B = self.alloc("pB", [16 + S], F32)
        for b_ in (hp, A, B):
            self.v("pool", "memset", [], [b_.r()], b_[:, 0:16], 0.0)
        pbf = self.alloc("pbf", [S], BF16)
        ao = self.alloc("ao", [S], BF16)
        t16 = self.alloc("t16", [16], F32)
        pstate = {}

        def pool_a(gi):
            w = 2 << gi
            for t8 in range(8):
                (pb,) = self.ps(1)
                for k in range(8):
                    self.mm(pb[:], wp[:, k, gi * 128:(gi + 1) * 128], hT[:, k, t8 * 512:(t8 + 1) * 512], k == 0, k == 7,
                            [wp.r(k)], [pb.r()])
                self.P.op("act", lambda e, pb=pb, t8=t8: e.copy(out=hp[:, 16 + t8 * 512:16 + (t8 + 1) * 512], in_=pb[:]),
                          reads=[pb.r()], writes=[hp.r()])
            cur, sh, i = hp, 1, 0
            while sh < w:
                nxt = (A, B)[i % 2]
                self.v("pool", "tensor_add", [cur.r()], [nxt.r()], out=nxt[:, 16:16 + S], in0=cur[:, 16:16 + S],
                       in1=cur[:, 16 - sh:16 - sh + S])
                cur, sh, i = nxt, sh * 2, i + 1
            pstate[gi] = cur

        def pool_b(gi):
            w = 2 << gi
            cur = pstate[gi]
            self.v("dve", "scalar_tensor_tensor", [cur.r(), hp.r()], [pbf.r()], out=pbf[:], in0=cur[:, 16:16 + S],
                   scalar=1.0 / w, in1=hp[:, 16:16 + S], op0=ALU.mult, op1=ALU.subtract)
            self.v("dve", "tensor_tensor", [cur.r(), invc.r()], [t16.r()], out=t16[:], in0=cur[:, 16:32], in1=invc[:, gi * 16:(gi + 1) * 16], op=ALU.mult)
            self.v("dve", "tensor_tensor", [t16.r(), hp.r(), pbf.r()], [pbf.r()], out=pbf[:, 0:16], in0=t16[:], in1=hp[:, 16:32],
                   op=ALU.subtract)
            for t8 in range(8):
                (pb,) = self.ps(1)
                self.mm(pb[:], pw[:, gi, :], pbf[:, t8 * 512:(t8 + 1) * 512], True, True, [pw.r(), pbf.r()], [pb.r()])
                self.v("dve", "tensor_scalar", [pb.r(), psc.r()], [ao.r()], out=ao[:, t8 * 512:(t8 + 1) * 512], in0=pb[:],
                       scalar1=psc[:, gi:gi + 1], scalar2=None, op0=ALU.mult)
            self.dma("sp", d["bout"][0, gi * 128:(gi + 1) * 128, :], ao[:], reads=[ao.r()], writes=[self.dr("bout", (0, gi))])

        wz = self.load_w("wz", d["w_in"][l][:, OFF_SGU:OFF_SGU + 1024], 8, 1024)
        wraw = self.alloc("wraw", [4, 128], F32)
        self.dma("sp", wraw[:], d["wsT"][l].rearrange("g s t -> s g t"), writes=[wraw.r()])
        tril = self.alloc("tril", [128], F32)
        self.dma("sp", tril[:], d["c_tril"], writes=[tril.r()])
        wsb = self.alloc("wsb", [4, 128], BF16)
        for g in range(4):
            self.v("dve", "tensor_tensor", [wraw.r(), tril.r()], [wsb.r()], out=wsb[:, g, :], in0=wraw[:, g, :], in1=tril[:], op=ALU.mult)
        lngc = self.alloc("lngc", [4], F32)
        lnbc = self.alloc("lnbc", [4], F32)
        self.dma("sp", lngc[:], d["sgu_ln_g_c"][l], writes=[lngc.r()])
        self.dma("sp", lnbc[:], d["sgu_ln_b_c"][l], writes=[lnbc.r()])
        bsb = self.load_bcast("bsb", d["b_s_r"][l:l + 1, :], 512)
        bias2 = self.alloc("bias2", [4, 128], F32)
        (prs,) = self.ps(1)
        for g in range(4):
            self.mm(prs[:, g * 128:(g + 1) * 128], self.ones[:], wsb[:, g, :], True, True, [self.ones.r(), wsb.r()], [prs.r()])
        for g in range(4):
            self.v("dve", "scalar_tensor_tensor", [prs.r(), lnbc.r(), bsb.r()], [bias2.r()], out=bias2[:, g, :], in0=prs[:, g * 128:(g + 1) * 128],
                   scalar=lnbc[:, g:g + 1], in1=bsb[:, g * 128:(g + 1) * 128], op0=ALU.mult, op1=ALU.add)
        uT = [self.alloc("uT", [4, 512], BF16) for _ in range(2)]
        dT = [self.alloc("dT", [4, 512], BF16) for _ in range(2)]
        tm = [self.alloc("tm", [4, 128], F32) for _ in range(2)]
        NR = 3
        vg = [self.alloc("vg", [512], F32) for _ in range(NR)]
        vf = [self.alloc("vf", [512], BF16) for _ in range(NR)]
        st6 = [self.alloc("st6", [12], F32) for _ in range(NR)]

        def sgu_u(t8):
            u_ = uT[t8 % 2]
            for c in range(4):
                (pb,) = self.ps(1)
                for k in range(8):
                    self.mm(pb[:], wz[:, k, c * 128:(c + 1) * 128], hT[:, k, t8 * 512:(t8 + 1) * 512], k == 0, k == 7, [wz.r(k)], [pb.r()])
                self.act(u_[:, c, :], pb[:], AF.Gelu, [pb.r()], [u_.r()])

        def sgu_s1(t):
            tok0 = t * 128
            vg_, vf_, s6 = vg[t % NR], vf[t % NR], st6[t % NR]
            (pb,) = self.ps(1)
            for k in range(8):
                self.mm(pb[:], hT[:, k, tok0:tok0 + 128], wz[:, k, 512:1024], k == 0, k == 7, [wz.r(k)], [pb.r()])
            self.act(vg_[:], pb[:], AF.Gelu, [pb.r()], [vg_.r()])
            self.v("dve", "bn_stats", [vg_.r()], [s6.r()], out=s6[:, 0:6], in_=vg_[:])
            self.v("dve", "bn_aggr", [s6.r()], [s6.r()], out=s6[:, 6:8], in_=s6[:, 0:6])
            rstd, rr = self.rstd_from_ss(s6[:, 7:8], s6.r(), 1)
            self.v("dve", "scalar_tensor_tensor", [s6.r(), rr], [s6.r()], out=s6[:, 8:9], in0=s6[:, 6:7], scalar=-1.0, in1=rstd,
                   op0=ALU.mult, op1=ALU.mult)
            self.act(vf_[:], vg_[:], AF.Identity, [vg_.r(), s6.r(), rr], [vf_.r()], bias=s6[:, 8:9], scale=rstd)

        def sgu_s2(t):
            t8, t4 = t // 4, t % 4
            u_ = uT[t8 % 2]
            d_ = dT[t8 % 2]
            vf_ = vf[t % NR]
            tm_ = tm[t % 2]
            (pb2,) = self.ps(1)
            for c in range(4):
                self.mm(pb2[:, c * 128:(c + 1) * 128], vf_[:, c * 128:(c + 1) * 128], wsb[:, c, :], True, True,
                        [vf_.r(), wsb.r()], [pb2.r()])
            for c in range(4):
                self.v("dve", "scalar_tensor_tensor", [pb2.r(), lngc.r(), bias2.r()], [tm_.r()], out=tm_[:, c, :], in0=pb2[:, c * 128:(c + 1) * 128],
                       scalar=lngc[:, c:c + 1], in1=bias2[:, c, :], op0=ALU.mult, op1=ALU.add)
            self.v("dve", "tensor_tensor", [tm_.r(), u_.r()], [d_.r()], out=d_[:, :, t4 * 128:(t4 + 1) * 128],
                   in0=tm_[:], in1=u_[:, :, t4 * 128:(t4 + 1) * 128], op=ALU.mult)
            if t4 == 3:
                self.dma("sp", d["bout"][3, :, t8 * 512:(t8 + 1) * 512].rearrange("(c p) t -> p c t", p=128), d_[:],
                         reads=[d_.r()], writes=[self.dr("bout", (3, t8))])

        LA = 2
        emitted_u = set()

        def need_u(t):
            t8 = t // 4
            if t8 not in emitted_u:
                emitted_u.add(t8)
                sgu_u(t8)

        def sgu_step(t8):
            for t in range(t8 * 4, t8 * 4 + 4):
                if t + LA < 32:
                    need_u(t + LA)
                    sgu_s1(t + LA)
                sgu_s2(t)

        need_u(0)
        for t in range(LA):
            sgu_s1(t)

        pool_a(0)
        for step in range(8):
            sgu_step(step)
            gi = step // 2
            if step % 2 == 0:
                pool_b(gi)
            elif gi + 1 < 4:
                pool_a(gi + 1)

    def mix_att(self, l, hT):
        d = self.d
        amask = self.alloc("amask", [2, 256], BF16)
        self.dma("pool", amask[:], d["c_att_mask"], writes=[amask.r()])
        onesAB = self.alloc("onesAB", [2, 128], BF16)
        self.v("dve", "memset", [], [onesAB.r()], onesAB[:], 0.0)
        self.v("dve", "memset", [], [onesAB.r()], onesAB[:, 0, 0:64], 1.0)
        self.v("dve", "memset", [], [onesAB.r()], onesAB[:, 1, 64:128], 1.0)
        wq = self.alloc("wq", [8, 128], BF16)
        wk = self.alloc("wk", [8, 128], BF16)
        wv = self.alloc("wv", [8, 128], BF16)
        qT = self.alloc("qT", [S], BF16)
        kTA = self.alloc("kTA", [S], BF16)
        kTB = self.alloc("kTB", [S], BF16)
        vT = self.alloc("vT", [S], BF16)
        self.v("dve", "memset", [], [kTA.r()], kTA[64:128, :], 0.0)
        self.v("dve", "memset", [], [kTB.r()], kTB[0:64, :], 0.0)
        vpad = self.alloc("vpad", [32, 2, 128], BF16)
        self.v("pool", "memset", [], [vpad.r()], vpad[:], 0.0)
        acc = self.alloc("acc", [2, S], F32)
        braw = self.alloc("braw", [2, 256], F32)
        ebias = self.alloc("ebias", [2, 256], BF16)
        E = [self.alloc("E", [2, 256], BF16) for _ in range(2)]
        PT = [self.alloc("PT", [2, 256], BF16) for _ in range(4)]
        bo = self.alloc("bo", [S], BF16)
        wsets = [(wq, wk, wv), (self.alloc("wq2", [8, 128], BF16), self.alloc("wk2", [8, 128], BF16), self.alloc("wv2", [8, 128], BF16))]
        braw2 = [braw, self.alloc("braw2", [2, 256], F32)]
        ebias2 = [ebias, self.alloc("ebias2", [2, 256], BF16)]
        E = E + [self.alloc("E3", [2, 256], BF16)]
        units = [(c, g) for c in range(4) for g in range(3)]

        def load_unit(u):
            c, g = units[u]
            wts = wsets[u % 2]
            for wi in range(3):
                col = OFF_ATT + wi * 1536 + g * 512 + c * 128
                self.dma("pool", wts[wi][:, :, 0:128], d["w_in"][l][:, col:col + 128].rearrange("(k p) n -> p k n", p=128),
                         writes=[wts[wi].r()])
            br_, eb_ = braw2[u % 2], ebias2[u % 2]
            self.dma("sp", br_[:], d["att_bias"][g, c], writes=[br_.r()])
            self.act(br_[:], br_[:], AF.Exp, [br_.r()], [br_.r()])
            self.v("dve", "tensor_tensor", [br_.r(), amask.r()], [eb_.r()], out=eb_[:], in0=br_[:], in1=amask[:], op=ALU.mult)

        load_unit(0)
        ipt = 0
        for u, (c, g) in enumerate(units):
            dil = DILS[g]
            L = S // dil
            nb = L // 128
            wts = wsets[u % 2]
            ebias_u = ebias2[u % 2]
            if u + 1 < len(units):
                load_unit(u + 1)
            for wi in range(3):
                for t8 in range(8):
                    (pb,) = self.ps(1)
                    for k in range(8):
                        self.mm(pb[:], wts[wi][:, k, 0:128], hT[:, k, t8 * 512:(t8 + 1) * 512], k == 0, k == 7, [wts[wi].r()], [pb.r()])
                    mw = 512 // dil
                    m0 = t8 * mw

                    def views(dst_, p0, p1):
                        ov = dst_[p0:p1, :].rearrange("p (r m) -> p r m", r=dil)[:, :, m0:m0 + mw]
                        iv = pb[p0:p1, :].rearrange("p (m r) -> p r m", r=dil)
                        return ov, iv
                    if wi == 0:
                        ov, iv = views(qT, 0, 128)
                        self.P.op("act", lambda e, ov=ov, iv=iv: e.mul(out=ov, in_=iv, mul=0.125), reads=[pb.r()], writes=[qT.r()])
                    elif wi == 1:
                        ov, iv = views(kTA, 0, 64)
                        self.v("dve", "tensor_copy", [pb.r()], [kTA.r()], out=ov, in_=iv)
                        ov, iv = views(kTB, 64, 128)
                        self.P.op("act", lambda e, ov=ov, iv=iv: e.copy(out=ov, in_=iv), reads=[pb.r()], writes=[kTB.r()])
                    else:
                        ov, iv = views(vT, 0, 128)
                        self.v("dve", "tensor_copy", [pb.r()], [vT.r()], out=ov, in_=iv)
            for b0 in range(0, 32, 8):
                (pb,) = self.ps(1)
                pv = pb[:].bitcast(BF16).rearrange("p (a b) -> p a b", a=8)
                for bi in range(8):
                    self.tr(pv[:, bi, :], vT[:, (b0 + bi) * 128:(b0 + bi + 1) * 128], self.ident[:], [vT.r(), self.ident.r()], [pb.r()])
                self.P.op("act", lambda e, pv=pv, b0=b0: e.copy(out=vpad[:, b0:b0 + 8, 0, 0:64], in_=pv[:, :, 0:64]), reads=[pb.r()], writes=[vpad.r()])
                self.v("dve", "tensor_copy", [pb.r()], [vpad.r()], out=vpad[:, b0:b0 + 8, 1, 64:128], in_=pv[:, :, 64:128])
            blocks = [(r, n) for r in range(dil) for n in range(nb)]
            ptof = {}

            def stage1(bi):
                nonlocal ipt
                r, n = blocks[bi]
                q0 = r * L + 128 * n
                nq = 256 if n + 1 < nb else 128
                (pb,) = self.ps(1)
                for hd in range(2):
                    kTh = (kTA, kTB)[hd]
                    self.mm(pb[:, hd * 256:hd * 256 + nq], kTh[:, q0:q0 + 128], qT[:, q0:q0 + nq], True, True, [kTh.r(), qT.r()], [pb.r()])
                e_ = E[ipt % 3]
                pt = PT[bi % 4]
                ipt += 1
                pv3 = pb[:].rearrange("p (h j) -> p h j", h=2)
                self.act(e_[:, :, 0:nq], pv3[:, :, 0:nq], AF.Exp, [pb.r()], [e_.r()])
                eng = "pool" if ipt % 2 == 0 else "dve"
                self.v(eng, "tensor_tensor", [e_.r(), ebias_u.r()], [pt.r()], out=pt[:, :, 0:nq], in0=e_[:, :, 0:nq],
                       in1=ebias_u[:, :, 0:nq], op=ALU.mult)
                ptof[bi] = pt

            def stage2(bi):
                r, n = blocks[bi]
                blk = r * nb + n
                pt = ptof[bi]
                (po,) = self.ps(1)
                srcs = [(blk, pt, 0)]
                if n > 0:
                    srcs.append((blk - 1, ptof[bi - 1], 128))
                nmm = 2 * len(srcs)
                for which in range(2):
                    i = 0
                    for (bk, ptile, j0) in srcs:
                        for hd in range(2):
                            lhs = vpad[:, bk, hd, :] if which == 0 else onesAB[:, hd, :]
                            self.mm(po[:, which * 128:(which + 1) * 128], lhs, ptile[:, hd, j0:j0 + 128], i == 0, i == nmm - 1,
                                    [vpad.r(), onesAB.r(), ptile.r()], [po.r()])
                            i += 1
                av = acc[:].rearrange("p a (m r) -> p a m r", r=dil)[:, :, 128 * n:128 * (n + 1), r]
                pov = po[:, 0:256].rearrange("p (a q) -> p a q", a=2)
                if g == 0:
                    self.P.op("act", lambda e, av=av, pov=pov: e.copy(out=av, in_=pov), reads=[po.r()], writes=[acc.r()])
                else:
                    self.v("dve", "tensor_tensor", [po.r(), acc.r()], [acc.r()], out=av, in0=pov, in1=av, op=ALU.add)

            LA = 2
            for bi in range(min(LA, len(blocks))):
                stage1(bi)
            for bi in range(len(blocks)):
                if bi + LA < len(blocks):
                    stage1(bi + LA)
                stage2(bi)
            if g == 2:
                self.act(acc[:, 1, :], acc[:, 1, :], AF.Ln, [acc.r()], [acc.r()])
                self.act(acc[:, 1, :], acc[:, 1, :], AF.Exp, [acc.r()], [acc.r()], scale=-1.0)
                self.v("pool", "tensor_tensor", [acc.r()], [bo.r()], out=bo[:], in0=acc[:, 0, :], in1=acc[:, 1, :], op=ALU.mult)
                self.dma("sp", d["bout"][1, c * 128:(c + 1) * 128, :], bo[:], reads=[bo.r()], writes=[self.dr("bout", (1, c))])

    def mix_s5(self, l, hT):
        d = self.d
        PI_ = float(np.pi)
        tabr = Res("s5tab")

        def T32(name, n=1):
            return self.alloc(name, [n, 32], F32) if n > 1 else self.alloc(name, [32], F32)

        def tt(out, a, b, op, eng="dve"):
            self.v(eng, "tensor_tensor", [tabr], [tabr], out=out, in0=a, in1=b, op=op)

        def ts(out, a, s1, op0, s2=None, op1=None, eng="dve"):
            kw = dict(out=out, in0=a, scalar1=s1, scalar2=s2, op0=op0)
            if op1 is not None:
                kw["op1"] = op1
            self.v(eng, "tensor_scalar", [tabr], [tabr], **kw)

        def actf(out, in_, func):
            self.act(out, in_, func, [tabr], [tabr])

        ldt, are, aim = T32("ldt"), T32("are"), T32("aim")
        self.dma("sp", ldt[:], d["s5_ldt"][l], writes=[tabr])
        self.dma("sp", are[:], d["s5_are"][l], writes=[tabr])
        self.dma("sp", aim[:], d["s5_aim"][l], writes=[tabr])
        sgn = self.alloc("sgn", [4], F32)
        self.dma("sp", sgn[:], d["c_sgn"], writes=[tabr])
        B1 = self.alloc("B1", [32, 16], F32)
        B2 = self.alloc("B2", [32, 16], F32)
        C1 = self.alloc("C1", [32, 16], F32)
        C2 = self.alloc("C2", [32, 16], F32)
        for t_, n_ in ((B1, "s5_b1"), (B2, "s5_b2"), (C1, "s5_c1"), (C2, "s5_c2")):
            self.dma("sp", t_[:], d[n_][l], writes=[tabr])
        swap = self.alloc("swap", [128], F32)
        bdm = self.alloc("bdm", [128], F32)
        rowm = self.alloc("rowm", [8], F32)
        colm = self.alloc("colm", [8, 128], F32)
        dcol = self.alloc("dcol", [4], F32)
        for t_, n_ in ((swap, "c_swap"), (bdm, "c_bdmask"), (rowm, "c_rowmask"), (colm, "c_colmask")):
            self.dma("sp", t_[:], d[n_], writes=[tabr])
        self.dma("sp", dcol[:], d["d_skip_c"][l], writes=[tabr])
        dt, lre, x1, mag, ang = T32("dt"), T32("lre"), T32("x1"), T32("mag"), T32("ang")
        actf(dt[:], ldt[:], AF.Exp)
        ts(lre[:], are[:], -1e-4, ALU.min)
        tt(x1[:], lre[:], dt[:], ALU.mult)
        actf(mag[:], x1[:], AF.Exp)
        tt(ang[:], aim[:], dt[:], ALU.mult)
        z, y_, yf, r_, m_ = T32("z"), T32("y"), T32("yf"), T32("r"), T32("m")
        yi = self.alloc("yi", [32], I32)

        def sin_of(dst, shift):
            ts(z[:], ang[:], shift, ALU.add)
            ts(y_[:], z[:], 1.0 / (2 * PI_), ALU.mult)
            self.v("dve", "tensor_copy", [tabr], [tabr], out=yi[:], in_=y_[:])
            self.v("dve", "tensor_copy", [tabr], [tabr], out=yf[:], in_=yi[:])
            self.v("dve", "scalar_tensor_tensor", [tabr], [tabr], out=r_[:], in0=yf[:], scalar=-2 * PI_, in1=z[:], op0=ALU.mult, op1=ALU.add)
            ts(m_[:], r_[:], PI_, ALU.is_gt, -2 * PI_, ALU.mult)
            tt(r_[:], r_[:], m_[:], ALU.add)
            ts(m_[:], r_[:], -PI_, ALU.is_lt, 2 * PI_, ALU.mult)
            tt(r_[:], r_[:], m_[:], ALU.add)
            ts(r_[:], r_[:], -3.14159, ALU.max, 3.14159, ALU.min)
            actf(dst, r_[:], AF.Sin)

        cosv, sinv, ar, ai = T32("cosv"), T32("sinv"), T32("ar"), T32("ai")
        sin_of(cosv[:], PI_ / 2)
        sin_of(sinv[:], 0.0)
        tt(ar[:], mag[:], cosv[:], ALU.mult)
        tt(ai[:], mag[:], sinv[:], ALU.mult)
        PR, PIm = T32("PR", 9), T32("PI", 9)
        t1, t2 = T32("t1"), T32("t2")
        self.v("dve", "memset", [tabr], [tabr], PR[:, 0, :], 1.0)
        self.v("dve", "memset", [tabr], [tabr], PIm[:, 0, :], 0.0)
        for j in range(8):
            tt(t1[:], PR[:, j, :], ar[:], ALU.mult)
            tt(t2[:], PIm[:, j, :], ai[:], ALU.mult)
            tt(PR[:, j + 1, :], t1[:], t2[:], ALU.subtract)
            tt(t1[:], PR[:, j, :], ai[:], ALU.mult)
            tt(t2[:], PIm[:, j, :], ar[:], ALU.mult)
            tt(PIm[:, j + 1, :], t1[:], t2[:], ALU.add)
        QR, QI, QIs = T32("QR", 9), T32("QI", 9), T32("QIs", 9)
        self.v("dve", "tensor_copy", [tabr], [tabr], out=QR[:, 0, :], in_=PR[:, 8, :])
        self.v("dve", "tensor_copy", [tabr], [tabr], out=QI[:, 0, :], in_=PIm[:, 8, :])
        for i in range(8):
            tt(t1[:], QR[:, i, :], QR[:, i, :], ALU.mult)
            tt(t2[:], QI[:, i, :], QI[:, i, :], ALU.mult)
            tt(QR[:, i + 1, :], t1[:], t2[:], ALU.subtract)
            tt(t1[:], QR[:, i, :], QI[:, i, :], ALU.mult)
            ts(QI[:, i + 1, :], t1[:], 2.0, ALU.mult)
        ts(QIs[:], QI[:], sgn[:, 1:2], ALU.mult)
        PIs1, TA, TB = T32("PIs1", 9), T32("TA", 9), T32("TB", 9)
        ts(PIs1[:], PIm[:], sgn[:, 0:1], ALU.mult)
        ts(TA[:], PR[:], sgn[:, 1:2], ALU.mult)
        ts(TB[:], PIm[:], -1.0, ALU.mult)
        den, am1, fr, fi, FIs, FIs2 = T32("den"), T32("am1"), T32("fr"), T32("fi"), T32("FIs"), T32("FIs2")
        tt(t1[:], lre[:], lre[:], ALU.mult)
        tt(t2[:], aim[:], aim[:], ALU.mult)
        tt(den[:], t1[:], t2[:], ALU.add)
        self.v("dve", "reciprocal", [tabr], [tabr], out=den[:], in_=den[:])
        ts(am1[:], ar[:], -1.0, ALU.add)
        tt(t1[:], am1[:], lre[:], ALU.mult)
        tt(t2[:], ai[:], aim[:], ALU.mult)
        tt(t1[:], t1[:], t2[:], ALU.add)
        tt(fr[:], t1[:], den[:], ALU.mult)
        tt(t1[:], ai[:], lre[:], ALU.mult)
        tt(t2[:], am1[:], aim[:], ALU.mult)
        tt(t1[:], t1[:], t2[:], ALU.subtract)
        tt(fi[:], t1[:], den[:], ALU.mult)
        ts(FIs[:], fi[:], sgn[:, 0:1], ALU.mult)
        ts(FIs2[:], fi[:], sgn[:, 1:2], ALU.mult)

        def bc(tab_ap_q):
            return tab_ap_q.unsqueeze(2).to_broadcast([128, 8, 16])

        bs = self.alloc("bs", [8, 16], F32)
        bw = self.alloc("bw", [8, 16], F32)
        tf1 = self.alloc("tf1", [8, 16], F32)
        tf2 = self.alloc("tf2", [8, 16], F32)
        XB = self.alloc("XB", [8, 128], BF16)
        CP = self.alloc("CP", [8, 128], BF16)
        Cst = self.alloc("Cst", [128], BF16)
        XA = self.alloc("XA", [8, 128], BF16)
        BD = self.alloc("BD", [8, 128], BF16)
        wu = self.alloc("wu5", [8, 128], BF16)
        uq = self.alloc("uq", [S], BF16)
        gq = self.alloc("gq", [S], BF16)
        GS = [self.alloc("GS", [8, 128], BF16) for _ in range(4)]
        AM = [self.alloc("AM", [9, 128], BF16) for _ in range(4)]
        CPm = [self.alloc("CPm", [8, 128], BF16) for _ in range(8)]
        H = [[self.alloc("H", [512], BF16) for _ in range(2)] for _ in range(4)]
        HP = self.alloc("HP", [8, 512], BF16)
        self.v("pool", "memset", [], [HP.r(g8) for g8 in range(8)], HP[:], 0.0)
        ytmp = [self.alloc("ytmp", [512], F32) for _ in range(2)]
        amt9 = [[self.alloc("amt9", [9, 128], BF16) for _ in range(2)] for _ in range(2)]
        swapb = self.alloc("swapb", [128], BF16)
        self.v("dve", "tensor_copy", [tabr], [swapb.r()], out=swapb[:], in_=swap[:])
        colmb = self.alloc("colmb", [8, 128], BF16)
        self.v("dve", "tensor_copy", [tabr], [colmb.r()], out=colmb[:], in_=colm[:])
        for q in range(4):
            gsl = slice(q * 8, (q + 1) * 8)
            tt(bs[:], B1[:, gsl, :], bc(fr[:, gsl]), ALU.mult)
            tt(tf1[:], B2[:, gsl, :], bc(FIs[:, gsl]), ALU.mult)
            tt(bs[:], bs[:], tf1[:], ALU.add)
            tt(bw[:], B2[:, gsl, :], bc(fr[:, gsl]), ALU.mult)
            tt(tf1[:], B1[:, gsl, :], bc(FIs2[:, gsl]), ALU.mult)
            tt(bw[:], bw[:], tf1[:], ALU.add)
            xbr, cpr = XB.r(), CP.r()
            for j in range(8):
                tt(tf1[:], bs[:], bc(PR[:, j, gsl]), ALU.mult)
                tt(tf2[:], bw[:], bc(PIs1[:, j, gsl]), ALU.mult)
                self.v("dve", "tensor_tensor", [tabr], [tabr, xbr], out=XB[:, j, :].rearrange("p (g c) -> p g c", g=8), in0=tf1[:], in1=tf2[:], op=ALU.add)
            for t in range(8):
                tt(tf1[:], C1[:, gsl, :], bc(TA[:, t + 1, gsl]), ALU.mult)
                tt(tf2[:], C2[:, gsl, :], bc(TB[:, t + 1, gsl]), ALU.mult)
                self.v("dve", "tensor_tensor", [tabr], [tabr, cpr], out=CP[:, t, :].rearrange("p (g c) -> p g c", g=8), in0=tf1[:], in1=tf2[:], op=ALU.add)
            self.v("dve", "tensor_scalar", [tabr], [tabr, Cst.r()], out=Cst[:].rearrange("p (g c) -> p g c", g=8), in0=C1[:, gsl, :],
                   scalar1=sgn[:, 1:2], scalar2=None, op0=ALU.mult)
            (pbx,) = self.ps(1)
            pxv = pbx[:].bitcast(BF16).rearrange("p (a b) -> p a b", a=8)
            for s_ in range(8):
                self.tr(pxv[:, s_, :], XB[:, 7 - s_, :], self.ident[:], [xbr, self.ident.r()], [pbx.r()])
            self.P.op("act", lambda e, pxv=pxv: e.copy(out=XA[:], in_=pxv), reads=[pbx.r()], writes=[XA.r()])
            for jh in range(2):
                (pbd,) = self.ps(1)
                for jj in range(4):
                    self.mm(pbd[:, jj * 128:(jj + 1) * 128], XB[:, jh * 4 + jj, :], Cst[:], True, True, [xbr, Cst.r()], [pbd.r()])
                self.v("dve", "tensor_tensor", [pbd.r(), tabr], [BD.r()], out=BD[:, jh * 4:(jh + 1) * 4, :],
                       in0=pbd[:].rearrange("p (j c) -> p j c", j=4), in1=bdm[:].unsqueeze(1).to_broadcast([128, 4, 128]), op=ALU.mult)
            col = OFF_SSM + q * 128
            self.dma("pool", wu[:], d["w_in"][l][:, col:col + 128].rearrange("(k p) n -> p k n", p=128), writes=[wu.r()])
            for t8 in range(8):
                (pb,) = self.ps(1)
                for k in range(8):
                    self.mm(pb[:], wu[:, k, :], hT[:, k, t8 * 512:(t8 + 1) * 512], k == 0, k == 7, [wu.r()], [pb.r()])
                self.P.op("act", lambda e, pb=pb, t8=t8: e.copy(out=uq[:, t8 * 512:(t8 + 1) * 512], in_=pb[:]), reads=[pb.r()], writes=[uq.r()])
            uq8 = uq[:].rearrange("p (k s) -> p s k", s=8)
            for st4 in range(2):
                grp = [st4 * 4 + i for i in range(4)]
                for gi, g8 in enumerate(grp):
                    g = q * 8 + g8
                    self.P.op("act", lambda e, gi=gi, g8=g8: e.activation(out=GS[gi][:], in_=XA[:], func=AF.Copy, scale=rowm[:, g8:g8 + 1]),
                              reads=[XA.r(), tabr], writes=[GS[gi].r()])
                    self.v("dve", "tensor_tensor", [cpr, colmb.r()], [CPm[g8].r()], out=CPm[g8][:], in0=CP[:],
                           in1=colmb[:, g8, :].unsqueeze(1).to_broadcast([128, 8, 128]), op=ALU.mult)
                    am_a = amt9[gi % 2][0]
                    am_b = amt9[gi % 2][1]
                    for i in range(9):
                        self.P.op("act", lambda e, am_a=am_a, i=i, g=g: e.activation(out=am_a[:, i, :], in_=self.ident[:], func=AF.Copy, scale=QR[:, i, g:g + 1]),
                                  reads=[tabr, self.ident.r()], writes=[am_a.r()])
                        self.P.op("act", lambda e, am_b=am_b, i=i, g=g: e.activation(out=am_b[:, i, :], in_=swapb[:], func=AF.Copy, scale=QIs[:, i, g:g + 1]),
                                  reads=[tabr, swapb.r()], writes=[am_b.r()])
                    self.v("dve", "tensor_tensor", [am_a.r(), am_b.r()], [AM[gi].r()], out=AM[gi][:], in0=am_a[:], in1=am_b[:], op=ALU.add)
                cur = {}
                for gi, g8 in enumerate(grp):
                    (pb,) = self.ps(1)
                    for s_ in range(8):
                        self.mm(pb[:], GS[gi][:, s_, :], uq8[:, s_, :], s_ == 0, s_ == 7, [GS[gi].r(), uq.r()], [pb.r()])
                    self.P.op("act", lambda e, pb=pb, gi=gi: e.copy(out=H[gi][0][:], in_=pb[:]), reads=[pb.r()], writes=[H[gi][0].r()])
                    cur[gi] = 0
                for i in range(9):
                    dd = 1 << i
                    for gi, g8 in enumerate(grp):
                        hc = H[gi][cur[gi]]
                        (pb,) = self.ps(1)
                        if i < 8:
                            self.mm(pb[:, dd:512], AM[gi][:, i, :], hc[:, 0:512 - dd], True, True, [AM[gi].r(), hc.r()], [pb.r()])
                            self.v("dve", "tensor_tensor", [pb.r(), hc.r()], [hc.r()], out=hc[:, dd:512], in0=pb[:, dd:512], in1=hc[:, dd:512], op=ALU.add)
                            continue
                        self.mm(pb[:, 0:dd], self.ident[:], hc[:, 0:dd], True, True, [self.ident.r(), hc.r()], [pb.r()])
                        self.mm(pb[:, dd:512], self.ident[:], hc[:, dd:512], True, False, [self.ident.r(), hc.r()], [pb.r()])
                        self.mm(pb[:, dd:512], AM[gi][:, i, :], hc[:, 0:512 - dd], False, True, [AM[gi].r(), hc.r()], [pb.r()])
                        if i < 8:
                            hn = H[gi][1 - cur[gi]]
                            self.P.op("act", lambda e, pb=pb, hn=hn: e.copy(out=hn[:], in_=pb[:]), reads=[pb.r()], writes=[hn.r()])
                            cur[gi] = 1 - cur[gi]
                        else:
                            self.v("dve", "tensor_copy", [pb.r()], [HP.r(g8)], out=HP[:, g8, 1:512], in_=pb[:, 0:511])
            for t in range(8):
                (py,) = self.ps(1)
                n_mm = (t + 1) + 8
                i_mm = 0
                for s_ in range(t + 1):
                    self.mm(py[:], BD[:, t - s_, :], uq8[:, s_, :], i_mm == 0, i_mm == n_mm - 1, [BD.r(), uq.r()], [py.r()])
                    i_mm += 1
                for g8 in range(8):
                    self.mm(py[:], CPm[g8][:, t, :], HP[:, g8, :], i_mm == 0, i_mm == n_mm - 1, [CPm[g8].r(), HP.r(g8)], [py.r()])
                    i_mm += 1
                yt_ = ytmp[t % 2]
                self.v("dve", "scalar_tensor_tensor", [uq.r(), py.r(), tabr], [yt_.r()], out=yt_[:], in0=uq8[:, t, :], scalar=dcol[:, q:q + 1],
                       in1=py[:], op0=ALU.mult, op1=ALU.add)
                self.act(gq[:].rearrange("p (k s) -> p s k", s=8)[:, t, :], yt_[:], AF.Gelu, [yt_.r()], [gq.r()])
            self.dma("sp", d["gsc"][q * 128:(q + 1) * 128, :], gq[:], reads=[gq.r()], writes=[self.dr("gsc", q)])

    def mix_s5_glu(self, l):
        d = self.d
        wgl = self.load_w("wglu", d["w_glu"][l], 4, 512)
        bgl = self.alloc("bgl", [4], F32)
        self.dma("sp", bgl[:], d["b_glu_c"][l], writes=[bgl.r()])
        gT = [self.alloc("gTt", [4, 512], BF16) for _ in range(2)]
        co = [self.alloc("co", [4, 512], BF16) for _ in range(2)]
        sg = [self.alloc("sg5", [512], F32) for _ in range(2)]
        for t8 in range(8):
            g_ = gT[t8 % 2]
            c_ = co[t8 % 2]
            self.dma("sp", g_[:], d["gsc"][:, t8 * 512:(t8 + 1) * 512].rearrange("(c p) t -> p c t", p=128),
                     reads=[self.dr("gsc", q) for q in range(4)], writes=[g_.r()])
            for oc in range(4):
                s_ = sg[oc % 2]
                (pz,) = self.ps(1)
                for kc in range(4):
                    self.mm(pz[:], wgl[:, kc, oc * 128:(oc + 1) * 128], g_[:, kc, :], kc == 0, kc == 3, [wgl.r(kc), g_.r()], [pz.r()])
                self.act(s_[:], pz[:], AF.Sigmoid, [pz.r(), bgl.r()], [s_.r()], bias=bgl[:, oc:oc + 1])
                eng = "pool" if oc % 2 == 0 else "dve"
                self.v(eng, "tensor_tensor", [s_.r(), g_.r()], [c_.r()], out=c_[:, oc, :], in0=s_[:], in1=g_[:, oc, :], op=ALU.mult)
            self.dma("sp", d["bout"][2, :, t8 * 512:(t8 + 1) * 512].rearrange("(c p) t -> p c t", p=128), c_[:],
                     reads=[c_.r()], writes=[self.dr("bout", (2, t8))])

    def mix_merge_a(self, l, hT, h2):
        d = self.d
        wg = self.alloc("wg", [8, 4, 512], BF16)
        for i in range(4):
            col = OFF_GATE + i * 1024 + h2 * 512
            self.dma("pool", wg[:, :, i, :], d["w_in"][l][:, col:col + 512].rearrange("(k p) n -> p k n", p=128), writes=[wg.r(i)])
        wu = self.alloc("wu", [16, 512], BF16)
        for i in range(4):
            self.dma("pool", wu[:, i * 4:(i + 1) * 4, :], d["w_up"][l, i][:, h2 * 512:(h2 + 1) * 512].rearrange("(k p) n -> p k n", p=128),
                     writes=[wu.r(i)])
        gb = self.alloc("gb", [32], F32)
        self.dma("sp", gb[:], d["gate_b_c"][l], writes=[gb.r()])
        brT = [self.alloc("brT", [16, 512], BF16) for _ in range(2)]
        mT = [self.alloc("mT", [4, 512], BF16) for _ in range(2)]
        sg = [self.alloc("sg", [512], F32) for _ in range(2)]
        pr = [self.alloc("pr", [512], F32) for _ in range(2)]
        ac = [self.alloc("ac", [512], F32) for _ in range(2)]
        isg = 0
        for t8 in range(8):
            br = brT[t8 % 2]
            m_ = mT[t8 % 2]
            for i in range(4):
                self.dma("sp", br[:, i * 4:(i + 1) * 4, :], d["bout"][i, :, t8 * 512:(t8 + 1) * 512].rearrange("(c p) t -> p c t", p=128),
                         reads=[self.dr("bout", (i, x)) for x in range(8)], writes=[br.r(i)])
            for dcl in range(4):
                dc = h2 * 4 + dcl
                a_ = ac[dcl % 2]
                for i in range(4):
                    s_ = sg[isg % 2]
                    p_ = pr[isg % 2]
                    isg += 1
                    (pg,) = self.ps(1)
                    for k in range(8):
                        self.mm(pg[:], wg[:, k, i, dcl * 128:(dcl + 1) * 128], hT[:, k, t8 * 512:(t8 + 1) * 512], k == 0, k == 7, [wg.r(i)], [pg.r()])
                    self.act(s_[:], pg[:], AF.Sigmoid, [pg.r(), gb.r()], [s_.r()], bias=gb[:, i * 8 + dc:i * 8 + dc + 1])
                    (pu,) = self.ps(1)
                    for kc in range(4):
                        self.mm(pu[:], wu[:, i * 4 + kc, dcl * 128:(dcl + 1) * 128], br[:, i * 4 + kc, :], kc == 0, kc == 3, [wu.r(i), br.r(i)], [pu.r()])
                    if i == 0:
                        self.v("dve", "tensor_tensor", [pu.r(), s_.r()], [a_.r()], out=a_[:], in0=pu[:], in1=s_[:], op=ALU.mult)
                    else:
                        self.v("dve", "tensor_tensor", [pu.r(), s_.r()], [p_.r()], out=p_[:], in0=pu[:], in1=s_[:], op=ALU.mult)
                        if i < 3:
                            self.v("dve", "tensor_tensor", [p_.r(), a_.r()], [a_.r()], out=a_[:], in0=p_[:], in1=a_[:], op=ALU.add)
                        else:
                            self.v("dve", "tensor_tensor", [p_.r(), a_.r()], [m_.r()], out=m_[:, dcl, :], in0=p_[:], in1=a_[:], op=ALU.add)
            self.dma("sp", d["mrg"][h2 * 512:(h2 + 1) * 512, t8 * 512:(t8 + 1) * 512].rearrange("(c p) t -> p c t", p=128), m_[:],
                     reads=[m_.r()], writes=[self.dr("mrg", (h2, t8))])

    def mix_merge_b(self, l, src, dst):
        d = self.d
        wo = self.load_w("wout", d["w_out"][l], 8, D)
        gpost = self.load_bcast("gpost", d["g_mix_post"][l:l + 1, :], D)
        mT = [self.alloc("mTb", [8, 512], BF16) for _ in range(2)]
        xt = [self.alloc("xtb", [D], F32) for _ in range(2)]
        for t8 in range(8):
            m_ = mT[t8 % 2]
            self.dma("sp", m_[:], d["mrg"][:, t8 * 512:(t8 + 1) * 512].rearrange("(c p) t -> p c t", p=128),
                     reads=[self.dr("mrg", (0, t8)), self.dr("mrg", (1, t8))], writes=[m_.r()])
            for t4 in range(4):
                tt = t8 * 4 + t4
                x_ = xt[tt % 2]
                self.dma("sp", x_[:], d[src][tt * 128:(tt + 1) * 128, :], reads=[self.dr(src, tt)], writes=[x_.r()])
                pbs = self.ps(2)
                for half in range(2):
                    for dc in range(8):
                        self.mm(pbs[half][:], m_[:, dc, t4 * 128:(t4 + 1) * 128], wo[:, dc, half * 512:(half + 1) * 512], dc == 0, dc == 7,
                                [m_.r(), wo.r(dc)], [pbs[half].r()])
                self.post_tile(pbs, x_[:], x_.r(), gpost, d[dst][tt * 128:(tt + 1) * 128, :], self.dr(dst, tt), self.junk)

W_SHAPES = {
    "g_mix_pre": (DEPTH, D), "g_mix_post": (DEPTH, D), "w_in": (DEPTH, D, IN_WIDTH), "gate_b": (DEPTH, 4, D),
    "w_up": (DEPTH, 4, 512, D), "w_out": (DEPTH, D, D),
    "g_x_pre": (DEPTH, D), "g_x_post": (DEPTH, D), "g_mem": (DEPTH, D),
    "w_cq": (DEPTH, D, 512), "w_ckv": (DEPTH, D, 1024), "w_co": (DEPTH, 512, D),
    "g_ff_pre": (DEPTH, D), "g_ff_post": (DEPTH, D), "w_ff1": (DEPTH, D, 4096), "w_ff2": (DEPTH, 4096, D),
}
W_SHAPES.update({
    "pool_w": (DEPTH, 4, 128, 128), "pool_scale_c": (DEPTH, 128, 4), "wsT": (DEPTH, 4, 128, 128),
    "sgu_ln_g_c": (DEPTH, 128, 4), "sgu_ln_b_c": (DEPTH, 128, 4), "b_s_r": (DEPTH, 512), "gate_b_c": (DEPTH, 128, 32),
    "att_bias": (3, 4, 128, 2, 256),
    "s5_are": (DEPTH, 128, 32), "s5_aim": (DEPTH, 128, 32), "s5_ldt": (DEPTH, 128, 32),
    "s5_b1": (DEPTH, 128, 32, 16), "s5_b2": (DEPTH, 128, 32, 16), "s5_c1": (DEPTH, 128, 32, 16), "s5_c2": (DEPTH, 128, 32, 16),
    "d_skip_c": (DEPTH, 128, 4), "b_glu_c": (DEPTH, 128, 4), "w_glu": (DEPTH, 512, 512),
})
CONSTS = {"c_ident": (128, 128), "c_invc": (1, 64), "c_tril": (128, 128), "c_att_mask": (128, 2, 256),
          "c_sgn": (128, 4), "c_swap": (128, 128), "c_bdmask": (128, 128), "c_rowmask": (128, 8), "c_colmask": (128, 8, 128)}
DEBUG_BOUT = False


def build_program(plan):
    nc = bass.Bass("TRN2", target_bir_lowering=False)
    dram = {}
    dram["x"] = nc.dram_tensor("x", [S, D], F32, kind="ExternalInput").ap()
    dram["mem"] = nc.dram_tensor("mem", [NMEM, D], F32, kind="ExternalInput").ap()
    for n, shp in W_SHAPES.items():
        dram[n] = nc.dram_tensor(n, list(shp), F32, kind="ExternalInput").ap()
    for n, shp in CONSTS.items():
        dram[n] = nc.dram_tensor(n, list(shp), F32, kind="ExternalInput").ap()
    dram["y"] = nc.dram_tensor("y", [S, D], F32, kind="ExternalOutput").ap()
    dram["xr"] = nc.dram_tensor("xr", [S, D], F32, kind="Internal").ap()
    dram["bout"] = nc.dram_tensor("bout", [4, 512, S], BF16, kind="ExternalOutput" if DEBUG_BOUT else "Internal").ap()
    dram["mrg"] = nc.dram_tensor("mrg", [D, S], BF16, kind="Internal").ap()
    dram["gsc"] = nc.dram_tensor("gsc", [512, S], BF16, kind="Internal").ap()
    with contextlib.ExitStack() as st:
        kb = KB(nc, st, dram)
        kb.setup_consts()
        kb.mark_perm()
        for i, item in enumerate(plan):
            kind, l = item[0], item[1]
            src = "x" if i == 0 else "xr"
            dst = "y" if i == len(plan) - 1 else "xr"
            getattr(kb, kind)(l, src, dst, *item[2:])
        kb.P.emit()
        nops = kb.P.nops
    return nc, nops


def _t5_bucket(n):
    exact = 16
    nf = np.maximum(n, 1).astype(np.float32)
    large = exact + (np.log(nf / exact) / np.log(2048 / exact) * (32 - exact)).astype(np.int32)
    large = np.minimum(large, 31)
    return np.where(n < exact, n, large).astype(np.int32)


def host_consts():
    c = {"c_ident": np.eye(128, dtype=np.float32)}
    invc = np.zeros((1, 64), np.float32)
    for gi in range(4):
        w = 2 << gi
        invc[0, gi * 16:(gi + 1) * 16] = 1.0 / np.minimum(np.arange(16) + 1, w)
    c["c_invc"] = invc
    s_ = np.arange(128)
    c["c_tril"] = (s_[:, None] <= s_[None, :]).astype(np.float32)
    dist = np.arange(256)[None, :] - np.arange(128)[:, None]
    m = ((dist >= 0) & (dist <= 128)).astype(np.float32)
    c["c_att_mask"] = np.ascontiguousarray(np.broadcast_to(m[:, None, :], (128, 2, 256)))
    sg = np.ones((128, 4), np.float32)
    sg[:64, 0] = -1.0
    sg[64:, 1] = -1.0
    c["c_sgn"] = sg
    p = np.arange(128)
    c["c_swap"] = (p[None, :] == ((p[:, None] + 64) % 128)).astype(np.float32)
    c["c_bdmask"] = ((p[:, None] // 16) == (p[None, :] // 16)).astype(np.float32)
    c["c_rowmask"] = ((p[:, None] // 16) == np.arange(8)[None, :]).astype(np.float32)
    c["c_colmask"] = np.ascontiguousarray(np.broadcast_to(((p[None, None, :] // 16) == np.arange(8)[None, :, None]), (128, 8, 128)).astype(np.float32))
    return c


def host_layout(inputs):
    f = lambda n: np.asarray(inputs[n], dtype=np.float32)
    o = {}
    for n in ("g_mix_pre", "g_mix_post", "w_in", "w_up", "w_out", "g_x_pre", "g_x_post", "g_mem", "w_cq", "w_ckv", "w_co",
              "g_ff_pre", "g_ff_post", "w_ff1", "w_ff2", "pool_w"):
        o[n] = f(n)
    o["gate_b"] = f("gate_b")
    o["pool_scale_c"] = f("pool_scale").reshape(DEPTH, 4, 128).transpose(0, 2, 1)
    o["gate_b_c"] = f("gate_b").reshape(DEPTH, 4, 8, 128).transpose(0, 3, 1, 2).reshape(DEPTH, 128, 32)
    o["wsT"] = f("w_s").transpose(0, 1, 3, 2)
    o["b_s_r"] = f("b_s").reshape(DEPTH, 512)
    o["sgu_ln_g_c"] = f("sgu_ln_g").reshape(DEPTH, 4, 128).transpose(0, 2, 1)
    o["sgu_ln_b_c"] = f("sgu_ln_b").reshape(DEPTH, 4, 128).transpose(0, 2, 1)
    rb = f("rel_bias")
    dist = np.clip(np.arange(256)[None, :] - np.arange(128)[:, None], 0, 128)
    ab = np.zeros((3, 4, 128, 2, 256), np.float32)
    for g, dil in enumerate(DILS):
        bk = _t5_bucket(dist * dil)
        for c in range(4):
            for hd in range(2):
                ab[g, c, :, hd, :] = rb[bk, g * 8 + 2 * c + hd]
    o["att_bias"] = ab
    o["w_glu"] = f("w_glu")
    dup = lambda a: np.concatenate([a, a], axis=1)
    o["s5_are"] = dup(f("a_re").transpose(0, 2, 1))
    o["s5_aim"] = dup(f("a_im").transpose(0, 2, 1))
    o["s5_ldt"] = np.broadcast_to(f("log_dt")[:, None, :], (DEPTH, 128, 32))
    brt, bit = f("b_re").transpose(0, 2, 1, 3), f("b_im").transpose(0, 2, 1, 3)
    crt, cit = f("c_re").transpose(0, 3, 1, 2), f("c_im").transpose(0, 3, 1, 2)
    o["s5_b1"] = np.concatenate([brt, bit], axis=1)
    o["s5_b2"] = np.concatenate([bit, brt], axis=1)
    o["s5_c1"] = np.concatenate([crt, cit], axis=1)
    o["s5_c2"] = np.concatenate([cit, crt], axis=1)
    o["d_skip_c"] = f("d_skip").reshape(DEPTH, 4, 128).transpose(0, 2, 1)
    o["b_glu_c"] = f("b_glu").reshape(DEPTH, 4, 128).transpose(0, 2, 1)
    return {k: np.ascontiguousarray(v, dtype=np.float32) for k, v in o.items()}


FULL_PLAN = [(k, l) for l in range(DEPTH) for k in ("mixer", "cross", "ffn")]


def kernel(**inputs):
    plan = inputs.pop("_plan", FULL_PLAN)
    cores = inputs.pop("_cores", 8)
    import time as _t
    _t0 = _t.time()
    nc, _nops = build_program(plan)
    print(f"[kernel] build {_t.time() - _t0:.1f}s nops={_nops}", flush=True)
    lay = host_layout(inputs)
    shared = {n: lay[n] for n in W_SHAPES}
    shared.update(host_consts())
    x = np.asarray(inputs["x"], dtype=np.float32)
    mem = np.asarray(inputs["mem"], dtype=np.float32)
    in_maps = []
    for c in range(cores):
        m = dict(shared)
        m["x"] = np.ascontiguousarray(x[c])
        m["mem"] = np.ascontiguousarray(mem[c])
        in_maps.append(m)
    _t0 = _t.time()
    res = run_bass_kernel_spmd(nc, in_maps, core_ids=list(range(cores)))
    print(f"[kernel] run {_t.time() - _t0:.1f}s", flush=True)
    if DEBUG_BOUT:
        global _LAST_BOUT
        _LAST_BOUT = [np.asarray(r["bout"]) for r in res.results]
    return np.stack([np.asarray(r["y"], dtype=np.float32) for r in res.results], axis=0)
```

```python
import contextlib
import numpy as np
import concourse.bass as bass
import concourse.mybir as mybir
from concourse.bass_utils import run_bass_kernel_spmd

F32 = mybir.dt.float32
BF16 = mybir.dt.bfloat16
U8 = mybir.dt.uint8
I32 = mybir.dt.int32
ALU = mybir.AluOpType
AF = mybir.ActivationFunctionType

SEM_CAP = 30000
DMA_SLOTS = 8
ARENA_BYTES = 212480

S = 4096
D = 1024
NT = S // 128
DEPTH = 4
EPS = 1e-6
NMEM = 256
IN_WIDTH = 10752
OFF_POOL, OFF_ATT, OFF_SSM, OFF_SGU, OFF_GATE = 0, 512, 5120, 5632, 6656
DILS = (1, 4, 16)


class Res:
    __slots__ = ("name", "lw", "rd")

    def __init__(self, name=""):
        self.name = name
        self.lw = None
        self.rd = []


class T:
    def __init__(self, apview, name):
        self.v = apview
        self.name = name
        self._res = {}

    def __getitem__(self, k):
        return self.v[k]

    def r(self, key=None):
        x = self._res.get(key)
        if x is None:
            x = Res(f"{self.name}:{key}")
            self._res[key] = x
        return x


class Op:
    __slots__ = ("eng", "fn", "deps", "signal", "dma", "slot", "use", "idx", "cnt")

    def __init__(self, eng, fn, dma):
        self.eng = eng
        self.fn = fn
        self.deps = []
        self.signal = False
        self.dma = dma
        self.slot = None
        self.use = None
        self.idx = None
        self.cnt = None


class Prog:
    ENGS = ("pe", "act", "dve", "pool", "sp")

    def __init__(self, nc):
        self.nc = nc
        self.ops = {e: [] for e in self.ENGS}
        self.ndma = {e: 0 for e in self.ENGS}
        self.waited = {e: {} for e in self.ENGS}
        self.pending = {e: [] for e in self.ENGS}
        self.nops = 0

    def _need(self, op, key):
        if key is None:
            return
        X = op.eng
        if key[0] == "c":
            _, Y, j = key
            if Y == X and not op.dma:
                return
            k = ("c", Y)
            if self.waited[X].get(k, -1) >= j:
                return
            self.waited[X][k] = j
            self.ops[Y][j].signal = True
            op.deps.append(key)
        else:
            _, q, slot, use = key
            k = ("d", q, slot)
            if self.waited[X].get(k, 0) >= use:
                return
            self.waited[X][k] = use
            op.deps.append(key)

    def _same(self, o, key):
        eng = o.eng
        if eng == "pe":
            return
        k = ("c", eng)
        if self.waited[eng].get(k, -1) < key[2]:
            self.waited[eng][k] = key[2]
            self.ops[eng][key[2]].signal = True
            o.deps.append(key)

    def barrier(self):
        keys = []
        for e in self.ENGS:
            if self.ops[e]:
                last = None
                for o in reversed(self.ops[e]):
                    if not o.dma:
                        last = o
                        break
                if last is not None:
                    keys.append(("c", e, last.idx))
            n = self.ndma[e]
            for s in range(min(n, DMA_SLOTS)):
                keys.append(("d", e, s, (n - 1 - s) // DMA_SLOTS + 1))
        for e in self.ENGS:
            self.pending[e] = list(keys)

    def op(self, eng, fn, reads=(), writes=(), dma=False):
        o = Op(eng, fn, dma)
        o.idx = len(self.ops[eng])
        if self.pending[eng]:
            for k in self.pending[eng]:
                if k[0] == "c" and k[1] == eng:
                    if dma:
                        self._need(o, k)
                    continue
                self._need(o, k)
            self.pending[eng] = []
        if dma:
            n = self.ndma[eng]
            self.ndma[eng] = n + 1
            o.slot = n % DMA_SLOTS
            o.use = n // DMA_SLOTS + 1
            mykey = ("d", eng, o.slot, o.use)
            if o.use > 1:
                self._need(o, ("d", eng, o.slot, o.use - 1))
        else:
            mykey = ("c", eng, o.idx)
        for r in reads:
            lw = r.lw
            if lw is not None:
                if lw[0] == "c" and lw[1] == eng and not dma:
                    self._same(o, lw)
                else:
                    self._need(o, lw)
        for w in writes:
            if w.lw is not None:
                if w.lw[0] == "c" and w.lw[1] == eng and not dma:
                    self._same(o, w.lw)
                else:
                    self._need(o, w.lw)
            for rk in w.rd:
                if rk[0] == "c" and rk[1] == eng and not dma:
                    self._same(o, rk)
                else:
                    self._need(o, rk)
        for r in reads:
            r.rd.append(mykey)
            if len(r.rd) > 48:
                last = {}
                for kk in r.rd:
                    kid = kk[:2] if kk[0] == "c" else kk[:3]
                    if kid not in last or last[kid][-1] < kk[-1]:
                        last[kid] = kk
                r.rd = list(last.values())
        for w in writes:
            w.lw = mykey
            w.rd = []
        self.ops[eng].append(o)
        self.nops += 1
        return o

    def emit(self):
        nc = self.nc
        nsig = {}
        for e in self.ENGS:
            c = 0
            for o in self.ops[e]:
                if o.signal and not o.dma:
                    c += 1
                    o.cnt = c
            nsig[e] = c
        with contextlib.ExitStack() as st:
            csem = {}
            for e in self.ENGS:
                n = max(1, -(-nsig[e] // SEM_CAP))
                csem[e] = [st.enter_context(nc.semaphore(f"c_{e}_{i}")) for i in range(n)]
            dsem = {}
            for e in self.ENGS:
                if self.ndma[e]:
                    dsem[e] = [st.enter_context(nc.semaphore(f"d_{e}_{i}")) for i in range(DMA_SLOTS)]
            block = st.enter_context(nc.Block())

            def run(eng_name, engobj):
                for o in self.ops[eng_name]:
                    for d in o.deps:
                        if d[0] == "c":
                            p = self.ops[d[1]][d[2]]
                            c = p.cnt - 1
                            engobj.wait_ge(csem[d[1]][c // SEM_CAP], c % SEM_CAP + 1)
                        else:
                            engobj.wait_ge(dsem[d[1]][d[2]], 16 * d[3])
                    ins = o.fn(engobj)
                    if o.dma:
                        ins.then_inc(dsem[eng_name][o.slot], 16)
                    elif o.signal:
                        c = o.cnt - 1
                        ins.then_inc(csem[eng_name][c // SEM_CAP], 1)

            @block.tensor
            def _(e):
                run("pe", e)

            @block.scalar
            def _(e):
                run("act", e)

            @block.vector
            def _(e):
                run("dve", e)

            @block.gpsimd
            def _(e):
                run("pool", e)

            @block.sync
            def _(e):
                run("sp", e)
                for q in self.ENGS:
                    n = self.ndma[q]
                    for s in range(min(n, DMA_SLOTS)):
                        e.wait_ge(dsem[q][s], 16 * ((n - 1 - s) // DMA_SLOTS + 1))


DT_SIZE = {F32: 4, BF16: 2, U8: 1, I32: 4}


class KB:
    def __init__(self, nc, st, dram):
        self.nc = nc
        self.P = Prog(nc)
        self.d = dram
        self.arena = st.enter_context(nc.sbuf_tensor("arena", [128, ARENA_BYTES], U8))
        self.aoff = 0
        self.perm_off = 0
        self.psb = [T(st.enter_context(nc.psum_tensor(f"psb{i}", [128, 512], F32)), f"psb{i}") for i in range(8)]
        self.ps_i = 0
        self.dres = {}
        self.uid = 0

    def alloc(self, name, free_shape, dt, perm=False):
        n = int(np.prod(free_shape)) * DT_SIZE[dt]
        n_al = (n + 63) // 64 * 64
        off = self.aoff
        assert off + n_al <= ARENA_BYTES, f"arena overflow at {name}: {off}+{n_al}"
        self.aoff += n_al
        v = self.arena[:, off:off + n].bitcast(dt)
        if len(free_shape) == 2:
            v = v.rearrange("p (a b) -> p a b", a=free_shape[0])
        elif len(free_shape) == 3:
            v = v.rearrange("p (a b c) -> p a b c", a=free_shape[0], b=free_shape[1])
        self.uid += 1
        return T(v, f"{name}{self.uid}")

    def mark_perm(self):
        self.perm_off = self.aoff

    def reset(self):
        self.P.barrier()
        self.aoff = self.perm_off

    def ps(self, n=1):
        if n == 2 and self.ps_i % 2 == 1:
            self.ps_i += 1
        out = [self.psb[(self.ps_i + i) % 8] for i in range(n)]
        self.ps_i = (self.ps_i + n) % 8
        return out

    def dr(self, name, key=None):
        k = (name, key)
        x = self.dres.get(k)
        if x is None:
            x = Res(f"dram:{name}:{key}")
            self.dres[k] = x
        return x

    def dma(self, q, out, in_, reads=(), writes=()):
        return self.P.op(q, lambda e: e.dma_start(out=out, in_=in_), reads=reads, writes=writes, dma=True)

    def mm(self, out, lhsT, rhs, start, stop, reads, writes):
        return self.P.op("pe", lambda e: e.matmul(out, lhsT=lhsT, rhs=rhs, start=start, stop=stop),
                         reads=reads, writes=writes)

    def tr(self, out, in_, ident, reads, writes):
        return self.P.op("pe", lambda e: e.transpose(out=out, in_=in_, identity=ident), reads=reads, writes=writes)

    def act(self, out, in_, func, reads, writes, bias=None, scale=None, accum_out=None):
        kw = {}
        if bias is not None:
            kw["bias"] = bias
        if scale is not None:
            kw["scale"] = scale
        if accum_out is not None:
            kw["accum_out"] = accum_out
        return self.P.op("act", lambda e: e.activation(out=out, in_=in_, func=func, **kw), reads=reads, writes=writes)

    def v(self, eng, name, reads, writes, *a, **kw):
        return self.P.op(eng, lambda e: getattr(e, name)(*a, **kw), reads=reads, writes=writes)

    def setup_consts(self):
        self.ident_f = self.alloc("identf", [128], F32, perm=True)
        self.ident = self.alloc("ident", [128], BF16, perm=True)
        self.ones = self.alloc("ones", [128], BF16, perm=True)
        self.dma("sp", self.ident_f[:], self.d["c_ident"], writes=[self.ident_f.r()])
        self.v("dve", "tensor_copy", [self.ident_f.r()], [self.ident.r()], out=self.ident[:], in_=self.ident_f[:])
        self.v("dve", "memset", [], [self.ones.r()], self.ones[:], 1.0)
        self.stat_i = 0
        self.stats = self.alloc("stats", [64, 4], F32, perm=True)

    def stat(self):
        i = self.stat_i
        self.stat_i = (i + 1) % 64
        return self.stats[:, i, :], self.stats.r(i)

    def load_bcast(self, name, row_ap, n):
        t = self.alloc(name, [n], F32)
        self.dma("sp", t[:], row_ap.broadcast_to([128, n]), writes=[t.r()])
        return t

    def load_w(self, name, src, kch, ncols):
        t = self.alloc(name, [kch, ncols], BF16)
        step = max(1, (4 * 1024 * 1024) // (128 * ncols * 4))
        for k0 in range(0, kch, step):
            k1 = min(kch, k0 + step)
            self.dma("pool", t[:, k0:k1, :], src[k0 * 128:k1 * 128, :].rearrange("(c p) n -> p c n", p=128),
                     writes=[t.r(k) for k in range(k0, k1)])
        return t

    def rstd_from_ss(self, ss_ap, ss_res, n):
        sq, sq_r = self.stat()
        self.act(sq[:, 0:1], ss_ap, AF.Ln, [ss_res, self.epsb.r()], [sq_r], bias=self.epsb[:, 0:1], scale=1.0 / n)
        self.act(sq[:, 1:2], sq[:, 0:1], AF.Exp, [sq_r], [sq_r], scale=-0.5)
        return sq[:, 1:2], sq_r

    def norm_tile(self, xt, xt_res, gt, hT, hT_res, col0, junk, defer=False):
        st_, st_r = self.stat()
        self.act(junk[:], xt, AF.Square, [xt_res], [junk.r(), st_r], accum_out=st_[:, 0:1])
        rstd, rr = self.rstd_from_ss(st_[:, 0:1], st_r, D)
        hb = self.hb[self.hb_i % len(self.hb)]
        self.hb_i += 1
        self.v("dve", "scalar_tensor_tensor", [xt_res, rr, gt.r()], [hb.r()],
               out=hb[:], in0=xt, scalar=rstd, in1=gt[:], op0=ALU.mult, op1=ALU.mult)
        (pb,) = self.ps(1)
        pv = pb[:].bitcast(BF16).rearrange("p (a b) -> p a b", a=8)
        for k in range(8):
            self.tr(pv[:, k, :], hb[:, k * 128:(k + 1) * 128], self.ident[:], [hb.r(), self.ident.r()], [pb.r()])
        def part_b():
            self.P.op("act", lambda e: e.copy(out=hT[:, 0:8, col0:col0 + 128], in_=pv), reads=[pb.r()], writes=[hT_res])
        if defer:
            return part_b
        part_b()
        return None

    def post_tile(self, pbs, xt, xt_res, gpost, dst_ap, dst_res, junk):
        st_, st_r = self.stat()
        for i in range(2):
            self.act(junk[:, i * 512:(i + 1) * 512], pbs[i][:], AF.Square, [pbs[i].r()], [junk.r(), st_r],
                     accum_out=st_[:, i:i + 1])
        self.v("dve", "tensor_add", [st_r], [st_r], out=st_[:, 2:3], in0=st_[:, 0:1], in1=st_[:, 1:2])
        rstd, rr = self.rstd_from_ss(st_[:, 2:3], st_r, D)
        tmp = self.ptmp[self.ptmp_i % 2]
        self.ptmp_i += 1
        for i in range(2):
            self.v("dve", "tensor_tensor", [pbs[i].r(), gpost.r()], [tmp.r()], out=tmp[:, i * 512:(i + 1) * 512], in0=pbs[i][:],
                   in1=gpost[:, i * 512:(i + 1) * 512], op=ALU.mult)
        self.v("dve", "scalar_tensor_tensor", [tmp.r(), rr, xt_res], [tmp.r()], out=tmp[:], in0=tmp[:], scalar=rstd, in1=xt,
               op0=ALU.mult, op1=ALU.add)
        self.dma("pool", dst_ap, tmp[:], reads=[tmp.r()], writes=[dst_res])

    def common_bufs(self):
        self.epsb = self.alloc("epsb", [1], F32)
        self.v("dve", "memset", [], [self.epsb.r()], self.epsb[:], EPS)
        self.hb = [self.alloc("hb", [D], BF16) for _ in range(2)]
        self.hb_i = 0
        self.ptmp = [self.alloc("ptmp", [D], F32) for _ in range(2)]
        self.ptmp_i = 0
        self.junk = self.alloc("junk", [D], BF16)

    def ffn(self, l, src, dst):
        d = self.d
        G = 256
        ng = S // G
        tpg = G // 128
        self.reset()
        self.common_bufs()
        gpre = self.load_bcast("gpre", d["g_ff_pre"][l:l + 1, :], D)
        gpost = self.load_bcast("gpost", d["g_ff_post"][l:l + 1, :], D)
        w1 = self.load_w("w1", d["w_ff1"][l], 8, 4096)
        w2 = self.load_w("w2", d["w_ff2"][l], 32, D)
        xg = [self.alloc("xg", [tpg, D], F32) for _ in range(2)]
        hTg = [self.alloc("hTg", [8, G], BF16) for _ in range(2)]
        hid = self.alloc("hid", [32, G], BF16)
        rl = self.alloc("rl", [2, G], F32)
        def norm_group(g):
            xb = xg[g % 2]
            hT = hTg[g % 2]
            pend = None
            for t in range(tpg):
                tt = g * tpg + t
                self.dma("sp", xb[:, t, :], d[src][tt * 128:(tt + 1) * 128, :], reads=[self.dr(src, tt)], writes=[xb.r(t)])
                nb_ = self.norm_tile(xb[:, t, :], xb.r(t), gpre, hT, hT.r(), t * 128, self.junk, defer=True)
                if pend is not None:
                    pend()
                pend = nb_
            pend()

        norm_group(0)
        for g in range(ng):
            xb = xg[g % 2]
            hT = hTg[g % 2]
            for j in range(32):
                (pb,) = self.ps(1)
                for k in range(8):
                    self.mm(pb[:, 0:G], w1[:, k, j * 128:(j + 1) * 128], hT[:, k, :], k == 0, k == 7,
                            [w1.r(k), hT.r()], [pb.r()])
                self.act(rl[:, j % 2, :], pb[:, 0:G], AF.Relu, [pb.r()], [rl.r(j % 2)])
                eng = "pool" if j % 2 == 0 else "dve"
                self.v(eng, "tensor_tensor", [rl.r(j % 2)], [hid.r(j)], out=hid[:, j, :], in0=rl[:, j % 2, :],
                       in1=rl[:, j % 2, :], op=ALU.mult)
            if g + 1 < ng:
                norm_group(g + 1)
            for t in range(tpg):
                tt = g * tpg + t
                pbs = self.ps(2)
                for half in range(2):
                    for j in range(32):
                        self.mm(pbs[half][:], hid[:, j, t * 128:(t + 1) * 128], w2[:, j, half * 512:(half + 1) * 512],
                                j == 0, j == 31, [hid.r(j), w2.r(j)], [pbs[half].r()])
                self.post_tile(pbs, xb[:, t, :], xb.r(t), gpost, d[dst][tt * 128:(tt + 1) * 128, :], self.dr(dst, tt), self.junk)

    def cross(self, l, src, dst):
        d = self.d
        G = 512
        ng = S // G
        tpg = G // 128
        self.reset()
        self.common_bufs()
        gpre = self.load_bcast("gpre", d["g_x_pre"][l:l + 1, :], D)
        gpost = self.load_bcast("gpost", d["g_x_post"][l:l + 1, :], D)
        gmem = self.load_bcast("gmem", d["g_mem"][l:l + 1, :], D)
        wq = self.load_w("wq", d["w_cq"][l], 8, 512)
        wkv = self.load_w("wkv", d["w_ckv"][l], 8, 1024)
        wo = self.load_w("wo", d["w_co"][l], 4, D)
        memT = self.alloc("memT", [8, NMEM], BF16)
        kT = self.alloc("kT", [4, NMEM], BF16)
        vv = self.alloc("vv", [2, 512], BF16)
        mt_ = self.alloc("memt", [D], F32)
        for mt in range(2):
            self.dma("sp", mt_[:], d["mem"][mt * 128:(mt + 1) * 128, :], writes=[mt_.r()])
            self.norm_tile(mt_[:], mt_.r(), gmem, memT, memT.r(), mt * 128, self.junk)
        for hd in range(4):
            (pb,) = self.ps(1)
            for k in range(8):
                self.mm(pb[:, 0:NMEM], wkv[:, k, hd * 128:(hd + 1) * 128], memT[:, k, :], k == 0, k == 7,
                        [wkv.r(k), memT.r()], [pb.r()])
            self.P.op("act", lambda e, pb=pb, hd=hd: e.copy(out=kT[:, hd, :], in_=pb[:, 0:NMEM]), reads=[pb.r()], writes=[kT.r()])
        for mt in range(2):
            (pb,) = self.ps(1)
            for k in range(8):
                self.mm(pb[:], memT[:, k, mt * 128:(mt + 1) * 128], wkv[:, k, 512:1024], k == 0, k == 7,
                        [wkv.r(k), memT.r()], [pb.r()])
            self.P.op("act", lambda e, pb=pb, mt=mt: e.copy(out=vv[:, mt, :], in_=pb[:]), reads=[pb.r()], writes=[vv.r()])
        xg = [self.alloc("xg", [tpg, D], F32) for _ in range(2)]
        hTg = [self.alloc("hTg", [8, G], BF16) for _ in range(2)]
        qT = [self.alloc("qT", [G], BF16) for _ in range(3)]
        PT = [self.alloc("PT", [2, G], BF16) for _ in range(3)]
        rden = [self.alloc("rden", [G], F32) for _ in range(2)]
        oTs = [self.alloc("oT", [4, G], BF16) for _ in range(2)]
        scale = 128.0 ** -0.5
        def norm_group(g):
            xb = xg[g % 2]
            hT = hTg[g % 2]
            pend = None
            for t in range(tpg):
                tt = g * tpg + t
                self.dma("sp", xb[:, t, :], d[src][tt * 128:(tt + 1) * 128, :], reads=[self.dr(src, tt)], writes=[xb.r(t)])
                nb_ = self.norm_tile(xb[:, t, :], xb.r(t), gpre, hT, hT.r(), t * 128, self.junk, defer=True)
                if pend is not None:
                    pend()
                pend = nb_
            pend()

        def stA(g, hd):
            hT = hTg[g % 2]
            q = qT[hd % 3]
            (pb,) = self.ps(1)
            for k in range(8):
                self.mm(pb[:], wq[:, k, hd * 128:(hd + 1) * 128], hT[:, k, :], k == 0, k == 7, [wq.r(k), hT.r()], [pb.r()])
            self.P.op("act", lambda e, pb=pb, q=q: e.copy(out=q[:], in_=pb[:]), reads=[pb.r()], writes=[q.r()])

        def stB(g, hd):
            q = qT[hd % 3]
            pt = PT[hd % 3]
            for mt in range(2):
                (pb,) = self.ps(1)
                self.mm(pb[:], kT[:, hd, mt * 128:(mt + 1) * 128], q[:], True, True, [kT.r(), q.r()], [pb.r()])
                self.act(pt[:, mt, :], pb[:], AF.Exp, [pb.r()], [pt.r()], scale=scale)

        def stC(g, hd):
            oT = oTs[g % 2]
            pt = PT[hd % 3]
            rd = rden[hd % 2]
            (pn,) = self.ps(1)
            (pd,) = self.ps(1)
            for mt in range(2):
                self.mm(pn[:], vv[:, mt, hd * 128:(hd + 1) * 128], pt[:, mt, :], mt == 0, mt == 1, [vv.r(), pt.r()], [pn.r()])
            for mt in range(2):
                self.mm(pd[:], self.ones[:], pt[:, mt, :], mt == 0, mt == 1, [self.ones.r(), pt.r()], [pd.r()])
            self.act(rd[:], pd[:], AF.Ln, [pd.r()], [rd.r()])
            self.act(rd[:], rd[:], AF.Exp, [rd.r()], [rd.r()], scale=-1.0)
            self.v("dve", "tensor_tensor", [pn.r(), rd.r()], [oT.r()], out=oT[:, hd, :], in0=pn[:], in1=rd[:], op=ALU.mult)

        norm_group(0)
        stA(0, 0)
        stA(0, 1)
        for g in range(ng):
            xb = xg[g % 2]
            oT = oTs[g % 2]
            stB(g, 0); stA(g, 2); stB(g, 1); stC(g, 0); stA(g, 3); stB(g, 2); stC(g, 1); stB(g, 3); stC(g, 2); stC(g, 3)
            if g + 1 < ng:
                norm_group(g + 1)
                stA(g + 1, 0)
                stA(g + 1, 1)
            for t in range(tpg):
                tt = g * tpg + t
                pbs = self.ps(2)
                for half in range(2):
                    for hd in range(4):
                        self.mm(pbs[half][:], oT[:, hd, t * 128:(t + 1) * 128], wo[:, hd, half * 512:(half + 1) * 512],
                                hd == 0, hd == 3, [oT.r(), wo.r(hd)], [pbs[half].r()])
                self.post_tile(pbs, xb[:, t, :], xb.r(t), gpost, d[dst][tt * 128:(tt + 1) * 128, :], self.dr(dst, tt), self.junk)

    def reset_to(self, off):
        self.P.barrier()
        self.aoff = off

    def mixer(self, l, src, dst, parts=("pool", "att", "s5", "sgu", "merge")):
        d = self.d
        self.reset()
        self.common_bufs()
        hT = self.alloc("hT", [8, S], BF16)
        base = self.aoff
        gpre = self.load_bcast("gpre", d["g_mix_pre"][l:l + 1, :], D)
        xt = [self.alloc("xt", [D], F32) for _ in range(4)]
        hb_save = self.hb
        self.hb = hb_save + [self.alloc("hbx", [D], BF16) for _ in range(2)]
        junks = [self.junk] + [self.alloc("junkx", [D], BF16) for _ in range(1)]
        pend0 = None
        for tt in range(NT):
            x_ = xt[tt % 4]
            self.dma("sp", x_[:], d[src][tt * 128:(tt + 1) * 128, :], reads=[self.dr(src, tt)], writes=[x_.r()])
            nb_ = self.norm_tile(x_[:], x_.r(), gpre, hT, hT.r(), tt * 128, junks[tt % 2], defer=True)
            if pend0 is not None:
                pend0()
            pend0 = nb_
        pend0()
        self.hb = hb_save
        if "pool" in parts or "sgu" in parts:
            self.reset_to(base)
            self.mix_pool_sgu(l, hT)
        if "att" in parts:
            self.reset_to(base)
            self.mix_att(l, hT)
        if "s5" in parts:
            self.reset_to(base)
            self.mix_s5(l, hT)
            self.reset_to(base)
            self.mix_s5_glu(l)
        if "merge" in parts:
            for h2 in range(2):
                self.reset_to(base)
                self.mix_merge_a(l, hT, h2)
            self.reset_to(base)
            self.mix_merge_b(l, src, dst)

    def mix_pool_sgu(self, l, hT):
        d = self.d
        wp = self.load_w("wp", d["w_in"][l][:, OFF_POOL:OFF_POOL + 512], 8, 512)
        pw = self.alloc("pw", [4, 128], BF16)
        self.dma("pool", pw[:], d["pool_w"][l].rearrange("g c e -> c g e"), writes=[pw.r()])
        psc = self.alloc("psc", [4], F32)
        self.dma("sp", psc[:], d["pool_scale_c"][l], writes=[psc.r()])
        invc = self.alloc("invc", [64], F32)
        self.dma("sp", invc[:], d["c_invc"].broadcast_to([128, 64]), writes=[invc.r()])
        hp = self.alloc("hp", [16 + S], F32)
        A = self.alloc("pA", [16 + S], F32)
        B = self.alloc("pB", [16 + S], F32)
        for b_ in (hp, A, B):
            self.v("pool", "memset", [], [b_.r()], b_[:, 0:16], 0.0)
        pbf = self.alloc("pbf", [S], BF16)
        ao = self.alloc("ao", [S], BF16)
        t16 = self.alloc("t16", [16], F32)
        pstate = {}

        def pool_a(gi):
            w = 2 << gi
            for t8 in range(8):
                (pb,) = self.ps(1)
                for k in range(8):
                    self.mm(pb[:], wp[:, k, gi * 128:(gi + 1) * 128], hT[:, k, t8 * 512:(t8 + 1) * 512], k == 0, k == 7,
                            [wp.r(k)], [pb.r()])
                self.P.op("act", lambda e, pb=pb, t8=t8: e.copy(out=hp[:, 16 + t8 * 512:16 + (t8 + 1) * 512], in_=pb[:]),
                          reads=[pb.r()], writes=[hp.r()])
            cur, sh, i = hp, 1, 0
            while sh < w:
                nxt = (A, B)[i % 2]
                self.v("pool", "tensor_add", [cur.r()], [nxt.r()], out=nxt[:, 16:16 + S], in0=cur[:, 16:16 + S],
                       in1=cur[:, 16 - sh:16 - sh + S])
                cur, sh, i = nxt, sh * 2, i + 1
            pstate[gi] = cur

        def pool_b(gi):
            w = 2 << gi
            cur = pstate[gi]
            self.v("dve", "scalar_tensor_tensor", [cur.r(), hp.r()], [pbf.r()], out=pbf[:], in0=cur[:, 16:16 + S],
                   scalar=1.0 / w, in1=hp[:, 16:16 + S], op0=ALU.mult, op1=ALU.subtract)
            self.v("dve", "tensor_tensor", [cur.r(), invc.r()], [t16.r()], out=t16[:], in0=cur[:, 16:32], in1=invc[:, gi * 16:(gi + 1) * 16], op=ALU.mult)
            self.v("dve", "tensor_tensor", [t16.r(), hp.r(), pbf.r()], [pbf.r()], out=pbf[:, 0:16], in0=t16[:], in1=hp[:, 16:32],
                   op=ALU.subtract)
            for t8 in range(8):
                (pb,) = self.ps(1)
                self.mm(pb[:], pw[:, gi, :], pbf[:, t8 * 512:(t8 + 1) * 512], True, True, [pw.r(), pbf.r()], [pb.r()])
                self.v("dve", "tensor_scalar", [pb.r(), psc.r()], [ao.r()], out=ao[:, t8 * 512:(t8 + 1) * 512], in0=pb[:],
                       scalar1=psc[:, gi:gi + 1], scalar2=None, op0=ALU.mult)
            self.dma("sp", d["bout"][0, gi * 128:(gi + 1) * 128, :], ao[:], reads=[ao.r()], writes=[self.dr("bout", (0, gi))])

        wz = self.load_w("wz", d["w_in"][l][:, OFF_SGU:OFF_SGU + 1024], 8, 1024)
        wraw = self.alloc("wraw", [4, 128], F32)
        self.dma("sp", wraw[:], d["wsT"][l].rearrange("g s t -> s g t"), writes=[wraw.r()])
        tril = self.alloc("tril", [128], F32)
        self.dma("sp", tril[:], d["c_tril"], writes=[tril.r()])
        wsb = self.alloc("wsb", [4, 128], BF16)
        for g in range(4):
            self.v("dve", "tensor_tensor", [wraw.r(), tril.r()], [wsb.r()], out=wsb[:, g, :], in0=wraw[:, g, :], in1=tril[:], op=ALU.mult)
        lngc = self.alloc("lngc", [4], F32)
        lnbc = self.alloc("lnbc", [4], F32)
        self.dma("sp", lngc[:], d["sgu_ln_g_c"][l], writes=[lngc.r()])
        self.dma("sp", lnbc[:], d["sgu_ln_b_c"][l], writes=[lnbc.r()])
        bsb = self.load_bcast("bsb", d["b_s_r"][l:l + 1, :], 512)
        bias2 = self.alloc("bias2", [4, 128], F32)
        (prs,) = self.ps(1)
        for g in range(4):
            self.mm(prs[:, g * 128:(g + 1) * 128], self.ones[:], wsb[:, g, :], True, True, [self.ones.r(), wsb.r()], [prs.r()])
        for g in range(4):
            self.v("dve", "scalar_tensor_tensor", [prs.r(), lnbc.r(), bsb.r()], [bias2.r()], out=bias2[:, g, :], in0=prs[:, g * 128:(g + 1) * 128],
                   scalar=lnbc[:, g:g + 1], in1=bsb[:, g * 128:(g + 1) * 128], op0=ALU.mult, op1=ALU.add)
        uT = [self.alloc("uT", [4, 512], BF16) for _ in range(2)]
        dT = [self.alloc("dT", [4, 512], BF16) for _ in range(2)]
        tm = [self.alloc("tm", [4, 128], F32) for _ in range(2)]
        NR = 3
        vg = [self.alloc("vg", [512], F32) for _ in range(NR)]
        vf = [self.alloc("vf", [512], BF16) for _ in range(NR)]
        st6 = [self.alloc("st6", [12], F32) for _ in range(NR)]

        def sgu_u(t8):
            u_ = uT[t8 % 2]
            for c in range(4):
                (pb,) = self.ps(1)
                for k in range(8):
                    self.mm(pb[:], wz[:, k, c * 128:(c + 1) * 128], hT[:, k, t8 * 512:(t8 + 1) * 512], k == 0, k == 7, [wz.r(k)], [pb.r()])
                self.act(u_[:, c, :], pb[:], AF.Gelu, [pb.r()], [u_.r()])

        def sgu_s1(t):
            tok0 = t * 128
            vg_, vf_, s6 = vg[t % NR], vf[t % NR], st6[t % NR]
            (pb,) = self.ps(1)
            for k in range(8):
                self.mm(pb[:], hT[:, k, tok0:tok0 + 128], wz[:, k, 512:1024], k == 0, k == 7, [wz.r(k)], [pb.r()])
            self.act(vg_[:], pb[:], AF.Gelu, [pb.r()], [vg_.r()])
            self.v("dve", "bn_stats", [vg_.r()], [s6.r()], out=s6[:, 0:6], in_=vg_[:])
            self.v("dve", "bn_aggr", [s6.r()], [s6.r()], out=s6[:, 6:8], in_=s6[:, 0:6])
            rstd, rr = self.rstd_from_ss(s6[:, 7:8], s6.r(), 1)
            self.v("dve", "scalar_tensor_tensor", [s6.r(), rr], [s6.r()], out=s6[:, 8:9], in0=s6[:, 6:7], scalar=-1.0, in1=rstd,
                   op0=ALU.mult, op1=ALU.mult)
            self.act(vf_[:], vg_[:], AF.Identity, [vg_.r(), s6.r(), rr], [vf_.r()], bias=s6[:, 8:9], scale=rstd)

        def sgu_s2(t):
            t8, t4 = t // 4, t % 4
            u_ = uT[t8 % 2]
            d_ = dT[t8 % 2]
            vf_ = vf[t % NR]
            tm_ = tm[t % 2]
            (pb2,) = self.ps(1)
            for c in range(4):
                self.mm(pb2[:, c * 128:(c + 1) * 128], vf_[:, c * 128:(c + 1) * 128], wsb[:, c, :], True, True,
                        [vf_.r(), wsb.r()], [pb2.r()])
            for c in range(4):
                self.v("dve", "scalar_tensor_tensor", [pb2.r(), lngc.r(), bias2.r()], [tm_.r()], out=tm_[:, c, :], in0=pb2[:, c * 128:(c + 1) * 128],
                       scalar=lngc[:, c:c + 1], in1=bias2[:, c, :], op0=ALU.mult, op1=ALU.add)
            self.v("dve", "tensor_tensor", [tm_.r(), u_.r()], [d_.r()], out=d_[:, :, t4 * 128:(t4 + 1) * 128],
                   in0=tm_[:], in1=u_[:, :, t4 * 128:(t4 + 1) * 128], op=ALU.mult)
            if t4 == 3:
                self.dma("sp", d["bout"][3, :, t8 * 512:(t8 + 1) * 512].rearrange("(c p) t -> p c t", p=128), d_[:],
                         reads=[d_.r()], writes=[self.dr("bout", (3, t8))])

        LA = 2
        emitted_u = set()

        def need_u(t):
            t8 = t // 4
            if t8 not in emitted_u:
                emitted_u.add(t8)
                sgu_u(t8)

        def sgu_step(t8):
            for t in range(t8 * 4, t8 * 4 + 4):
                if t + LA < 32:
                    need_u(t + LA)
                    sgu_s1(t + LA)
                sgu_s2(t)

        need_u(0)
        for t in range(LA):
            sgu_s1(t)

        pool_a(0)
        for step in range(8):
            sgu_step(step)
            gi = step // 2
            if step % 2 == 0:
                pool_b(gi)
            elif gi + 1 < 4:
                pool_a(gi + 1)

    def mix_att(self, l, hT):
        d = self.d
        amask = self.alloc("amask", [2, 256], BF16)
        self.dma("pool", amask[:], d["c_att_mask"], writes=[amask.r()])
        onesAB = self.alloc("onesAB", [2, 128], BF16)
        self.v("dve", "memset", [], [onesAB.r()], onesAB[:], 0.0)
        self.v("dve", "memset", [], [onesAB.r()], onesAB[:, 0, 0:64], 1.0)
        self.v("dve", "memset", [], [onesAB.r()], onesAB[:, 1, 64:128], 1.0)
        wq = self.alloc("wq", [8, 128], BF16)
        wk = self.alloc("wk", [8, 128], BF16)
        wv = self.alloc("wv", [8, 128], BF16)
        qT = self.alloc("qT", [S], BF16)
        kTA = self.alloc("kTA", [S], BF16)
        kTB = self.alloc("kTB", [S], BF16)
        vT = self.alloc("vT", [S], BF16)
        self.v("dve", "memset", [], [kTA.r()], kTA[64:128, :], 0.0)
        self.v("dve", "memset", [], [kTB.r()], kTB[0:64, :], 0.0)
        vpad = self.alloc("vpad", [32, 2, 128], BF16)
        self.v("pool", "memset", [], [vpad.r()], vpad[:], 0.0)
        acc = self.alloc("acc", [2, S], F32)
        braw = self.alloc("braw", [2, 256], F32)
        ebias = self.alloc("ebias", [2, 256], BF16)
        E = [self.alloc("E", [2, 256], BF16) for _ in range(2)]
        PT = [self.alloc("PT", [2, 256], BF16) for _ in range(4)]
        bo = self.alloc("bo", [S], BF16)
        wsets = [(wq, wk, wv), (self.alloc("wq2", [8, 128], BF16), self.alloc("wk2", [8, 128], BF16), self.alloc("wv2", [8, 128], BF16))]
        braw2 = [braw, self.alloc("braw2", [2, 256], F32)]
        ebias2 = [ebias, self.alloc("ebias2", [2, 256], BF16)]
        E = E + [self.alloc("E3", [2, 256], BF16)]
        units = [(c, g) for c in range(4) for g in range(3)]

        def load_unit(u):
            c, g = units[u]
            wts = wsets[u % 2]
            for wi in range(3):
                col = OFF_ATT + wi * 1536 + g * 512 + c * 128
                self.dma("pool", wts[wi][:, :, 0:128], d["w_in"][l][:, col:col + 128].rearrange("(k p) n -> p k n", p=128),
                         writes=[wts[wi].r()])
            br_, eb_ = braw2[u % 2], ebias2[u % 2]
            self.dma("sp", br_[:], d["att_bias"][g, c], writes=[br_.r()])
            self.act(br_[:], br_[:], AF.Exp, [br_.r()], [br_.r()])
            self.v("dve", "tensor_tensor", [br_.r(), amask.r()], [eb_.r()], out=eb_[:], in0=br_[:], in1=amask[:], op=ALU.mult)

        load_unit(0)
        ipt = 0
        for u, (c, g) in enumerate(units):
            dil = DILS[g]
            L = S // dil
            nb = L // 128
            wts = wsets[u % 2]
            ebias_u = ebias2[u % 2]
            if u + 1 < len(units):
                load_unit(u + 1)
            for wi in range(3):
                for t8 in range(8):
                    (pb,) = self.ps(1)
                    for k in range(8):
                        self.mm(pb[:], wts[wi][:, k, 0:128], hT[:, k, t8 * 512:(t8 + 1) * 512], k == 0, k == 7, [wts[wi].r()], [pb.r()])
                    mw = 512 // dil
                    m0 = t8 * mw

                    def views(dst_, p0, p1):
                        ov = dst_[p0:p1, :].rearrange("p (r m) -> p r m", r=dil)[:, :, m0:m0 + mw]
                        iv = pb[p0:p1, :].rearrange("p (m r) -> p r m", r=dil)
                        return ov, iv
                    if wi == 0:
                        ov, iv = views(qT, 0, 128)
                        self.P.op("act", lambda e, ov=ov, iv=iv: e.mul(out=ov, in_=iv, mul=0.125), reads=[pb.r()], writes=[qT.r()])
                    elif wi == 1:
                        ov, iv = views(kTA, 0, 64)
                        self.v("dve", "tensor_copy", [pb.r()], [kTA.r()], out=ov, in_=iv)
                        ov, iv = views(kTB, 64, 128)
                        self.P.op("act", lambda e, ov=ov, iv=iv: e.copy(out=ov, in_=iv), reads=[pb.r()], writes=[kTB.r()])
                    else:
                        ov, iv = views(vT, 0, 128)
                        self.v("dve", "tensor_copy", [pb.r()], [vT.r()], out=ov, in_=iv)
            for b0 in range(0, 32, 8):
                (pb,) = self.ps(1)
                pv = pb[:].bitcast(BF16).rearrange("p (a b) -> p a b", a=8)
                for bi in range(8):
                    self.tr(pv[:, bi, :], vT[:, (b0 + bi) * 128:(b0 + bi + 1) * 128], self.ident[:], [vT.r(), self.ident.r()], [pb.r()])
                self.P.op("act", lambda e, pv=pv, b0=b0: e.copy(out=vpad[:, b0:b0 + 8, 0, 0:64], in_=pv[:, :, 0:64]), reads=[pb.r()], writes=[vpad.r()])
                self.v("dve", "tensor_copy", [pb.r()], [vpad.r()], out=vpad[:, b0:b0 + 8, 1, 64:128], in_=pv[:, :, 64:128])
            blocks = [(r, n) for r in range(dil) for n in range(nb)]
            ptof = {}

            def stage1(bi):
                nonlocal ipt
                r, n = blocks[bi]
                q0 = r * L + 128 * n
                nq = 256 if n + 1 < nb else 128
                (pb,) = self.ps(1)
                for hd in range(2):
                    kTh = (kTA, kTB)[hd]
                    self.mm(pb[:, hd * 256:hd * 256 + nq], kTh[:, q0:q0 + 128], qT[:, q0:q0 + nq], True, True, [kTh.r(), qT.r()], [pb.r()])
                e_ = E[ipt % 3]
                pt = PT[bi % 4]
                ipt += 1
                pv3 = pb[:].rearrange("p (h j) -> p h j", h=2)
                self.act(e_[:, :, 0:nq], pv3[:, :, 0:nq], AF.Exp, [pb.r()], [e_.r()])
                eng = "pool" if ipt % 2 == 0 else "dve"
                self.v(eng, "tensor_tensor", [e_.r(), ebias_u.r()], [pt.r()], out=pt[:, :, 0:nq], in0=e_[:, :, 0:nq],
                       in1=ebias_u[:, :, 0:nq], op=ALU.mult)
                ptof[bi] = pt

            def stage2(bi):
                r, n = blocks[bi]
                blk = r * nb + n
                pt = ptof[bi]
                (po,) = self.ps(1)
                srcs = [(blk, pt, 0)]
                if n > 0:
                    srcs.append((blk - 1, ptof[bi - 1], 128))
                nmm = 2 * len(srcs)
                for which in range(2):
                    i = 0
                    for (bk, ptile, j0) in srcs:
                        for hd in range(2):
                            lhs = vpad[:, bk, hd, :] if which == 0 else onesAB[:, hd, :]
                            self.mm(po[:, which * 128:(which + 1) * 128], lhs, ptile[:, hd, j0:j0 + 128], i == 0, i == nmm - 1,
                                    [vpad.r(), onesAB.r(), ptile.r()], [po.r()])
                            i += 1
                av = acc[:].rearrange("p a (m r) -> p a m r", r=dil)[:, :, 128 * n:128 * (n + 1), r]
                pov = po[:, 0:256].rearrange("p (a q) -> p a q", a=2)
                if g == 0:
                    self.v("dve", "tensor_copy", [po.r()], [acc.r()], out=av, in_=pov)
                else:
                    self.v("dve", "tensor_tensor", [po.r(), acc.r()], [acc.r()], out=av, in0=pov, in1=av, op=ALU.add)

            LA = 2
            for bi in range(min(LA, len(blocks))):
                stage1(bi)
            for bi in range(len(blocks)):
                if bi + LA < len(blocks):
                    stage1(bi + LA)
                stage2(bi)
            if g == 2:
                self.act(acc[:, 1, :], acc[:, 1, :], AF.Ln, [acc.r()], [acc.r()])
                self.act(acc[:, 1, :], acc[:, 1, :], AF.Exp, [acc.r()], [acc.r()], scale=-1.0)
                self.v("pool", "tensor_tensor", [acc.r()], [bo.r()], out=bo[:], in0=acc[:, 0, :], in1=acc[:, 1, :], op=ALU.mult)
                self.dma("sp", d["bout"][1, c * 128:(c + 1) * 128, :], bo[:], reads=[bo.r()], writes=[self.dr("bout", (1, c))])

    def mix_s5(self, l, hT):
        d = self.d
        PI_ = float(np.pi)
        tabr = Res("s5tab")

        def T32(name, n=1):
            return self.alloc(name, [n, 32], F32) if n > 1 else self.alloc(name, [32], F32)

        def tt(out, a, b, op, eng="dve"):
            self.v(eng, "tensor_tensor", [tabr], [tabr], out=out, in0=a, in1=b, op=op)

        def ts(out, a, s1, op0, s2=None, op1=None, eng="dve"):
            kw = dict(out=out, in0=a, scalar1=s1, scalar2=s2, op0=op0)
            if op1 is not None:
                kw["op1"] = op1
            self.v(eng, "tensor_scalar", [tabr], [tabr], **kw)

        def actf(out, in_, func):
            self.act(out, in_, func, [tabr], [tabr])

        ldt, are, aim = T32("ldt"), T32("are"), T32("aim")
        self.dma("sp", ldt[:], d["s5_ldt"][l], writes=[tabr])
        self.dma("sp", are[:], d["s5_are"][l], writes=[tabr])
        self.dma("sp", aim[:], d["s5_aim"][l], writes=[tabr])
        sgn = self.alloc("sgn", [4], F32)
        self.dma("sp", sgn[:], d["c_sgn"], writes=[tabr])
        B1 = self.alloc("B1", [32, 16], F32)
        B2 = self.alloc("B2", [32, 16], F32)
        C1 = self.alloc("C1", [32, 16], F32)
        C2 = self.alloc("C2", [32, 16], F32)
        for t_, n_ in ((B1, "s5_b1"), (B2, "s5_b2"), (C1, "s5_c1"), (C2, "s5_c2")):
            self.dma("sp", t_[:], d[n_][l], writes=[tabr])
        swap = self.alloc("swap", [128], F32)
        bdm = self.alloc("bdm", [128], F32)
        rowm = self.alloc("rowm", [8], F32)
        colm = self.alloc("colm", [8, 128], F32)
        dcol = self.alloc("dcol", [4], F32)
        for t_, n_ in ((swap, "c_swap"), (bdm, "c_bdmask"), (rowm, "c_rowmask"), (colm, "c_colmask")):
            self.dma("sp", t_[:], d[n_], writes=[tabr])
        self.dma("sp", dcol[:], d["d_skip_c"][l], writes=[tabr])
        dt, lre, x1, mag, ang = T32("dt"), T32("lre"), T32("x1"), T32("mag"), T32("ang")
        actf(dt[:], ldt[:], AF.Exp)
        ts(lre[:], are[:], -1e-4, ALU.min)
        tt(x1[:], lre[:], dt[:], ALU.mult)
        actf(mag[:], x1[:], AF.Exp)
        tt(ang[:], aim[:], dt[:], ALU.mult)
        z, y_, yf, r_, m_ = T32("z"), T32("y"), T32("yf"), T32("r"), T32("m")
        yi = self.alloc("yi", [32], I32)

        def sin_of(dst, shift):
            ts(z[:], ang[:], shift, ALU.add)
            ts(y_[:], z[:], 1.0 / (2 * PI_), ALU.mult)
            self.v("dve", "tensor_copy", [tabr], [tabr], out=yi[:], in_=y_[:])
            self.v("dve", "tensor_copy", [tabr], [tabr], out=yf[:], in_=yi[:])
            self.v("dve", "scalar_tensor_tensor", [tabr], [tabr], out=r_[:], in0=yf[:], scalar=-2 * PI_, in1=z[:], op0=ALU.mult, op1=ALU.add)
            ts(m_[:], r_[:], PI_, ALU.is_gt, -2 * PI_, ALU.mult)
            tt(r_[:], r_[:], m_[:], ALU.add)
            ts(m_[:], r_[:], -PI_, ALU.is_lt, 2 * PI_, ALU.mult)
            tt(r_[:], r_[:], m_[:], ALU.add)
            ts(r_[:], r_[:], -3.14159, ALU.max, 3.14159, ALU.min)
            actf(dst, r_[:], AF.Sin)

        cosv, sinv, ar, ai = T32("cosv"), T32("sinv"), T32("ar"), T32("ai")
        sin_of(cosv[:], PI_ / 2)
        sin_of(sinv[:], 0.0)
        tt(ar[:], mag[:], cosv[:], ALU.mult)
        tt(ai[:], mag[:], sinv[:], ALU.mult)
        PR, PIm = T32("PR", 9), T32("PI", 9)
        t1, t2 = T32("t1"), T32("t2")
        self.v("dve", "memset", [tabr], [tabr], PR[:, 0, :], 1.0)
        self.v("dve", "memset", [tabr], [tabr], PIm[:, 0, :], 0.0)
        for j in range(8):
            tt(t1[:], PR[:, j, :], ar[:], ALU.mult)
            tt(t2[:], PIm[:, j, :], ai[:], ALU.mult)
            tt(PR[:, j + 1, :], t1[:], t2[:], ALU.subtract)
            tt(t1[:], PR[:, j, :], ai[:], ALU.mult)
            tt(t2[:], PIm[:, j, :], ar[:], ALU.mult)
            tt(PIm[:, j + 1, :], t1[:], t2[:], ALU.add)
        QR, QI, QIs = T32("QR", 9), T32("QI", 9), T32("QIs", 9)
        self.v("dve", "tensor_copy", [tabr], [tabr], out=QR[:, 0, :], in_=PR[:, 8, :])
        self.v("dve", "tensor_copy", [tabr], [tabr], out=QI[:, 0, :], in_=PIm[:, 8, :])
        for i in range(8):
            tt(t1[:], QR[:, i, :], QR[:, i, :], ALU.mult)
            tt(t2[:], QI[:, i, :], QI[:, i, :], ALU.mult)
            tt(QR[:, i + 1, :], t1[:], t2[:], ALU.subtract)
            tt(t1[:], QR[:, i, :], QI[:, i, :], ALU.mult)
            ts(QI[:, i + 1, :], t1[:], 2.0, ALU.mult)
        ts(QIs[:], QI[:], sgn[:, 1:2], ALU.mult)
        PIs1, TA, TB = T32("PIs1", 9), T32("TA", 9), T32("TB", 9)
        ts(PIs1[:], PIm[:], sgn[:, 0:1], ALU.mult)
        ts(TA[:], PR[:], sgn[:, 1:2], ALU.mult)
        ts(TB[:], PIm[:], -1.0, ALU.mult)
        den, am1, fr, fi, FIs, FIs2 = T32("den"), T32("am1"), T32("fr"), T32("fi"), T32("FIs"), T32("FIs2")
        tt(t1[:], lre[:], lre[:], ALU.mult)
        tt(t2[:], aim[:], aim[:], ALU.mult)
        tt(den[:], t1[:], t2[:], ALU.add)
        self.v("dve", "reciprocal", [tabr], [tabr], out=den[:], in_=den[:])
        ts(am1[:], ar[:], -1.0, ALU.add)
        tt(t1[:], am1[:], lre[:], ALU.mult)
        tt(t2[:], ai[:], aim[:], ALU.mult)
        tt(t1[:], t1[:], t2[:], ALU.add)
        tt(fr[:], t1[:], den[:], ALU.mult)
        tt(t1[:], ai[:], lre[:], ALU.mult)
        tt(t2[:], am1[:], aim[:], ALU.mult)
        tt(t1[:], t1[:], t2[:], ALU.subtract)
        tt(fi[:], t1[:], den[:], ALU.mult)
        ts(FIs[:], fi[:], sgn[:, 0:1], ALU.mult)
        ts(FIs2[:], fi[:], sgn[:, 1:2], ALU.mult)

        def bc(tab_ap_q):
            return tab_ap_q.unsqueeze(2).to_broadcast([128, 8, 16])

        bs = self.alloc("bs", [8, 16], F32)
        bw = self.alloc("bw", [8, 16], F32)
        tf1 = self.alloc("tf1", [8, 16], F32)
        tf2 = self.alloc("tf2", [8, 16], F32)
        XB = self.alloc("XB", [8, 128], BF16)
        CP = self.alloc("CP", [8, 128], BF16)
        Cst = self.alloc("Cst", [128], BF16)
        XA = self.alloc("XA", [8, 128], BF16)
        BD = self.alloc("BD", [8, 128], BF16)
        wu = self.alloc("wu5", [8, 128], BF16)
        uq = self.alloc("uq", [S], BF16)
        gq = self.alloc("gq", [S], BF16)
        GS = [self.alloc("GS", [8, 128], BF16) for _ in range(4)]
        AM = [self.alloc("AM", [9, 128], BF16) for _ in range(4)]
        CPm = [self.alloc("CPm", [8, 128], BF16) for _ in range(8)]
        H = [[self.alloc("H", [512], BF16) for _ in range(2)] for _ in range(4)]
        HP = self.alloc("HP", [8, 512], BF16)
        self.v("pool", "memset", [], [HP.r(g8) for g8 in range(8)], HP[:], 0.0)
        ytmp = [self.alloc("ytmp", [512], F32) for _ in range(2)]
        amt9 = [[self.alloc("amt9", [9, 128], BF16) for _ in range(2)] for _ in range(2)]
        swapb = self.alloc("swapb", [128], BF16)
        self.v("dve", "tensor_copy", [tabr], [swapb.r()], out=swapb[:], in_=swap[:])
        colmb = self.alloc("colmb", [8, 128], BF16)
        self.v("dve", "tensor_copy", [tabr], [colmb.r()], out=colmb[:], in_=colm[:])
        for q in range(4):
            gsl = slice(q * 8, (q + 1) * 8)
            tt(bs[:], B1[:, gsl, :], bc(fr[:, gsl]), ALU.mult)
            tt(tf1[:], B2[:, gsl, :], bc(FIs[:, gsl]), ALU.mult)
            tt(bs[:], bs[:], tf1[:], ALU.add)
            tt(bw[:], B2[:, gsl, :], bc(fr[:, gsl]), ALU.mult)
            tt(tf1[:], B1[:, gsl, :], bc(FIs2[:, gsl]), ALU.mult)
            tt(bw[:], bw[:], tf1[:], ALU.add)
            xbr, cpr = XB.r(), CP.r()
            for j in range(8):
                tt(tf1[:], bs[:], bc(PR[:, j, gsl]), ALU.mult)
                tt(tf2[:], bw[:], bc(PIs1[:, j, gsl]), ALU.mult)
                self.v("dve", "tensor_tensor", [tabr], [tabr, xbr], out=XB[:, j, :].rearrange("p (g c) -> p g c", g=8), in0=tf1[:], in1=tf2[:], op=ALU.add)
            for t in range(8):
                tt(tf1[:], C1[:, gsl, :], bc(TA[:, t + 1, gsl]), ALU.mult)
                tt(tf2[:], C2[:, gsl, :], bc(TB[:, t + 1, gsl]), ALU.mult)
                self.v("dve", "tensor_tensor", [tabr], [tabr, cpr], out=CP[:, t, :].rearrange("p (g c) -> p g c", g=8), in0=tf1[:], in1=tf2[:], op=ALU.add)
            self.v("dve", "tensor_scalar", [tabr], [tabr, Cst.r()], out=Cst[:].rearrange("p (g c) -> p g c", g=8), in0=C1[:, gsl, :],
                   scalar1=sgn[:, 1:2], scalar2=None, op0=ALU.mult)
            (pbx,) = self.ps(1)
            pxv = pbx[:].bitcast(BF16).rearrange("p (a b) -> p a b", a=8)
            for s_ in range(8):
                self.tr(pxv[:, s_, :], XB[:, 7 - s_, :], self.ident[:], [xbr, self.ident.r()], [pbx.r()])
            self.P.op("act", lambda e, pxv=pxv: e.copy(out=XA[:], in_=pxv), reads=[pbx.r()], writes=[XA.r()])
            for jh in range(2):
                (pbd,) = self.ps(1)
                for jj in range(4):
                    self.mm(pbd[:, jj * 128:(jj + 1) * 128], XB[:, jh * 4 + jj, :], Cst[:], True, True, [xbr, Cst.r()], [pbd.r()])
                self.v("dve", "tensor_tensor", [pbd.r(), tabr], [BD.r()], out=BD[:, jh * 4:(jh + 1) * 4, :],
                       in0=pbd[:].rearrange("p (j c) -> p j c", j=4), in1=bdm[:].unsqueeze(1).to_broadcast([128, 4, 128]), op=ALU.mult)
            col = OFF_SSM + q * 128
            self.dma("pool", wu[:], d["w_in"][l][:, col:col + 128].rearrange("(k p) n -> p k n", p=128), writes=[wu.r()])
            for t8 in range(8):
                (pb,) = self.ps(1)
                for k in range(8):
                    self.mm(pb[:], wu[:, k, :], hT[:, k, t8 * 512:(t8 + 1) * 512], k == 0, k == 7, [wu.r()], [pb.r()])
                self.P.op("act", lambda e, pb=pb, t8=t8: e.copy(out=uq[:, t8 * 512:(t8 + 1) * 512], in_=pb[:]), reads=[pb.r()], writes=[uq.r()])
            uq8 = uq[:].rearrange("p (k s) -> p s k", s=8)
            for st4 in range(2):
                grp = [st4 * 4 + i for i in range(4)]
                for gi, g8 in enumerate(grp):
                    g = q * 8 + g8
                    self.P.op("act", lambda e, gi=gi, g8=g8: e.activation(out=GS[gi][:], in_=XA[:], func=AF.Copy, scale=rowm[:, g8:g8 + 1]),
                              reads=[XA.r(), tabr], writes=[GS[gi].r()])
                    self.v("dve", "tensor_tensor", [cpr, colmb.r()], [CPm[g8].r()], out=CPm[g8][:], in0=CP[:],
                           in1=colmb[:, g8, :].unsqueeze(1).to_broadcast([128, 8, 128]), op=ALU.mult)
                    am_a = amt9[gi % 2][0]
                    am_b = amt9[gi % 2][1]
                    for i in range(9):
                        self.P.op("act", lambda e, am_a=am_a, i=i, g=g: e.activation(out=am_a[:, i, :], in_=self.ident[:], func=AF.Copy, scale=QR[:, i, g:g + 1]),
                                  reads=[tabr, self.ident.r()], writes=[am_a.r()])
                        self.P.op("act", lambda e, am_b=am_b, i=i, g=g: e.activation(out=am_b[:, i, :], in_=swapb[:], func=AF.Copy, scale=QIs[:, i, g:g + 1]),
                                  reads=[tabr, swapb.r()], writes=[am_b.r()])
                    self.v("dve", "tensor_tensor", [am_a.r(), am_b.r()], [AM[gi].r()], out=AM[gi][:], in0=am_a[:], in1=am_b[:], op=ALU.add)
                cur = {}
                for gi, g8 in enumerate(grp):
                    (pb,) = self.ps(1)
                    for s_ in range(8):
                        self.mm(pb[:], GS[gi][:, s_, :], uq8[:, s_, :], s_ == 0, s_ == 7, [GS[gi].r(), uq.r()], [pb.r()])
                    self.P.op("act", lambda e, pb=pb, gi=gi: e.copy(out=H[gi][0][:], in_=pb[:]), reads=[pb.r()], writes=[H[gi][0].r()])
                    cur[gi] = 0
                for i in range(9):
                    dd = 1 << i
                    for gi, g8 in enumerate(grp):
                        hc = H[gi][cur[gi]]
                        (pb,) = self.ps(1)
                        if i < 8:
                            self.mm(pb[:, dd:512], AM[gi][:, i, :], hc[:, 0:512 - dd], True, True, [AM[gi].r(), hc.r()], [pb.r()])
                            self.v("dve", "tensor_tensor", [pb.r(), hc.r()], [hc.r()], out=hc[:, dd:512], in0=pb[:, dd:512], in1=hc[:, dd:512], op=ALU.add)
                            continue
                        self.mm(pb[:, 0:dd], self.ident[:], hc[:, 0:dd], True, True, [self.ident.r(), hc.r()], [pb.r()])
                        self.mm(pb[:, dd:512], self.ident[:], hc[:, dd:512], True, False, [self.ident.r(), hc.r()], [pb.r()])
                        self.mm(pb[:, dd:512], AM[gi][:, i, :], hc[:, 0:512 - dd], False, True, [AM[gi].r(), hc.r()], [pb.r()])
                        if i < 8:
                            hn = H[gi][1 - cur[gi]]
                            self.P.op("act", lambda e, pb=pb, hn=hn: e.copy(out=hn[:], in_=pb[:]), reads=[pb.r()], writes=[hn.r()])
                            cur[gi] = 1 - cur[gi]
                        else:
                            self.v("dve", "tensor_copy", [pb.r()], [HP.r(g8)], out=HP[:, g8, 1:512], in_=pb[:, 0:511])
            for t in range(8):
                (py,) = self.ps(1)
                n_mm = (t + 1) + 8
                i_mm = 0
                for s_ in range(t + 1):
                    self.mm(py[:], BD[:, t - s_, :], uq8[:, s_, :], i_mm == 0, i_mm == n_mm - 1, [BD.r(), uq.r()], [py.r()])
                    i_mm += 1
                for g8 in range(8):
                    self.mm(py[:], CPm[g8][:, t, :], HP[:, g8, :], i_mm == 0, i_mm == n_mm - 1, [CPm[g8].r(), HP.r(g8)], [py.r()])
                    i_mm += 1
                yt_ = ytmp[t % 2]
                self.v("dve", "scalar_tensor_tensor", [uq.r(), py.r(), tabr], [yt_.r()], out=yt_[:], in0=uq8[:, t, :], scalar=dcol[:, q:q + 1],
                       in1=py[:], op0=ALU.mult, op1=ALU.add)
                self.act(gq[:].rearrange("p (k s) -> p s k", s=8)[:, t, :], yt_[:], AF.Gelu, [yt_.r()], [gq.r()])
            self.dma("sp", d["gsc"][q * 128:(q + 1) * 128, :], gq[:], reads=[gq.r()], writes=[self.dr("gsc", q)])

    def mix_s5_glu(self, l):
        d = self.d
        wgl = self.load_w("wglu", d["w_glu"][l], 4, 512)
        bgl = self.alloc("bgl", [4], F32)
        self.dma("sp", bgl[:], d["b_glu_c"][l], writes=[bgl.r()])
        gT = [self.alloc("gTt", [4, 512], BF16) for _ in range(2)]
        co = [self.alloc("co", [4, 512], BF16) for _ in range(2)]
        sg = [self.alloc("sg5", [512], F32) for _ in range(2)]
        for t8 in range(8):
            g_ = gT[t8 % 2]
            c_ = co[t8 % 2]
            self.dma("sp", g_[:], d["gsc"][:, t8 * 512:(t8 + 1) * 512].rearrange("(c p) t -> p c t", p=128),
                     reads=[self.dr("gsc", q) for q in range(4)], writes=[g_.r()])
            for oc in range(4):
                s_ = sg[oc % 2]
                (pz,) = self.ps(1)
                for kc in range(4):
                    self.mm(pz[:], wgl[:, kc, oc * 128:(oc + 1) * 128], g_[:, kc, :], kc == 0, kc == 3, [wgl.r(kc), g_.r()], [pz.r()])
                self.act(s_[:], pz[:], AF.Sigmoid, [pz.r(), bgl.r()], [s_.r()], bias=bgl[:, oc:oc + 1])
                eng = "pool" if oc % 2 == 0 else "dve"
                self.v(eng, "tensor_tensor", [s_.r(), g_.r()], [c_.r()], out=c_[:, oc, :], in0=s_[:], in1=g_[:, oc, :], op=ALU.mult)
            self.dma("sp", d["bout"][2, :, t8 * 512:(t8 + 1) * 512].rearrange("(c p) t -> p c t", p=128), c_[:],
                     reads=[c_.r()], writes=[self.dr("bout", (2, t8))])

    def mix_merge_a(self, l, hT, h2):
        d = self.d
        wg = self.alloc("wg", [8, 4, 512], BF16)
        for i in range(4):
            col = OFF_GATE + i * 1024 + h2 * 512
            self.dma("pool", wg[:, :, i, :], d["w_in"][l][:, col:col + 512].rearrange("(k p) n -> p k n", p=128), writes=[wg.r(i)])
        wu = self.alloc("wu", [16, 512], BF16)
        for i in range(4):
            self.dma("pool", wu[:, i * 4:(i + 1) * 4, :], d["w_up"][l, i][:, h2 * 512:(h2 + 1) * 512].rearrange("(k p) n -> p k n", p=128),
                     writes=[wu.r(i)])
        gb = self.alloc("gb", [32], F32)
        self.dma("sp", gb[:], d["gate_b_c"][l], writes=[gb.r()])
        brT = [self.alloc("brT", [16, 512], BF16) for _ in range(2)]
        mT = [self.alloc("mT", [4, 512], BF16) for _ in range(2)]
        sg = [self.alloc("sg", [512], F32) for _ in range(2)]
        pr = [self.alloc("pr", [512], F32) for _ in range(2)]
        ac = [self.alloc("ac", [512], F32) for _ in range(2)]
        isg = 0
        for t8 in range(8):
            br = brT[t8 % 2]
            m_ = mT[t8 % 2]
            for i in range(4):
                self.dma("sp", br[:, i * 4:(i + 1) * 4, :], d["bout"][i, :, t8 * 512:(t8 + 1) * 512].rearrange("(c p) t -> p c t", p=128),
                         reads=[self.dr("bout", (i, x)) for x in range(8)], writes=[br.r(i)])
            for dcl in range(4):
                dc = h2 * 4 + dcl
                a_ = ac[dcl % 2]
                for i in range(4):
                    s_ = sg[isg % 2]
                    p_ = pr[isg % 2]
                    isg += 1
                    (pg,) = self.ps(1)
                    for k in range(8):
                        self.mm(pg[:], wg[:, k, i, dcl * 128:(dcl + 1) * 128], hT[:, k, t8 * 512:(t8 + 1) * 512], k == 0, k == 7, [wg.r(i)], [pg.r()])
                    self.act(s_[:], pg[:], AF.Sigmoid, [pg.r(), gb.r()], [s_.r()], bias=gb[:, i * 8 + dc:i * 8 + dc + 1])
                    (pu,) = self.ps(1)
                    for kc in range(4):
                        self.mm(pu[:], wu[:, i * 4 + kc, dcl * 128:(dcl + 1) * 128], br[:, i * 4 + kc, :], kc == 0, kc == 3, [wu.r(i), br.r(i)], [pu.r()])
                    if i == 0:
                        self.v("dve", "tensor_tensor", [pu.r(), s_.r()], [a_.r()], out=a_[:], in0=pu[:], in1=s_[:], op=ALU.mult)
                    else:
                        self.v("dve", "tensor_tensor", [pu.r(), s_.r()], [p_.r()], out=p_[:], in0=pu[:], in1=s_[:], op=ALU.mult)
                        if i < 3:
                            self.v("dve", "tensor_tensor", [p_.r(), a_.r()], [a_.r()], out=a_[:], in0=p_[:], in1=a_[:], op=ALU.add)
                        else:
                            self.v("dve", "tensor_tensor", [p_.r(), a_.r()], [m_.r()], out=m_[:, dcl, :], in0=p_[:], in1=a_[:], op=ALU.add)
            self.dma("sp", d["mrg"][h2 * 512:(h2 + 1) * 512, t8 * 512:(t8 + 1) * 512].rearrange("(c p) t -> p c t", p=128), m_[:],
                     reads=[m_.r()], writes=[self.dr("mrg", (h2, t8))])

    def mix_merge_b(self, l, src, dst):
        d = self.d
        wo = self.load_w("wout", d["w_out"][l], 8, D)
        gpost = self.load_bcast("gpost", d["g_mix_post"][l:l + 1, :], D)
        mT = [self.alloc("mTb", [8, 512], BF16) for _ in range(2)]
        xt = [self.alloc("xtb", [D], F32) for _ in range(2)]
        for t8 in range(8):
            m_ = mT[t8 % 2]
            self.dma("sp", m_[:], d["mrg"][:, t8 * 512:(t8 + 1) * 512].rearrange("(c p) t -> p c t", p=128),
                     reads=[self.dr("mrg", (0, t8)), self.dr("mrg", (1, t8))], writes=[m_.r()])
            for t4 in range(4):
                tt = t8 * 4 + t4
                x_ = xt[tt % 2]
                self.dma("sp", x_[:], d[src][tt * 128:(tt + 1) * 128, :], reads=[self.dr(src, tt)], writes=[x_.r()])
                pbs = self.ps(2)
                for half in range(2):
                    for dc in range(8):
                        self.mm(pbs[half][:], m_[:, dc, t4 * 128:(t4 + 1) * 128], wo[:, dc, half * 512:(half + 1) * 512], dc == 0, dc == 7,
                                [m_.r(), wo.r(dc)], [pbs[half].r()])
                self.post_tile(pbs, x_[:], x_.r(), gpost, d[dst][tt * 128:(tt + 1) * 128, :], self.dr(dst, tt), self.junk)

W_SHAPES = {
    "g_mix_pre": (DEPTH, D), "g_mix_post": (DEPTH, D), "w_in": (DEPTH, D, IN_WIDTH), "gate_b": (DEPTH, 4, D),
    "w_up": (DEPTH, 4, 512, D), "w_out": (DEPTH, D, D),
    "g_x_pre": (DEPTH, D), "g_x_post": (DEPTH, D), "g_mem": (DEPTH, D),
    "w_cq": (DEPTH, D, 512), "w_ckv": (DEPTH, D, 1024), "w_co": (DEPTH, 512, D),
    "g_ff_pre": (DEPTH, D), "g_ff_post": (DEPTH, D), "w_ff1": (DEPTH, D, 4096), "w_ff2": (DEPTH, 4096, D),
}
W_SHAPES.update({
    "pool_w": (DEPTH, 4, 128, 128), "pool_scale_c": (DEPTH, 128, 4), "wsT": (DEPTH, 4, 128, 128),
    "sgu_ln_g_c": (DEPTH, 128, 4), "sgu_ln_b_c": (DEPTH, 128, 4), "b_s_r": (DEPTH, 512), "gate_b_c": (DEPTH, 128, 32),
    "att_bias": (3, 4, 128, 2, 256),
    "s5_are": (DEPTH, 128, 32), "s5_aim": (DEPTH, 128, 32), "s5_ldt": (DEPTH, 128, 32),
    "s5_b1": (DEPTH, 128, 32, 16), "s5_b2": (DEPTH, 128, 32, 16), "s5_c1": (DEPTH, 128, 32, 16), "s5_c2": (DEPTH, 128, 32, 16),
    "d_skip_c": (DEPTH, 128, 4), "b_glu_c": (DEPTH, 128, 4), "w_glu": (DEPTH, 512, 512),
})
CONSTS = {"c_ident": (128, 128), "c_invc": (1, 64), "c_tril": (128, 128), "c_att_mask": (128, 2, 256),
          "c_sgn": (128, 4), "c_swap": (128, 128), "c_bdmask": (128, 128), "c_rowmask": (128, 8), "c_colmask": (128, 8, 128)}
DEBUG_BOUT = False


def build_program(plan):
    nc = bass.Bass("TRN2", target_bir_lowering=False)
    dram = {}
    dram["x"] = nc.dram_tensor("x", [S, D], F32, kind="ExternalInput").ap()
    dram["mem"] = nc.dram_tensor("mem", [NMEM, D], F32, kind="ExternalInput").ap()
    for n, shp in W_SHAPES.items():
        dram[n] = nc.dram_tensor(n, list(shp), F32, kind="ExternalInput").ap()
    for n, shp in CONSTS.items():
        dram[n] = nc.dram_tensor(n, list(shp), F32, kind="ExternalInput").ap()
    dram["y"] = nc.dram_tensor("y", [S, D], F32, kind="ExternalOutput").ap()
    dram["xr"] = nc.dram_tensor("xr", [S, D], F32, kind="Internal").ap()
    dram["bout"] = nc.dram_tensor("bout", [4, 512, S], BF16, kind="ExternalOutput" if DEBUG_BOUT else "Internal").ap()
    dram["mrg"] = nc.dram_tensor("mrg", [D, S], BF16, kind="Internal").ap()
    dram["gsc"] = nc.dram_tensor("gsc", [512, S], BF16, kind="Internal").ap()
    with contextlib.ExitStack() as st:
        kb = KB(nc, st, dram)
        kb.setup_consts()
        kb.mark_perm()
        for i, item in enumerate(plan):
            kind, l = item[0], item[1]
            src = "x" if i == 0 else "xr"
            dst = "y" if i == len(plan) - 1 else "xr"
            getattr(kb, kind)(l, src, dst, *item[2:])
        kb.P.emit()
        nops = kb.P.nops
    return nc, nops


def _t5_bucket(n):
    exact = 16
    nf = np.maximum(n, 1).astype(np.float32)
    large = exact + (np.log(nf / exact) / np.log(2048 / exact) * (32 - exact)).astype(np.int32)
    large = np.minimum(large, 31)
    return np.where(n < exact, n, large).astype(np.int32)


def host_consts():
    c = {"c_ident": np.eye(128, dtype=np.float32)}
    invc = np.zeros((1, 64), np.float32)
    for gi in range(4):
        w = 2 << gi
        invc[0, gi * 16:(gi + 1) * 16] = 1.0 / np.minimum(np.arange(16) + 1, w)
    c["c_invc"] = invc
    s_ = np.arange(128)
    c["c_tril"] = (s_[:, None] <= s_[None, :]).astype(np.float32)
    dist = np.arange(256)[None, :] - np.arange(128)[:, None]
    m = ((dist >= 0) & (dist <= 128)).astype(np.float32)
    c["c_att_mask"] = np.ascontiguousarray(np.broadcast_to(m[:, None, :], (128, 2, 256)))
    sg = np.ones((128, 4), np.float32)
    sg[:64, 0] = -1.0
    sg[64:, 1] = -1.0
    c["c_sgn"] = sg
    p = np.arange(128)
    c["c_swap"] = (p[None, :] == ((p[:, None] + 64) % 128)).astype(np.float32)
    c["c_bdmask"] = ((p[:, None] // 16) == (p[None, :] // 16)).astype(np.float32)
    c["c_rowmask"] = ((p[:, None] // 16) == np.arange(8)[None, :]).astype(np.float32)
    c["c_colmask"] = np.ascontiguousarray(np.broadcast_to(((p[None, None, :] // 16) == np.arange(8)[None, :, None]), (128, 8, 128)).astype(np.float32))
    return c


def host_layout(inputs):
    f = lambda n: np.asarray(inputs[n], dtype=np.float32)
    o = {}
    for n in ("g_mix_pre", "g_mix_post", "w_in", "w_up", "w_out", "g_x_pre", "g_x_post", "g_mem", "w_cq", "w_ckv", "w_co",
              "g_ff_pre", "g_ff_post", "w_ff1", "w_ff2", "pool_w"):
        o[n] = f(n)
    o["gate_b"] = f("gate_b")
    o["pool_scale_c"] = f("pool_scale").reshape(DEPTH, 4, 128).transpose(0, 2, 1)
    o["gate_b_c"] = f("gate_b").reshape(DEPTH, 4, 8, 128).transpose(0, 3, 1, 2).reshape(DEPTH, 128, 32)
    o["wsT"] = f("w_s").transpose(0, 1, 3, 2)
    o["b_s_r"] = f("b_s").reshape(DEPTH, 512)
    o["sgu_ln_g_c"] = f("sgu_ln_g").reshape(DEPTH, 4, 128).transpose(0, 2, 1)
    o["sgu_ln_b_c"] = f("sgu_ln_b").reshape(DEPTH, 4, 128).transpose(0, 2, 1)
    rb = f("rel_bias")
    dist = np.clip(np.arange(256)[None, :] - np.arange(128)[:, None], 0, 128)
    ab = np.zeros((3, 4, 128, 2, 256), np.float32)
    for g, dil in enumerate(DILS):
        bk = _t5_bucket(dist * dil)
        for c in range(4):
            for hd in range(2):
                ab[g, c, :, hd, :] = rb[bk, g * 8 + 2 * c + hd]
    o["att_bias"] = ab
    o["w_glu"] = f("w_glu")
    dup = lambda a: np.concatenate([a, a], axis=1)
    o["s5_are"] = dup(f("a_re").transpose(0, 2, 1))
    o["s5_aim"] = dup(f("a_im").transpose(0, 2, 1))
    o["s5_ldt"] = np.broadcast_to(f("log_dt")[:, None, :], (DEPTH, 128, 32))
    brt, bit = f("b_re").transpose(0, 2, 1, 3), f("b_im").transpose(0, 2, 1, 3)
    crt, cit = f("c_re").transpose(0, 3, 1, 2), f("c_im").transpose(0, 3, 1, 2)
    o["s5_b1"] = np.concatenate([brt, bit], axis=1)
    o["s5_b2"] = np.concatenate([bit, brt], axis=1)
    o["s5_c1"] = np.concatenate([crt, cit], axis=1)
    o["s5_c2"] = np.concatenate([cit, crt], axis=1)
    o["d_skip_c"] = f("d_skip").reshape(DEPTH, 4, 128).transpose(0, 2, 1)
    o["b_glu_c"] = f("b_glu").reshape(DEPTH, 4, 128).transpose(0, 2, 1)
    return {k: np.ascontiguousarray(v, dtype=np.float32) for k, v in o.items()}


FULL_PLAN = [(k, l) for l in range(DEPTH) for k in ("mixer", "cross", "ffn")]


def kernel(**inputs):
    plan = inputs.pop("_plan", FULL_PLAN)
    cores = inputs.pop("_cores", 8)
    import time as _t
    _t0 = _t.time()
    nc, _nops = build_program(plan)
    print(f"[kernel] build {_t.time() - _t0:.1f}s nops={_nops}", flush=True)
    lay = host_layout(inputs)
    shared = {n: lay[n] for n in W_SHAPES}
    shared.update(host_consts())
    x = np.asarray(inputs["x"], dtype=np.float32)
    mem = np.asarray(inputs["mem"], dtype=np.float32)
    in_maps = []
    for c in range(cores):
        m = dict(shared)
        m["x"] = np.ascontiguousarray(x[c])
        m["mem"] = np.ascontiguousarray(mem[c])
        in_maps.append(m)
    _t0 = _t.time()
    res = run_bass_kernel_spmd(nc, in_maps, core_ids=list(range(cores)))
    print(f"[kernel] run {_t.time() - _t0:.1f}s", flush=True)
    if DEBUG_BOUT:
        global _LAST_BOUT
        _LAST_BOUT = [np.asarray(r["bout"]) for r in res.results]
    return np.stack([np.asarray(r["y"], dtype=np.float32) for r in res.results], axis=0)
```
